# Optimizing a Trainium2 kernel written in Bass

```python
import math
import jax, jax.numpy as jnp
from jax import lax
import numpy as np

D_MODEL = 1024
BATCH = 8
SEQ = 2048
DEPTH = 2
DEC_BATCH = 128
DEC_SEQ = 8
PAST_LEN = 2048
PAGE_SIZE = 128

N_EVEN = (DEPTH + 1) // 2
N_ODD = DEPTH // 2
EPS = 1e-6

A_HEADS = 4
A_DK = 64
A_DV = 128
A_WIDTH = A_HEADS * A_DV
A_QK = A_HEADS * A_DK
A_GATE_RANK = 16
A_GATE_NORM = 16.0
A_CHUNK = 64
B_WIDTH = D_MODEL // 2
B_CONV = 31
EVEN_IN = 2 * A_QK + A_WIDTH + A_GATE_RANK + A_WIDTH + 2 * B_WIDTH
EVEN_OUT = A_WIDTH + B_WIDTH
C_PATTERNS = ((128, 1), (512, 4), (2048, 16))
C_GROUPS = len(C_PATTERNS)
C_HEADS = 4
C_DH = 64
C_QKV = 3 * C_GROUPS * C_HEADS * C_DH
C_WIDTH = C_HEADS * C_DH
C_QBLOCK = 128
D_GROUPS = 4
D_DH = 64
D_WIDTH = D_GROUPS * D_DH
D_CHUNK = 128
ODD_IN = C_QKV + 2 * D_WIDTH
ODD_OUT = C_WIDTH + D_WIDTH
N_BUCKETS = 32
MAX_DIST = 2048
D_FF = 2816
FFN_CONV = 3

kernel_name = "hybrid_gla_conformer_dilated_gmlp_step"


def rmsnorm(x, g):
    x32 = x.astype(jnp.float32)
    y = x32 * lax.rsqrt(jnp.mean(x32 * x32, axis=-1, keepdims=True) + EPS)
    return (y * g.astype(jnp.float32)).astype(x.dtype)


def layernorm(x, g, b):
    x32 = x.astype(jnp.float32)
    mu = jnp.mean(x32, axis=-1, keepdims=True)
    var = jnp.mean(jnp.square(x32 - mu), axis=-1, keepdims=True)
    y = (x32 - mu) * lax.rsqrt(var + EPS)
    return (y * g.astype(jnp.float32) + b.astype(jnp.float32)).astype(x.dtype)


def causal_dwconv(x, buf, w, b):
    xp = jnp.concatenate([buf.astype(x.dtype), x], axis=1)
    y = lax.conv_general_dilated(xp, w[:, None, :].astype(x.dtype), window_strides=(1,), padding='VALID',
                                 dimension_numbers=('NWC', 'WIO', 'NWC'), feature_group_count=x.shape[-1])
    return y + b.astype(x.dtype), xp[:, -(w.shape[0] - 1):]


def gla_scan(q, k, v, log_a, s0):
    Bn, L, H, _ = q.shape
    C = A_CHUNK if L % A_CHUNK == 0 else L
    n = L // C

    def to_chunks(t):
        return t.astype(jnp.float32).reshape(Bn, n, C, H, t.shape[-1]).transpose(1, 0, 3, 2, 4)

    qc, kc, vc, gc = to_chunks(q), to_chunks(k), to_chunks(v), to_chunks(log_a)
    mask = jnp.tril(jnp.ones((C, C), dtype=bool))

    def step(S, inp):
        qb, kb, vb, gb = inp
        b = jnp.cumsum(gb, axis=2)
        diff = b[:, :, :, None, :] - b[:, :, None, :, :]
        decay = jnp.exp(jnp.where(mask[None, None, :, :, None], diff, -jnp.inf))
        attn = jnp.einsum('bhtd,bhsd,bhtsd->bhts', qb, kb, decay)
        o = jnp.einsum('bhts,bhsv->bhtv', attn, vb) + jnp.einsum('bhtd,bhdv->bhtv', qb * jnp.exp(b), S)
        b_last = b[:, :, -1:, :]
        S_new = jnp.exp(b_last[:, :, 0, :])[..., None] * S + jnp.einsum('bhsd,bhsv->bhdv', kb * jnp.exp(b_last - b), vb)
        return S_new, o

    S, o = lax.scan(step, s0.astype(jnp.float32), (qc, kc, vc, gc))
    o = o.transpose(1, 0, 3, 2, 4).reshape(Bn, L, H, -1)
    return o, S.astype(s0.dtype)


def t5_bucket(dist):
    max_exact = N_BUCKETS // 2
    d32 = jnp.maximum(dist, 1).astype(jnp.float32)
    large = max_exact + (jnp.log(d32 / max_exact) / math.log(MAX_DIST / max_exact)
                         * (N_BUCKETS - max_exact)).astype(jnp.int32)
    large = jnp.minimum(large, N_BUCKETS - 1)
    return jnp.where(dist < max_exact, dist, large)


def dilated_attn(q, kv_full, offset, window, dilation, bias):
    Bn, Lq, H, dh = q.shape
    J = window // dilation + 1
    pos = offset + jnp.arange(Lq, dtype=jnp.int32)
    idx = pos[:, None] - dilation * jnp.arange(J, dtype=jnp.int32)[None, :]
    valid = idx >= 0
    idx = jnp.maximum(idx, 0)
    bias_hj = bias.T.astype(jnp.float32)

    def attend(args):
        qb, ib, vb = args
        kv_sel = kv_full[:, ib]
        s = jnp.einsum('bqhd,bqjhd->bqhj', qb, kv_sel[:, :, :, 0]).astype(jnp.float32) + bias_hj[None, None]
        s = jnp.where(vb[None, :, None, :], s, -jnp.inf)
        lse = jax.nn.logsumexp(s, axis=-1)
        p = jnp.exp(s - lse[..., None])
        o = jnp.einsum('bqhj,bqjhd->bqhd', p.astype(qb.dtype), kv_sel[:, :, :, 1])
        return o, lse

    qbs = C_QBLOCK if Lq % C_QBLOCK == 0 else Lq
    n = Lq // qbs
    if n == 1:
        return attend((q, idx, valid))
    qs = q.reshape(Bn, n, qbs, H, dh).transpose(1, 0, 2, 3, 4)
    o, lse = lax.map(attend, (qs, idx.reshape(n, qbs, J), valid.reshape(n, qbs, J)))
    o = o.transpose(1, 0, 2, 3, 4).reshape(Bn, Lq, H, dh)
    lse = lse.transpose(1, 0, 2, 3).reshape(Bn, Lq, H)
    return o, lse


def even_mixer(h, gla_s, conv_buf, w_in, w_gate2, b_gate, gla_norm, conv_w, conv_b, ln_g, ln_b, w_out):
    Bn, L, _ = h.shape
    z = h @ w_in
    splits = tuple(int(c) for c in np.cumsum([A_QK, A_QK, A_WIDTH, A_GATE_RANK, A_WIDTH]))
    q, k, v, glr, r, glu = jnp.split(z, splits, axis=-1)
    q = q.reshape(Bn, L, A_HEADS, A_DK) * (A_DK ** -0.5)
    k = k.reshape(Bn, L, A_HEADS, A_DK)
    v = v.reshape(Bn, L, A_HEADS, A_DV)
    log_a = (jax.nn.log_sigmoid((glr @ w_gate2 + b_gate).astype(jnp.float32)) / A_GATE_NORM)
    log_a = log_a.reshape(Bn, L, A_HEADS, A_DK)
    o, s_new = gla_scan(q, k, v, log_a, gla_s)
    oa = rmsnorm(o.astype(h.dtype), gla_norm).reshape(Bn, L, A_WIDTH) * jax.nn.silu(r)
    ga, gg = jnp.split(glu, 2, axis=-1)
    u = ga * jax.nn.sigmoid(gg)
    c, buf_new = causal_dwconv(u, conv_buf, conv_w, conv_b)
    c = jax.nn.silu(layernorm(c, ln_g, ln_b))
    y = jnp.concatenate([oa, c], axis=-1) @ w_out
    return y, s_new, buf_new


def odd_mixer(h, kv_bufs, w_in, rel_bias, sgu_ln_g, sgu_ln_b, sgu_w, sgu_b, w_out):
    Bn, L, _ = h.shape
    z = h @ w_in
    qkv, du, dv = jnp.split(z, (C_QKV, C_QKV + D_WIDTH), axis=-1)
    qkv = qkv.reshape(Bn, L, 3, C_GROUPS, C_HEADS, C_DH)
    q = qkv[:, :, 0] * (C_DH ** -0.5)
    kv = qkv[:, :, 1:]
    outs, lses, new_bufs = [], [], []
    for g, (window, dilation) in enumerate(C_PATTERNS):
        kv_g = kv[:, :, :, g]
        if kv_bufs is None:
            kv_full, offset = kv_g, 0
        else:
            kv_full = jnp.concatenate([kv_bufs[g].astype(kv_g.dtype), kv_g], axis=1)
            offset = kv_bufs[g].shape[1]
        J = window // dilation + 1
        bias = rel_bias[t5_bucket(dilation * jnp.arange(J, dtype=jnp.int32)), g * C_HEADS:(g + 1) * C_HEADS]
        o_g, lse_g = dilated_attn(q[:, :, g], kv_full, offset, window, dilation, bias)
        outs.append(o_g)
        lses.append(lse_g)
        new_bufs.append(kv_full[:, -min(window, kv_full.shape[1]):])
    wts = jax.nn.softmax(jnp.stack(lses, axis=0), axis=0)
    oc = jnp.sum(wts[..., None].astype(h.dtype) * jnp.stack(outs, axis=0), axis=0).reshape(Bn, L, C_WIDTH)
    u = jax.nn.gelu(du)
    vn = layernorm(jax.nn.gelu(dv), sgu_ln_g, sgu_ln_b)
    C = D_CHUNK if L >= D_CHUNK else L
    n = L // C
    vg = vn.reshape(Bn, n, C, D_GROUPS, D_DH)
    ws = sgu_w[:, :C, :C] * jnp.tril(jnp.ones((C, C), dtype=sgu_w.dtype))
    mixed = jnp.einsum('gts,bnsgd->bntgd', ws, vg) + sgu_b[:, :C].T[None, None, :, :, None]
    od = u * mixed.reshape(Bn, L, D_WIDTH)
    y = jnp.concatenate([oc, od], axis=-1) @ w_out
    return y, new_bufs, vn


def conv_ffn(h, buf, w_up, dw_w, dw_b, w_down):
    up = h @ w_up
    c, buf_new = causal_dwconv(up, buf, dw_w, dw_b)
    g, val = jnp.split(c, 2, axis=-1)
    return (jax.nn.gelu(g) * val) @ w_down, buf_new


def trunk(x, gla_s, convb_buf, kv_bufs, ffn_buf, p):
    new_gla, new_convb, new_v, new_ffn = [], [], [], []
    new_kv = [[] for _ in C_PATTERNS]
    for layer in range(DEPTH):
        h = rmsnorm(x, p['norm_pre_mix'][layer])
        if layer % 2 == 0:
            e = layer // 2
            y, s_new, cb_new = even_mixer(h, gla_s[e], convb_buf[e], p['w_in_even'][e], p['w_gate2'][e],
                                          p['b_gate'][e], p['gla_norm'][e], p['conv_b_w'][e], p['conv_b_b'][e],
                                          p['ln_b_g'][e], p['ln_b_b'][e], p['w_out_even'][e])
            new_gla.append(s_new)
            new_convb.append(cb_new)
        else:
            o = layer // 2
            bufs = None if kv_bufs is None else [b[o] for b in kv_bufs]
            y, kv_new, v_new = odd_mixer(h, bufs, p['w_in_odd'][o], p['rel_bias'], p['sgu_ln_g'][o],
                                         p['sgu_ln_b'][o], p['sgu_w'][o], p['sgu_b'][o], p['w_out_odd'][o])
            for g in range(C_GROUPS):
                new_kv[g].append(kv_new[g])
            new_v.append(v_new)
        x = x + rmsnorm(y, p['norm_post_mix'][layer])
        h = rmsnorm(x, p['norm_pre_ffn'][layer])
        f, fb = conv_ffn(h, ffn_buf[layer], p['w_up'][layer], p['ffn_dw_w'][layer], p['ffn_dw_b'][layer],
                         p['w_down'][layer])
        new_ffn.append(fb)
        x = x + rmsnorm(f, p['norm_post_ffn'][layer])
    return (x, jnp.stack(new_gla), jnp.stack(new_convb), [jnp.stack(l) for l in new_kv],
            jnp.stack(new_v), jnp.stack(new_ffn))


def setup_inputs(seed: int = 0) -> dict:
    key = jax.random.key(seed)
    ks = iter(jax.random.split(key, 48))

    def nrm(shape, scale):
        return scale * jax.random.normal(next(ks), shape, jnp.float32)

    def gain(shape):
        return 1.0 + 0.1 * jax.random.normal(next(ks), shape, jnp.float32)

    D = D_MODEL
    return {
        "x_prompt": nrm((BATCH, SEQ, D), 1.0),
        "x_sample": nrm((DEC_BATCH, DEC_SEQ, D), 1.0),
        "state_gla": nrm((N_EVEN, DEC_BATCH, A_HEADS, A_DK, A_DV), 0.1),
        "state_conv_b": nrm((N_EVEN, DEC_BATCH, B_CONV - 1, B_WIDTH), 0.5),
        "cache_c_w128": nrm((N_ODD, DEC_BATCH, min(C_PATTERNS[0][0], PAST_LEN), 2, C_HEADS, C_DH), 1.0),
        "cache_c_w512": nrm((N_ODD, DEC_BATCH, min(C_PATTERNS[1][0], PAST_LEN), 2, C_HEADS, C_DH), 1.0),
        "cache_c_w2048": nrm((N_ODD, DEC_BATCH, min(C_PATTERNS[2][0], PAST_LEN), 2, C_HEADS, C_DH), 1.0),
        "state_ffn_conv": nrm((DEPTH, DEC_BATCH, FFN_CONV - 1, 2 * D_FF), 1.0),
        "norm_pre_mix": gain((DEPTH, D)),
        "norm_post_mix": gain((DEPTH, D)),
        "norm_pre_ffn": gain((DEPTH, D)),
        "norm_post_ffn": gain((DEPTH, D)),
        "w_in_even": nrm((N_EVEN, D, EVEN_IN), D ** -0.5),
        "w_gate2": nrm((N_EVEN, A_GATE_RANK, A_QK), A_GATE_RANK ** -0.5),
        "b_gate": nrm((N_EVEN, A_QK), 0.1),
        "gla_norm": gain((N_EVEN, A_DV)),
        "conv_b_w": nrm((N_EVEN, B_CONV, B_WIDTH), B_CONV ** -0.5),
        "conv_b_b": nrm((N_EVEN, B_WIDTH), 0.02),
        "ln_b_g": gain((N_EVEN, B_WIDTH)),
        "ln_b_b": nrm((N_EVEN, B_WIDTH), 0.02),
        "w_out_even": nrm((N_EVEN, EVEN_OUT, D), EVEN_OUT ** -0.5),
        "w_in_odd": nrm((N_ODD, D, ODD_IN), D ** -0.5),
        "rel_bias": nrm((N_BUCKETS, C_GROUPS * C_HEADS), 0.5),
        "sgu_ln_g": gain((N_ODD, D_WIDTH)),
        "sgu_ln_b": nrm((N_ODD, D_WIDTH), 0.02),
        "sgu_w": nrm((N_ODD, D_GROUPS, D_CHUNK, D_CHUNK), D_CHUNK ** -0.5),
        "sgu_b": gain((N_ODD, D_GROUPS, D_CHUNK)),
        "w_out_odd": nrm((N_ODD, ODD_OUT, D), ODD_OUT ** -0.5),
        "w_up": nrm((DEPTH, D, 2 * D_FF), D ** -0.5),
        "ffn_dw_w": nrm((DEPTH, FFN_CONV, 2 * D_FF), FFN_CONV ** -0.5),
        "ffn_dw_b": nrm((DEPTH, 2 * D_FF), 0.02),
        "w_down": nrm((DEPTH, D_FF, D), D_FF ** -0.5),
    }


def reference(x_prompt, x_sample, state_gla, state_conv_b, cache_c_w128, cache_c_w512, cache_c_w2048,
              state_ffn_conv, norm_pre_mix, norm_post_mix, norm_pre_ffn, norm_post_ffn, w_in_even, w_gate2,
              b_gate, gla_norm, conv_b_w, conv_b_b, ln_b_g, ln_b_b, w_out_even, w_in_odd, rel_bias, sgu_ln_g,
              sgu_ln_b, sgu_w, sgu_b, w_out_odd, w_up, ffn_dw_w, ffn_dw_b, w_down):
    p = dict(norm_pre_mix=norm_pre_mix, norm_post_mix=norm_post_mix, norm_pre_ffn=norm_pre_ffn,
             norm_post_ffn=norm_post_ffn, w_in_even=w_in_even, w_gate2=w_gate2, b_gate=b_gate,
             gla_norm=gla_norm, conv_b_w=conv_b_w, conv_b_b=conv_b_b, ln_b_g=ln_b_g, ln_b_b=ln_b_b,
             w_out_even=w_out_even, w_in_odd=w_in_odd, rel_bias=rel_bias, sgu_ln_g=sgu_ln_g,
             sgu_ln_b=sgu_ln_b, sgu_w=sgu_w, sgu_b=sgu_b, w_out_odd=w_out_odd, w_up=w_up,
             ffn_dw_w=ffn_dw_w, ffn_dw_b=ffn_dw_b, w_down=w_down)
    dt = x_prompt.dtype
    gla0 = jnp.zeros((N_EVEN, BATCH, A_HEADS, A_DK, A_DV), dt)
    convb0 = jnp.zeros((N_EVEN, BATCH, B_CONV - 1, B_WIDTH), dt)
    ffn0 = jnp.zeros((DEPTH, BATCH, FFN_CONV - 1, 2 * D_FF), dt)
    y_prompt, p_gla, p_conv_b, p_kv, _, p_ffn_conv = trunk(x_prompt, gla0, convb0, None, ffn0, p)
    y_sample, s_gla, s_conv_b, s_kv, s_sgu_v, s_ffn_conv = trunk(
        x_sample, state_gla, state_conv_b, [cache_c_w128, cache_c_w512, cache_c_w2048], state_ffn_conv, p)
    return (y_prompt, y_sample, p_gla, p_conv_b, p_kv[0], p_kv[1], p_kv[2], p_ffn_conv,
            s_gla, s_conv_b, s_kv[0], s_kv[1], s_kv[2], s_sgu_v, s_ffn_conv)
```

```python
import contextlib
import math
import os
DBG = set(os.environ.get('KDEBUG', '').split(','))
ODDSTOP = int(os.environ.get('ODDSTOP', '99'))
import numpy as np
import concourse.bass as bass
import concourse.mybir as mybir
from concourse.bass_utils import run_bass_kernel_spmd
from concourse.alu_op_type import AluOpType as ALU

F32 = mybir.dt.float32
BF16 = mybir.dt.bfloat16
AF = mybir.ActivationFunctionType
AX = mybir.AxisListType

NCORES = 8
D = 1024
DFF = 2816
EPS = 1e-6
ENGS = ("pe", "act", "dve", "pool", "sp")
COMPUTE = ("pe", "act", "dve", "pool")


class Op:
    __slots__ = ("eng", "fn", "reads", "writes", "is_dma", "idx", "deps", "has_dependents", "sem", "semval")

    def __init__(self, eng, fn, reads, writes, is_dma):
        self.eng, self.fn, self.reads, self.writes, self.is_dma = eng, fn, reads, writes, is_dma
        self.deps, self.has_dependents, self.sem, self.semval = [], False, None, None


def _bname(k):
    return k if isinstance(k, str) else k[0]


class Prog:
    def __init__(self, nc, n_dma_sems=48, same_engine_sync=True):
        self.nc = nc
        self.ops = []
        self.last_writer = {}
        self.readers = {}
        self.n_dma_sems = n_dma_sems
        self.same_engine_sync = same_engine_sync
        self.final_waits = []
        self.touch = {}
        self.alias = {}
        self.mw = {}

    def retire(self, names):
        deps = set()
        for n in names:
            t = self.touch.get(n)
            if t:
                deps.update(t["c"].values())
                deps.update(t["d"])
        return sorted(deps)

    def op(self, eng, fn, reads=(), writes=(), is_dma=False, multi=False):
        o = Op(eng, fn, tuple(reads), tuple(writes), is_dma)
        o.idx = len(self.ops)
        deps = set()
        for k in o.reads:
            w = self.last_writer.get(k)
            if w is not None:
                deps.add(w)
            deps.update(self.mw.get(k, ()))
        for k in o.writes:
            w = self.last_writer.get(k)
            if w is not None:
                deps.add(w)
            deps.update(self.readers.get(k, {}).values())
            if not multi:
                deps.update(self.mw.get(k, ()))
        for k in o.reads + o.writes:
            n = _bname(k)
            a = self.alias.get(n)
            if a:
                deps.update(a)
            t = self.touch.setdefault(n, {"c": {}, "d": []})
            if is_dma:
                t["d"].append(o.idx)
            else:
                t["c"][eng] = o.idx
        for k in o.reads:
            r = self.readers.setdefault(k, {})
            if is_dma:
                r[("dma", o.idx)] = o.idx
            else:
                r[eng] = o.idx
        for k in o.writes:
            if multi:
                self.mw.setdefault(k, []).append(o.idx)
            else:
                self.last_writer[k] = o.idx
                self.mw[k] = []
                self.readers[k] = {}
        deps.discard(o.idx)
        o.deps = sorted(deps)
        self.ops.append(o)
        return o

    def emit(self, stack):
        nc, ops = self.nc, self.ops
        for o in ops:
            nd = []
            for d in o.deps:
                p = ops[d]
                if (not p.is_dma) and (not o.is_dma) and p.eng == o.eng:
                    if o.eng == "pe" or not self.same_engine_sync:
                        continue
                nd.append(d)
            o.deps = nd
        dma_last = [None] * self.n_dma_sems
        dma_cnt = [0] * self.n_dma_sems
        NSW = 16
        rrs = {"pool": 0, "sp": 0}
        for o in ops:
            if o.is_dma:
                if o.eng == "pool":
                    s = rrs["pool"]
                    rrs["pool"] = (s + 1) % NSW
                else:
                    s = NSW + rrs["sp"]
                    rrs["sp"] = (rrs["sp"] + 1) % (self.n_dma_sems - NSW)
                if dma_last[s] is not None and dma_last[s] not in o.deps:
                    o.deps.append(dma_last[s])
                dma_cnt[s] += 16
                o.sem, o.semval = ("dma", s), dma_cnt[s]
                dma_last[s] = o.idx
        for o in ops:
            for d in o.deps:
                ops[d].has_dependents = True
        for idx in self.final_waits:
            ops[idx].has_dependents = True
        cnt = {e: 0 for e in ENGS}
        pending = {e: [] for e in ENGS}
        for o in ops:
            if o.is_dma:
                continue
            pending[o.eng].append(o)
            if o.has_dependents:
                cnt[o.eng] += 1
                for p in pending[o.eng]:
                    p.sem, p.semval = ("eng", o.eng), cnt[o.eng]
                pending[o.eng] = []
        self.stats = dict(cnt)
        esem = {e: stack.enter_context(nc.semaphore("s_" + e)) for e in COMPUTE}
        dsem = [stack.enter_context(nc.semaphore("d_%d" % i)) for i in range(self.n_dma_sems)]
        block = stack.enter_context(nc.Block())

        def semh(key):
            return esem[key[1]] if key[0] == "eng" else dsem[key[1]]

        per_eng = {e: [o for o in ops if o.eng == e] for e in ENGS}
        finals = [ops[i] for i in self.final_waits]

        def run(ename, eobj):
            seen = {}
            for o in per_eng[ename]:
                need = {}
                for d in o.deps:
                    p = ops[d]
                    if p.sem is None or seen.get(p.sem, 0) >= p.semval:
                        continue
                    if need.get(p.sem, 0) < p.semval:
                        need[p.sem] = p.semval
                for k, v in need.items():
                    eobj.wait_ge(semh(k), v)
                    seen[k] = v
                ins = o.fn(eobj)
                if o.is_dma:
                    ins.then_inc(semh(o.sem), 16)
                elif o.has_dependents:
                    ins.then_inc(semh(o.sem), 1)
            if ename == "sp":
                for p in finals:
                    if seen.get(p.sem, 0) < p.semval:
                        eobj.wait_ge(semh(p.sem), p.semval)
                        seen[p.sem] = p.semval

        @block.tensor
        def _(e):
            run("pe", e)

        @block.scalar
        def _(e):
            run("act", e)

        @block.vector
        def _(e):
            run("dve", e)

        @block.gpsimd
        def _(e):
            run("pool", e)

        @block.sync
        def _(e):
            run("sp", e)


def weight_blocks():
    B = {}
    ev = [("E0", 0, 512), ("E1", 512, 512), ("E2", 1024, 16), ("E3", 1040, 512), ("E4", 1552, 512), ("E5", 2064, 512)]
    for n, c0, nc_ in ev:
        B[n] = ("w_in_even", 0, 8, [(0, c0, nc_)])
    B["EO0"] = ("w_out_even", 0, 8, [(0, 0, 512)])
    B["EO1"] = ("w_out_even", 0, 8, [(0, 512, 512)])
    od = [("O0", 0, 512), ("O1", 512, 256), ("O2", 768, 512), ("O3", 1280, 256), ("O4", 1536, 512), ("O5", 2048, 256),
          ("O6", 2304, 512)]
    for n, c0, nc_ in od:
        B[n] = ("w_in_odd", 0, 8, [(0, c0, nc_)])
    B["OO0"] = ("w_out_odd", 0, 4, [(0, 0, 512)])
    B["OO1"] = ("w_out_odd", 0, 4, [(0, 512, 512)])
    for l in range(2):
        for b in range(11):
            B["U%d_%d" % (l, b)] = ("w_up", l, 8, [(0, 256 * b, 256), (256, DFF + 256 * b, 256)])
        for o in range(8):
            B["D%d_%d" % (l, o)] = ("w_down", l, 22, [(0, 128 * o, 128)])
    return B


def group_wplan():
    p = ["E0", "E2", "E1", "E3", "E4", "E5", "EO0", "EO1"]
    p += ["U0_%d" % b for b in range(11)] + ["D0_%d" % o for o in range(8)]
    p += ["O0", "O1", "O2", "O4", "O3", "O5", "O6", "OO0", "OO1"]
    p += ["U1_%d" % b for b in range(11)] + ["D1_%d" % o for o in range(8)]
    return p


WEIGHT_SHAPES = {
    "w_in_even": [1, 1024, 2576], "w_out_even": [1, 1024, 1024], "w_in_odd": [1, 1024, 2816],
    "w_out_odd": [1, 512, 1024], "w_up": [2, 1024, 5632], "w_down": [2, 2816, 1024],
}
SMALL_SHAPES = {
    "norm_pre_mix": [2, 1024], "norm_post_mix": [2, 1024], "norm_pre_ffn": [2, 1024], "norm_post_ffn": [2, 1024],
    "w_gate2": [1, 16, 256], "b_gate": [1, 256], "gla_norm": [1, 128], "conv_b_w": [1, 31, 512], "conv_b_b": [1, 512],
    "ln_b_g": [1, 512], "ln_b_b": [1, 512], "rel_bias": [32, 12], "sgu_ln_g": [1, 256], "sgu_ln_b": [1, 256],
    "sgu_w": [1, 4, 128, 128], "sgu_b": [1, 4, 128], "ffn_dw_w": [2, 3, 5632], "ffn_dw_b": [2, 5632],
}
CORE_IN = {
    "xp": [2048, 1024], "xs": [128, 1024], "sgla": [16, 4, 64, 128], "scb": [16, 30, 512],
    "c128": [16, 128, 512], "c512": [16, 512, 512], "c2048": [16, 2048, 512], "sffn": [2, 16, 2, 5632],
}
CORE_OUT = {
    "y_p": [2048, 1024], "y_s": [128, 1024], "p_gla": [4, 64, 128], "p_conv_b": [30, 512],
    "p_kv128": [128, 512], "p_kv512": [512, 512], "p_kv2048": [2048, 512], "p_ffn": [2, 2, 5632],
    "s_gla": [16, 4, 64, 128], "s_conv_b": [16, 30, 512], "s_kv128": [16, 128, 512], "s_kv512": [16, 512, 512],
    "s_kv2048": [16, 2048, 512], "s_sgu_v": [128, 256], "s_ffn": [2, 16, 2, 5632],
}


def host_consts():
    c = {}
    c["ident"] = np.eye(128, dtype=np.float32)
    m = np.ones((128, 512), np.float32)
    m[:, 0::128] = 0.0
    c["scanmask_p"] = m
    m = np.ones((128, 128), np.float32)
    m[:, 0::8] = 0.0
    c["scanmask_s"] = m
    s = np.arange(128)[:, None]
    t = np.arange(128)[None, :]
    c["causal_st"] = (s <= t).astype(np.float32)
    c["causal_s8"] = ((s <= t) & (s // 8 == t // 8)).astype(np.float32)
    dil = (1, 4, 16)
    oh = np.zeros((32, 3, 129), np.float32)
    for g in range(3):
        dist = (dil[g] * np.arange(129)).astype(np.int64)
        d32 = np.maximum(dist, 1).astype(np.float32)
        large = 16 + (np.log(d32 / np.float32(16)) / np.float32(math.log(2048 / 16)) * np.float32(16)).astype(np.int32)
        large = np.minimum(large, 31)
        bk = np.where(dist < 16, dist, large)
        oh[bk, g, np.arange(129)] = 1.0
    c["onehot"] = oh
    c["jrev"] = np.ascontiguousarray(np.eye(128, dtype=np.float32)[::-1])
    c["seqmask"] = (np.arange(128)[:, None] // 8 == np.arange(16)[None, :]).astype(np.float32)
    return c


class KB:
    def __init__(self, stages):
        self.stages = stages
        self.nc = bass.Bass("TRN2", target_bir_lowering=False)
        self.st = contextlib.ExitStack()
        self.P = Prog(self.nc)
        self.dram = {}
        self.psi = 0
        self.ps_avail = list(range(8))
        self.wptr = 0
        self.wissued = 0

    def din(self, name, shape, dt=F32):
        self.dram[name] = self.nc.dram_tensor(name, list(shape), dt, kind="ExternalInput").ap()
        return self.dram[name]

    def dout(self, name, shape, dt=F32):
        self.dram[name] = self.nc.dram_tensor(name, list(shape), dt, kind="ExternalOutput").ap()
        return self.dram[name]

    def sb(self, name, shape, dt=F32, stack=None):
        return (stack or self.st).enter_context(self.nc.sbuf_tensor(name, list(shape), dt))

    def psum(self):
        av = self.ps_avail
        i = av[self.psi % len(av)]
        self.psi += 1
        return self.ps[i], ("ps", i)

    def next_w(self, name):
        P = self.P
        assert self.wplan[self.wptr] == name, (self.wplan[self.wptr], name)
        i = self.wptr
        self.wptr += 1
        while self.wissued < min(len(self.wplan), i + self.NSLOT - 1):
            j = self.wissued
            self.precast_upto(j + 6)
            bn = self.wplan[j]
            bi = self.wbidx[bn]
            _, _, kc, parts = self.wblocks[bn]
            ncols = sum(p[2] for p in parts)
            slot = j % self.NSLOT
            dst = self.wslots[slot][:, 0:kc * ncols]
            src = self.wsc[bi, :, 0:kc * ncols]
            P.op("sp", lambda e, dst=dst, src=src: e.dma_start(out=dst, in_=src),
                 reads=[("wsc", bi)], writes=[("wslot", slot)], is_dma=True)
            self.wissued += 1
        slot = i % self.NSLOT
        _, _, kc, parts = self.wblocks[name]
        ncols = sum(p[2] for p in parts)
        view = self.wslots[slot][:, 0:kc * ncols].rearrange("p (k c) -> p k c", k=kc)
        return view, ("wslot", slot)

    def precast_upto(self, jmax):
        P, dr = self.P, self.dram
        nb = len(self.wblocks)
        while self.pc_ptr < min(nb, jmax + 1):
            bn = self.wplan[self.pc_ptr]
            self.pc_ptr += 1
            tn, l, kc, parts = self.wblocks[bn]
            bi = self.wbidx[bn]
            ncols = sum(p[2] for p in parts)
            for (d0, s0, n_) in parts:
                src = dr[tn][l, :, s0:s0 + n_].rearrange("(k p) c -> p k c", p=128)
                dst = self.wsc[bi, :, 0:kc * ncols].rearrange("p (k c) -> p k c", k=kc)[:, :, d0:d0 + n_]
                P.op("pool", lambda e, dst=dst, src=src: e.dma_start(out=dst, in_=src),
                     writes=[("wsc", bi)], is_dma=True, multi=True)

    def build(self):
        nc, P, st = self.nc, self.P, self.st
        for n, s in CORE_IN.items():
            self.din(n, s)
        for n, s in WEIGHT_SHAPES.items():
            self.din(n, s)
        for n, s in SMALL_SHAPES.items():
            self.din(n, s)
        self.consts = host_consts()
        for n, a in self.consts.items():
            self.din("c_" + n, a.shape)
        for n, s in CORE_OUT.items():
            self.dout(n, s)
        dr = self.dram
        self.wblocks = weight_blocks()
        self.wbidx = {n: i for i, n in enumerate(self.wblocks)}
        self.wsc = nc.dram_tensor("wsc", [len(self.wblocks), 128, 4096], BF16, kind="Internal").ap()
        self.NSLOT = 5
        gp = group_wplan()
        self.groups = [("P", g) for g in range(4)] + [("S", 0)]
        self.wplan = []
        for _ in self.groups:
            self.wplan += gp

        self.ps = [st.enter_context(nc.psum_tensor("ps%d" % i, [128, 512], F32)) for i in range(8)]
        self.wslots = [self.sb("wslot%d" % i, [128, 4096], BF16) for i in range(self.NSLOT)]
        self.ident = self.sb("ident", [128, 128])
        self.ones_bf = self.sb("ones_bf", [128, 128], BF16)
        self.gains = self.sb("gains", [128, 8, 8])
        self.ffnp = self.sb("ffnp", [128, 44, 8])
        self.hal = [self.sb("hal%d" % l, [128, 22, 2, 2]) for l in range(2)]
        self.xT = self.sb("xT", [128, 8, 512])
        self.hT = self.sb("hT", [128, 8, 512], BF16)
        self.rstd = self.sb("rstd", [128, 512])
        self.yT = self.sb("yT", [128, 8, 512])
        self.eps_t = self.sb("eps_t", [128, 1])
        self.alloc_persistent()

        P.op("sp", lambda e: e.dma_start(out=self.ident[:], in_=dr["c_ident"]), writes=["ident"], is_dma=True)
        P.op("pool", lambda e: e.memset(self.ones_bf[:], 1.0), writes=["ones_bf"])
        P.op("pool", lambda e: e.memset(self.eps_t[:], EPS), writes=["eps"])
        for l in range(2):
            P.op("pool", lambda e, l=l: e.memset(self.hal[l][:], 0.0), writes=["hal%d" % l])
        self.pc_ptr = 0
        self.load_params()
        if "noodd" not in DBG:
            self.load_odd_params()
        if "notab" not in DBG:
            self.build_bias_tables()
        def cache_copies():
            for nm, cn, W_ in (("s_kv128", "c128", 128), ("s_kv512", "c512", 512), ("s_kv2048", "c2048", 2048)):
                if "nocopy" in DBG:
                    continue
                for i in range(16):
                    o_ = P.op("sp", lambda e, nm=nm, cn=cn, W_=W_, i=i: e.dma_start(out=dr[nm][i, 0:W_ - 8, :], in_=dr[cn][i, 8:W_, :]), is_dma=True)
                    P.final_waits.append(o_.idx)

        for gi, (kind, g) in enumerate(self.groups):
            self.run_group(kind, g)
            if gi == 0:
                cache_copies()

        P.emit(st)
        return nc

    def rows_to_featmajor(self, row_aps, C, dst, dst_key, row0=0):
        P = self.P
        R = len(row_aps)
        with self.phase("stg_%s_%d" % (dst_key, row0)) as ph:
            stg = ph.buf("stg", [R, C])
            skey = ph.k("stg")
            for r, ap in enumerate(row_aps):
                P.op("sp", lambda e, r=r, ap=ap: e.dma_start(out=stg[r:r + 1, :], in_=ap.rearrange("(o c) -> o c", o=1)),
                     writes=[skey], is_dma=True, multi=True)
            nch = C // 128
            per = max(1, min(512 // R, nch))
            for c0 in range(0, nch, per):
                n = min(per, nch - c0)
                ps, pk = self.psum()
                for i in range(n):
                    c = c0 + i
                    P.op("pe", lambda e, ps=ps, i=i, c=c: e.transpose(out=ps[:, i * R:(i + 1) * R], in_=stg[0:R, c * 128:(c + 1) * 128],
                                                                        identity=self.ident[0:R, 0:R]),
                         reads=[skey, "ident"], writes=[pk])
                src = ps[:, 0:n * R].rearrange("p (c r) -> p c r", r=R)
                d = dst[:, c0:c0 + n, row0:row0 + R]
                P.op("dve", lambda e, d=d, src=src: e.tensor_copy(out=d, in_=src), reads=[pk], writes=[dst_key])

    def alloc_persistent(self):
        self.one_t = self.sb("one_t", [128, 1])
        self.negb = self.sb("negb", [128, 2, 1])
        self.gn = self.sb("gn", [128, 1, 1])
        self.cbp = self.sb("cbp", [128, 4, 34])
        self.wg2f = self.sb("wg2f", [16, 256])
        self.wg2 = self.sb("wg2", [16, 256], BF16)
        self.scanmask_p = self.sb("scanmask_p", [128, 512])
        self.scanmask_s = self.sb("scanmask_s", [128, 128])
        self.causal_st = self.sb("causal_st", [128, 128])
        self.causal_s8 = self.sb("causal_s8", [128, 128])
        self.seqmask = self.sb("seqmask", [128, 16], BF16)
        self.seqmaskf = self.sb("seqmaskf", [128, 16])
        self.S = self.sb("S", [128, 2, 128])
        self.Sbf = self.sb("Sbf", [128, 2, 128], BF16)
        self.uhalo = self.sb("uhalo", [128, 4, 30])
        self.Jrev = self.sb("Jrev", [128, 128])
        self.ones_f = self.sb("ones_f", [128, 64])
        self.lngb = self.sb("lngb", [128, 2, 256])
        self.sgub = self.sb("sgub", [128, 2, 128])
        self.sgubs = self.sb("sgubs", [128, 2, 128])
        self.wsT = self.sb("wsT", [128, 4, 128], BF16)
        self.wsTs = self.sb("wsTs", [128, 4, 128], BF16)
        self.tabn = self.sb("tabn", [128, 3, 512], BF16)
        self.tabc = self.sb("tabc", [128, 416], BF16)
        self.tabs = self.sb("tabs", [128, 8, 512], BF16)
        self.kT0 = self.sb("kT0", [128, 2, 640], BF16)
        self.kT1 = self.sb("kT1", [128, 2, 2, 512], BF16)
        self.kT2 = self.sb("kT2", [128, 2, 2048], BF16)
        self.v0 = self.sb("v0", [128, 5, 256], BF16)
        self.v1 = self.sb("v1", [128, 2, 4, 256], BF16)
        self.v2 = self.sb("v2", [128, 16, 256], BF16)

    def load_params(self):
        dr = self.dram
        P = self.P
        P.op("pool", lambda e: e.memset(self.one_t[:], 1.0), writes=["one_t"])
        P.op("pool", lambda e: e.memset(self.ones_f[:], 1.0), writes=["ones_f"])
        P.op("pool", lambda e: e.memset(self.S[:], 0.0), writes=[("S", h) for h in range(4)])
        P.op("pool", lambda e: e.memset(self.Sbf[:], 0.0), writes=[("Sbf", h) for h in range(4)])
        P.op("pool", lambda e: e.memset(self.uhalo[:], 0.0), writes=["uhalo"])
        for nm, t in [("scanmask_p", self.scanmask_p), ("scanmask_s", self.scanmask_s)]:
            P.op("sp", lambda e, nm=nm, t=t: e.dma_start(out=t[:], in_=dr["c_" + nm]), writes=["scanmask"], is_dma=True, multi=True)
        for nm, t in [("causal_st", self.causal_st), ("causal_s8", self.causal_s8), ("seqmask", self.seqmaskf)]:
            P.op("sp", lambda e, nm=nm, t=t: e.dma_start(out=t[:], in_=dr["c_" + nm]), writes=["masks" if nm != "seqmask" else "seqmaskf"], is_dma=True, multi=True)
        P.op("dve", lambda e: e.tensor_copy(out=self.seqmask[:], in_=self.seqmaskf[:]), reads=["seqmaskf"], writes=["masks"])
        P.op("sp", lambda e: e.dma_start(out=self.wg2f[:], in_=dr["w_gate2"][0]), writes=["wg2f"], is_dma=True)
        P.op("dve", lambda e: e.tensor_copy(out=self.wg2[:], in_=self.wg2f[:]), reads=["wg2f"], writes=["wg2"])
        self.rows_to_featmajor([dr["b_gate"][0]], 256, self.negb, "negb")
        P.op("dve", lambda e: e.tensor_scalar(out=self.negb[:], in0=self.negb[:], scalar1=-1.0, scalar2=None, op0=ALU.mult), reads=["negb"], writes=["negb"])
        self.rows_to_featmajor([dr["gla_norm"][0]], 128, self.gn, "gn")
        rows = [dr["conv_b_w"][0, j] for j in range(31)] + [dr["conv_b_b"][0], dr["ln_b_g"][0], dr["ln_b_b"][0]]
        self.rows_to_featmajor(rows, 512, self.cbp, "cbp")
        rows = []
        for n in ["norm_pre_mix", "norm_post_mix", "norm_pre_ffn", "norm_post_ffn"]:
            rows += [dr[n][0], dr[n][1]]
        self.rows_to_featmajor(rows, 1024, self.gains, "gains")
        rows = []
        for l in range(2):
            rows += [dr["ffn_dw_w"][l, 0], dr["ffn_dw_w"][l, 1], dr["ffn_dw_w"][l, 2], dr["ffn_dw_b"][l]]
        self.rows_to_featmajor(rows, 5632, self.ffnp, "ffnp")

    @contextlib.contextmanager
    def phase(self, tag):
        ph = Phase(self, tag)
        try:
            yield ph
        finally:
            ph.close()

    def ssq_rstd(self, src, src_key, T, scale_n, sq, sq_keys):
        P = self.P
        rstd = self.rstd
        P.op("act", lambda e: e.activation(out=sq[:, :, 0:T], in_=src[:, :, 0:T], func=AF.Square), reads=[src_key], writes=sq_keys)
        ps, pk = self.psum()
        for k in range(8):
            P.op("pe", lambda e, k=k, ps=ps: e.matmul(ps[:, 0:T], lhsT=self.ones_bf[:], rhs=sq[:, k, 0:T], start=(k == 0), stop=(k == 7)),
                 reads=sq_keys + ["ones_bf"], writes=[pk])
        P.op("act", lambda e, ps=ps: e.activation(out=rstd[:, 0:T], in_=ps[:, 0:T], func=AF.Sqrt, bias=self.eps_t[:], scale=1.0 / scale_n),
             reads=[pk, "eps"], writes=["rstd"])
        P.op("dve", lambda e: e.reciprocal(out=rstd[:, 0:T], in_=rstd[:, 0:T]), reads=["rstd"], writes=["rstd"])

    def prenorm(self, grow, T):
        P = self.P
        sqv = self.yT[:, 0:4, :].bitcast(BF16).rearrange("p a (b c) -> p (a b) c", b=2)
        self.ssq_rstd(self.xT, "xT", T, 1024.0, sqv, ["yT"])
        for k in range(8):
            P.op("dve", lambda e, k=k: e.scalar_tensor_tensor(out=self.hT[:, k, 0:T], in0=self.xT[:, k, 0:T], scalar=self.gains[:, k, grow:grow + 1],
                                                               in1=self.rstd[:, 0:T], op0=ALU.mult, op1=ALU.mult),
                 reads=["xT", "gains", "rstd"], writes=[("hT", k)])

    def postnorm_residual(self, grow, T):
        P = self.P
        self.ssq_rstd(self.yT, "yT", T, 1024.0, self.hT, [("hT", k) for k in range(8)])
        for k in range(8):
            P.op("dve", lambda e, k=k: e.tensor_tensor(out=self.yT[:, k, 0:T], in0=self.yT[:, k, 0:T], in1=self.rstd[:, 0:T], op=ALU.mult),
                 reads=["yT", "rstd"], writes=[("yTs", k)])
        for k in range(8):
            P.op("dve", lambda e, k=k: e.scalar_tensor_tensor(out=self.xT[:, k, 0:T], in0=self.yT[:, k, 0:T], scalar=self.gains[:, k, grow:grow + 1],
                                                               in1=self.xT[:, k, 0:T], op0=ALU.mult, op1=ALU.add),
                 reads=[("yTs", k), "yT", "gains", "xT"], writes=["xT"])

    def run_group(self, kind, g):
        P, dr = self.P, self.dram
        T = 512 if kind == "P" else 128
        nt = T // 128
        with self.phase("xin_%s%d" % (kind, g)) as ph:
            xin = ph.buf("xin", [128, nt, 1024])
            src = (dr["xp"][g * 512:(g + 1) * 512, :] if kind == "P" else dr["xs"]).rearrange("(t p) f -> p t f", p=128)
            P.op("sp", lambda e, src=src, xin=xin: e.dma_start(out=xin[:], in_=src), writes=[ph.k("xin")], is_dma=True)
            for k in range(8):
                ps, pk = self.psum()
                for t in range(nt):
                    P.op("pe", lambda e, ps=ps, t=t, k=k: e.transpose(out=ps[:, t * 128:(t + 1) * 128], in_=xin[:, t, k * 128:(k + 1) * 128],
                                                                        identity=self.ident[:]),
                         reads=[ph.k("xin"), "ident"], writes=[pk])
                P.op("act", lambda e, ps=ps, k=k: e.copy(out=self.xT[:, k, 0:T], in_=ps[:, 0:T]), reads=[pk], writes=["xT"])
        for layer in range(2):
            if layer == 0:
                self.even_mixer(kind, g, T)
            else:
                self.odd_mixer(kind, g, T)
            self.ffn(kind, g, T, layer)
        with self.phase("yout_%s%d" % (kind, g)) as ph:
            yo = ph.buf("yo", [128, nt, 1024])
            for t in range(nt):
                for k0 in range(0, 8, 4):
                    ps, pk = self.psum()
                    for k in range(k0, k0 + 4):
                        P.op("pe", lambda e, ps=ps, t=t, k=k, k0=k0: e.transpose(out=ps[:, (k - k0) * 128:(k - k0 + 1) * 128],
                                                                                  in_=self.xT[:, k, t * 128:(t + 1) * 128], identity=self.ident[:]),
                             reads=["xT", "ident"], writes=[pk])
                    P.op("act", lambda e, ps=ps, t=t, k0=k0: e.copy(out=yo[:, t, k0 * 128:(k0 + 4) * 128], in_=ps[:, 0:512]),
                         reads=[pk], writes=[ph.k("yo")])
            dst = (dr["y_p"][g * 512:(g + 1) * 512, :] if kind == "P" else dr["y_s"]).rearrange("(t p) f -> p t f", p=128)
            o = P.op("sp", lambda e, dst=dst, yo=yo: e.dma_start(out=dst, in_=yo[:]), reads=[ph.k("yo")], is_dma=True)
            P.final_waits.append(o.idx)

    def even_mixer(self, kind, g, T):
        P, dr = self.P, self.dram
        nt = T // 128
        isP = kind == "P"
        self.prenorm(0, T)
        hT = self.hT
        hreads = [("hT", k) for k in range(8)]

        def proj_fm(w, wk, col0, M, dst_fn, dst_key, func=None, psrows=128):
            ps, pk = self.psum()
            for k in range(8):
                P.op("pe", lambda e, ps=ps, k=k: e.matmul(ps[0:M, 0:T], lhsT=w[:, k, col0:col0 + M], rhs=hT[:, k, 0:T], start=(k == 0), stop=(k == 7)),
                     reads=[wk, ("hT", k)], writes=[pk])
            if func is None:
                P.op("act", lambda e, ps=ps: e.copy(out=dst_fn, in_=ps[0:M, 0:T]), reads=[pk], writes=[dst_key])
            else:
                P.op("act", lambda e, ps=ps: e.activation(out=dst_fn, in_=ps[0:M, 0:T], func=func), reads=[pk], writes=[dst_key])

        with self.phase("ev_%s%d" % (kind, g)) as ph:
            K = ph.k
            qt = ph.buf("qt", [128, 2, T], BF16)
            kt = ph.buf("kt", [128, 2, T], BF16)
            ebl_t = ph.buf("ebl", [128, 2, 16])
            khtok = ph.buf("khtok", [128, nt, 256], BF16)
            vtok = ph.buf("vtok", [128, nt, 512], BF16)
            sr = ph.buf("sr", [128, 4, T], BF16)
            cat = ph.buf("cat", [128, 8, T], BF16)
            oT = ph.buf("oT", [128, 4, T])
            if not isP:
                Ss = ph.buf("Ss", [128, 2, 16, 128])
                Ssb = ph.buf("Ssb", [128, 2, 16, 128], BF16)
                src = dr["sgla"].rearrange("i (c two) d v -> (two d) c i v", two=2)
                for c in range(2):
                    P.op("sp", lambda e, c=c, src=src: e.dma_start(out=Ss[:, c, :, :], in_=src[:, c, :, :]), writes=[K("Ss")], is_dma=True, multi=True)
                P.op("pool", lambda e: e.tensor_copy(out=Ssb[:], in_=Ss[:]), reads=[K("Ss")], writes=[K("Ssb")])
            with self.phase("evA_%s%d" % (kind, g)) as pa:
                KA = pa.k
                qk = pa.buf("qk", [128, 4, T])
                glr = pa.buf("glr", [16, T], BF16)
                la = pa.buf("la", [128, 2, T])
                enb = pa.buf("enb", [128, 2, T])
                eb = pa.buf("eb", [128, 2, T])
                kh = pa.buf("kh", [128, 2, T])
                w, wk = self.next_w("E0")
                for c in range(4):
                    proj_fm(w, wk, c * 128, 128, qk[:, c, 0:T], KA("qk", c))
                w, wk = self.next_w("E2")
                proj_fm(w, wk, 0, 16, glr[0:16, 0:T], KA("glr"))
                for c in range(2):
                    ps, pk = self.psum()
                    P.op("pe", lambda e, ps=ps, c=c: e.matmul(ps[:, 0:T], lhsT=self.wg2[0:16, c * 128:(c + 1) * 128], rhs=glr[0:16, 0:T], start=True, stop=True),
                         reads=["wg2", KA("glr")], writes=[pk])
                    P.op("act", lambda e, ps=ps, c=c: e.activation(out=la[:, c, :], in_=ps[:, 0:T], func=AF.Exp, bias=self.negb[:, c, :], scale=-1.0),
                         reads=[pk, "negb"], writes=[KA("la", c)])
                    P.op("act", lambda e, c=c: e.activation(out=la[:, c, :], in_=la[:, c, :], func=AF.Ln, bias=self.one_t[:], scale=1.0),
                         reads=[KA("la", c), "one_t"], writes=[KA("la", c)])
                    sm = self.scanmask_p if isP else self.scanmask_s
                    P.op("dve", lambda e, c=c, sm=sm: e.tensor_tensor_scan(out=la[:, c, :], data0=sm[:, 0:T], data1=la[:, c, :], initial=0.0,
                                                                          op0=ALU.mult, op1=ALU.add),
                         reads=[KA("la", c), "scanmask"], writes=[KA("la", c)])
                    P.op("act", lambda e, c=c: e.activation(out=eb[:, c, :], in_=la[:, c, :], func=AF.Exp, scale=-1.0 / 16.0), reads=[KA("la", c)], writes=[KA("eb", c)])
                    nseg, seg = (nt, 128) if isP else (16, 8)
                    ebv0 = eb[:, c, :]
                    ends = bass.AP(ebv0.tensor, ebv0.offset + seg - 1, [[ebv0.ap[0][0], 128], [seg, nseg]])
                    P.op("pool", lambda e, c=c, ends=ends, nseg=nseg: e.tensor_copy(out=ebl_t[:, c, 0:nseg], in_=ends), reads=[KA("eb", c)], writes=[K("ebl", c)])
                    P.op("act", lambda e, c=c: e.activation(out=enb[:, c, :], in_=la[:, c, :], func=AF.Exp, scale=1.0 / 16.0), reads=[KA("la", c)], writes=[KA("enb", c)])
                    P.op("dve", lambda e, c=c: e.scalar_tensor_tensor(out=qt[:, c, :], in0=qk[:, c, :], scalar=0.125, in1=eb[:, c, :], op0=ALU.mult, op1=ALU.mult),
                         reads=[KA("qk", c), KA("eb", c)], writes=[K("qt", c)])
                    P.op("dve", lambda e, c=c: e.tensor_tensor(out=kt[:, c, :], in0=qk[:, 2 + c, :], in1=enb[:, c, :], op=ALU.mult),
                         reads=[KA("qk", 2 + c), KA("enb", c)], writes=[K("kt", c)])
                    if isP:
                        for t in range(nt):
                            te = (t + 1) * 128
                            P.op("dve", lambda e, c=c, t=t, te=te: e.scalar_tensor_tensor(out=kh[:, c, t * 128:te], in0=qk[:, 2 + c, t * 128:te], scalar=eb[:, c, te - 1:te],
                                                                                         in1=enb[:, c, t * 128:te], op0=ALU.mult, op1=ALU.mult),
                                 reads=[KA("qk", 2 + c), KA("eb", c), KA("enb", c)], writes=[KA("kh", c)])
                    else:
                        P.op("dve", lambda e, c=c: e.tensor_tensor(out=kh[:, c, :], in0=qk[:, 2 + c, :], in1=enb[:, c, :], op=ALU.mult),
                             reads=[KA("qk", 2 + c), KA("enb", c)], writes=[KA("kh", c)])
                        ebv = eb[:, c, :]
                        ebl = bass.AP(ebv.tensor, ebv.offset + 7, [[ebv.ap[0][0], 128], [8, 16], [0, 8]])
                        P.op("dve", lambda e, c=c, ebl=ebl: e.tensor_tensor(out=kh[:, c, :].rearrange("p (i t) -> p i t", t=8), in0=kh[:, c, :].rearrange("p (i t) -> p i t", t=8),
                                                                           in1=ebl, op=ALU.mult),
                             reads=[KA("kh", c), KA("eb", c)], writes=[KA("kh", c)])
                pairs = [(t, c) for t in range(nt) for c in range(2)]
                for p0 in range(0, len(pairs), 4):
                    ps, pk = self.psum()
                    grp = pairs[p0:p0 + 4]
                    for i, (t, c) in enumerate(grp):
                        P.op("pe", lambda e, ps=ps, i=i, t=t, c=c: e.transpose(out=ps[:, i * 128:(i + 1) * 128], in_=kh[:, c, t * 128:(t + 1) * 128], identity=self.ident[:]),
                             reads=[KA("kh", c), "ident"], writes=[pk])
                    t0 = grp[0][0]
                    ntl = len(grp) // 2
                    P.op("act", lambda e, ps=ps, t0=t0, ntl=ntl: e.copy(out=khtok[:, t0:t0 + ntl, :], in_=ps[:, 0:ntl * 256].rearrange("p (t f) -> p t f", f=256)),
                         reads=[pk], writes=[K("khtok")])
                w, wk = self.next_w("E1")
                for t in range(nt):
                    ps, pk = self.psum()
                    for k in range(8):
                        P.op("pe", lambda e, ps=ps, k=k, t=t, w=w: e.matmul(ps[:, 0:512], lhsT=hT[:, k, t * 128:(t + 1) * 128], rhs=w[:, k, 0:512], start=(k == 0), stop=(k == 7)),
                             reads=[wk, ("hT", k)], writes=[pk])
                    P.op("act", lambda e, ps=ps, t=t: e.copy(out=vtok[:, t, :], in_=ps[:, 0:512]), reads=[pk], writes=[K("vtok", t)])
                w, wk = self.next_w("E3")
                for c in range(4):
                    proj_fm(w, wk, c * 128, 128, sr[:, c, 0:T], K("sr", c), func=AF.Silu)
            with self.phase("evB_%s%d" % (kind, g)) as pb:
                KB_ = pb.k
                at = [pb.buf("at%d" % i, [128, 128], BF16) for i in range(2)]
                mask = self.causal_st if isP else self.causal_s8
                if not isP:
                    vexp = [pb.buf("vexp%d" % i, [128, 16, 128], BF16) for i in range(2)]
                    tmpS = pb.buf("tmpS", [128, 4, 128])
                ai = 0
                for t in range(nt):
                    for h in range(4):
                        c, po = h // 2, (h % 2) * 64
                        a = at[ai % 2]
                        ak = KB_("at%d" % (ai % 2))
                        ai += 1
                        psa, pka = self.psum()
                        P.op("pe", lambda e, psa=psa, c=c, po=po, t=t: e.matmul(psa[:, 0:128], lhsT=kt[po:po + 64, c, t * 128:(t + 1) * 128], rhs=qt[po:po + 64, c, t * 128:(t + 1) * 128],
                                                                                 start=True, stop=True),
                             reads=[K("kt", c), K("qt", c)], writes=[pka])
                        P.op("dve", lambda e, psa=psa, a=a: e.tensor_tensor(out=a[:], in0=psa[:, 0:128], in1=mask[:], op=ALU.mult), reads=[pka, "masks"], writes=[ak])
                        pso, pko = self.psum()
                        P.op("pe", lambda e, pso=pso, a=a, t=t, h=h: e.matmul(pso[:, 0:128], lhsT=vtok[:, t, h * 128:(h + 1) * 128], rhs=a[:], start=True, stop=(not isP), skip_group_check=(not isP)),
                             reads=[K("vtok", t), ak], writes=[pko])
                        if isP:
                            P.op("pe", lambda e, pso=pso, c=c, po=po, t=t: e.matmul(pso[:, 0:128], lhsT=self.Sbf[po:po + 64, c, :], rhs=qt[po:po + 64, c, t * 128:(t + 1) * 128],
                                                                                     start=False, stop=True),
                                 reads=[("Sbf", h), K("qt", c)], writes=[pko])
                        else:
                            for i in range(16):
                                P.op("pe", lambda e, pso=pso, c=c, po=po, i=i: e.matmul(pso[:, 8 * i:8 * i + 8], lhsT=Ssb[po:po + 64, c, i, :], rhs=qt[po:po + 64, c, 8 * i:8 * i + 8],
                                                                                         start=False, stop=True, skip_group_check=True),
                                     reads=[K("Ssb"), K("qt", c)], writes=[pko])
                        P.op("act", lambda e, pso=pso, h=h, t=t: e.copy(out=oT[:, h, t * 128:(t + 1) * 128], in_=pso[:, 0:128]), reads=[pko], writes=[K("oT", h)])
                        if isP:
                            pss, pks = self.psum()
                            P.op("pe", lambda e, pss=pss, c=c, po=po, t=t, h=h: e.matmul(pss[po:po + 64, 0:128], lhsT=khtok[:, t, c * 128 + po:c * 128 + po + 64],
                                                                                          rhs=vtok[:, t, h * 128:(h + 1) * 128], start=True, stop=True),
                                 reads=[K("khtok"), K("vtok", t)], writes=[pks])
                            te = (t + 1) * 128
                            P.op("dve", lambda e, pss=pss, c=c, po=po, t=t: e.scalar_tensor_tensor(out=self.S[po:po + 64, c, :], in0=self.S[po:po + 64, c, :], scalar=ebl_t[po:po + 64, c, t:t + 1],
                                                                                                     in1=pss[po:po + 64, 0:128], op0=ALU.mult, op1=ALU.add),
                                 reads=[pks, ("S", h), K("ebl", c)], writes=[("S", h)])
                            P.op("act", lambda e, c=c, po=po: e.copy(out=self.Sbf[po:po + 64, c, :], in_=self.S[po:po + 64, c, :]), reads=[("S", h)], writes=[("Sbf", h)])
                        else:
                            vx = vexp[h % 2]
                            vk = KB_("vexp%d" % (h % 2))
                            vv = vtok[:, 0, h * 128:(h + 1) * 128]
                            vb = bass.AP(vv.tensor, vv.offset, [[vv.ap[0][0], 128], [0, 16], [1, 128]])
                            sm = self.seqmask[:]
                            smb = bass.AP(sm.tensor, sm.offset, [[sm.ap[0][0], 128], [1, 16], [0, 128]])
                            P.op("dve", lambda e, vx=vx, vb=vb, smb=smb: e.tensor_tensor(out=vx[:], in0=vb, in1=smb, op=ALU.mult), reads=[K("vtok", 0), "masks"], writes=[vk])
                            for blk in range(4):
                                pss, pks = self.psum()
                                P.op("pe", lambda e, pss=pss, c=c, po=po, vx=vx, blk=blk: e.matmul(pss[po:po + 64, 0:512], lhsT=khtok[:, 0, c * 128 + po:c * 128 + po + 64],
                                                                                                rhs=vx[:, 4 * blk:4 * blk + 4, :].rearrange("p i v -> p (i v)"), start=True, stop=True),
                                     reads=[K("khtok"), vk], writes=[pks])
                                ebv = ebl_t[po:po + 64, c, :]
                                ebl = bass.AP(ebv.tensor, ebv.offset + 4 * blk, [[ebv.ap[0][0], 64], [1, 4], [0, 128]])
                                P.op("dve", lambda e, c=c, po=po, blk=blk, ebl=ebl: e.tensor_tensor(out=tmpS[po:po + 64, :, :], in0=Ss[po:po + 64, c, 4 * blk:4 * blk + 4, :], in1=ebl, op=ALU.mult),
                                     reads=[K("Ss"), K("ebl", c)], writes=[KB_("tmpS", po)])
                                P.op("dve", lambda e, pss=pss, c=c, po=po, blk=blk: e.tensor_tensor(out=Ss[po:po + 64, c, 4 * blk:4 * blk + 4, :], in0=tmpS[po:po + 64, :, :],
                                                                                                  in1=pss[po:po + 64, 0:512].rearrange("p (i v) -> p i v", v=128), op=ALU.add),
                                     reads=[KB_("tmpS", po), pks], writes=[K("Ss")])
                if not isP:
                    dst = dr["s_gla"].rearrange("i (c two) d v -> (two d) c i v", two=2)
                    for c in range(2):
                        o_ = P.op("sp", lambda e, c=c, dst=dst: e.dma_start(out=dst[:, c, :, :], in_=Ss[:, c, :, :]), reads=[K("Ss")], is_dma=True)
                        P.final_waits.append(o_.idx)
                elif g == 3:
                    dst = dr["p_gla"].rearrange("(c two) d v -> (two d) c v", two=2)
                    o_ = P.op("sp", lambda e, dst=dst: e.dma_start(out=dst, in_=self.S[:]), reads=[("S", h) for h in range(4)], is_dma=True)
                    P.final_waits.append(o_.idx)
            with self.phase("evC_%s%d" % (kind, g)) as pc:
                KC_ = pc.k
                sqo = pc.buf("sqo", [128, T], BF16)
                rs = pc.buf("rs", [128, T])
                tmp = pc.buf("tmp", [128, T])
                for h in range(4):
                    P.op("act", lambda e, h=h: e.activation(out=sqo[:], in_=oT[:, h, :], func=AF.Square), reads=[K("oT", h)], writes=[KC_("sqo")])
                    ps, pk = self.psum()
                    P.op("pe", lambda e, ps=ps: e.matmul(ps[:, 0:T], lhsT=self.ones_bf[:], rhs=sqo[:], start=True, stop=True), reads=[KC_("sqo"), "ones_bf"], writes=[pk])
                    P.op("act", lambda e, ps=ps: e.activation(out=rs[:], in_=ps[:, 0:T], func=AF.Sqrt, bias=self.eps_t[:], scale=1.0 / 128.0), reads=[pk, "eps"], writes=[KC_("rs")])
                    P.op("dve", lambda e: e.reciprocal(out=rs[:], in_=rs[:]), reads=[KC_("rs")], writes=[KC_("rs")])
                    P.op("dve", lambda e, h=h: e.scalar_tensor_tensor(out=tmp[:], in0=oT[:, h, :], scalar=self.gn[:, 0, :], in1=rs[:], op0=ALU.mult, op1=ALU.mult),
                         reads=[K("oT", h), "gn", KC_("rs")], writes=[KC_("tmp")])
                    P.op("dve", lambda e, h=h: e.tensor_tensor(out=cat[:, h, :], in0=tmp[:], in1=sr[:, h, :], op=ALU.mult), reads=[KC_("tmp"), K("sr", h)], writes=[K("cat", h)])
            with self.phase("evD_%s%d" % (kind, g)) as pd:
                KD = pd.k
                if isP:
                    ub = pd.buf("ub", [128, 4, 30 + T])
                else:
                    ub = pd.buf("ub", [128, 4, 16, 38])
                    stg = pd.buf("stg", [120, 4, 512])
                    P.op("sp", lambda e: e.dma_start(out=stg[:], in_=dr["scb"].rearrange("(q a) j c -> (a j) q c", a=4)), writes=[KD("stg")], is_dma=True)
                    for q in range(4):
                        ps, pk = self.psum()
                        for c in range(4):
                            P.op("pe", lambda e, ps=ps, q=q, c=c: e.transpose(out=ps[:, c * 120:(c + 1) * 120], in_=stg[0:120, q, c * 128:(c + 1) * 128], identity=self.ident[0:120, 0:120]),
                                 reads=[KD("stg"), "ident"], writes=[pk])
                        P.op("act", lambda e, ps=ps, q=q: e.copy(out=ub[:, :, 4 * q:4 * q + 4, 0:30], in_=ps[:, 0:480].rearrange("p (c a j) -> p c a j", c=4, a=4)),
                             reads=[pk], writes=[KD("ub")])
                    o_ = P.op("sp", lambda e: e.dma_start(out=dr["s_conv_b"][:, 0:22, :], in_=dr["scb"][:, 8:30, :]), is_dma=True)
                    P.final_waits.append(o_.idx)
                sg = pd.buf("sg", [128, T])
                cc = pd.buf("cc", [128, 4, T])
                ccb = pd.buf("ccb", [128, 4, T], BF16)
                ccs = pd.buf("ccs", [128, 4, T], BF16)
                mean = pd.buf("mean", [128, T])
                var = pd.buf("var", [128, T])
                w4, wk4 = self.next_w("E4")
                w5, wk5 = self.next_w("E5")
                if isP:
                    P.op("pool", lambda e: e.tensor_copy(out=ub[:, :, 0:30], in_=self.uhalo[:]), reads=["uhalo"], writes=[KD("ub")])
                for c in range(4):
                    psa, pka = self.psum()
                    psg, pkg = self.psum()
                    for k in range(8):
                        P.op("pe", lambda e, psa=psa, k=k, c=c: e.matmul(psa[:, 0:T], lhsT=w4[:, k, c * 128:(c + 1) * 128], rhs=hT[:, k, 0:T], start=(k == 0), stop=(k == 7)),
                             reads=[wk4, ("hT", k)], writes=[pka])
                    for k in range(8):
                        P.op("pe", lambda e, psg=psg, k=k, c=c: e.matmul(psg[:, 0:T], lhsT=w5[:, k, c * 128:(c + 1) * 128], rhs=hT[:, k, 0:T], start=(k == 0), stop=(k == 7)),
                             reads=[wk5, ("hT", k)], writes=[pkg])
                    P.op("act", lambda e, psg=psg: e.activation(out=sg[:], in_=psg[:, 0:T], func=AF.Sigmoid), reads=[pkg], writes=[KD("sg")])
                    if isP:
                        udst, pv, sv = ub[:, c, 30:30 + T], psa[:, 0:T], sg[:]
                    else:
                        udst = ub[:, c, :, 30:38]
                        pv = psa[:, 0:T].rearrange("p (i t) -> p i t", t=8)
                        sv = sg[:].rearrange("p (i t) -> p i t", t=8)
                    P.op("dve", lambda e, udst=udst, pv=pv, sv=sv: e.tensor_tensor(out=udst, in0=pv, in1=sv, op=ALU.mult), reads=[pka, KD("sg")], writes=[KD("ub")])
                if isP:
                    P.op("pool", lambda e: e.tensor_copy(out=self.uhalo[:], in_=ub[:, :, T:T + 30]), reads=[KD("ub")], writes=["uhalo"])
                    if g == 3:
                        dst = dr["p_conv_b"].rearrange("j (c p) -> p c j", p=128)
                        for c in range(4):
                            o_ = P.op("sp", lambda e, c=c, dst=dst: e.dma_start(out=dst[:, c, :], in_=self.uhalo[:, c, :], allow_slow_non_contiguous=True), reads=["uhalo"], is_dma=True)
                            P.final_waits.append(o_.idx)
                def uview(c, j):
                    return ub[:, c, j:j + T] if isP else ub[:, c, :, j:j + 8]
                cvs = [cc[:, c, :] if isP else cc[:, c, :].rearrange("p (i t) -> p i t", t=8) for c in range(4)]
                for c in range(4):
                    P.op("dve", lambda e, c=c, cv=cvs[c], u0=uview(c, 0): e.tensor_scalar(out=cv, in0=u0, scalar1=self.cbp[:, c, 0:1], scalar2=self.cbp[:, c, 31:32], op0=ALU.mult, op1=ALU.add),
                         reads=[KD("ub"), "cbp"], writes=[KD("cc", c)])
                for j in range(1, 31):
                    for c in range(4):
                        P.op("dve", lambda e, c=c, cv=cvs[c], uj=uview(c, j), j=j: e.scalar_tensor_tensor(out=cv, in0=uj, scalar=self.cbp[:, c, j:j + 1], in1=cv, op0=ALU.mult, op1=ALU.add),
                             reads=[KD("ub"), "cbp", KD("cc", c)], writes=[KD("cc", c)])
                for c in range(4):
                    P.op("act", lambda e, c=c: e.copy(out=ccb[:, c, :], in_=cc[:, c, :]), reads=[KD("cc", c)], writes=[KD("ccb", c)])
                    P.op("act", lambda e, c=c: e.activation(out=ccs[:, c, :], in_=cc[:, c, :], func=AF.Square), reads=[KD("cc", c)], writes=[KD("ccs", c)])
                if not isP:
                    un = pd.buf("un", [128, 512])
                    psu, pku = self.psum()
                    uc = pd.buf("uc", [128, 4, 128])
                    P.op("pool", lambda e: e.tensor_copy(out=uc[:].rearrange("p c (i t) -> p c i t", t=8), in_=ub[:, :, :, 30:38]), reads=[KD("ub")], writes=[KD("uc")])
                    for c in range(4):
                        P.op("pe", lambda e, c=c: e.transpose(out=psu[:, c * 128:(c + 1) * 128], in_=uc[:, c, :], identity=self.ident[:]), reads=[KD("uc"), "ident"], writes=[pku])
                    P.op("act", lambda e: e.copy(out=un[:], in_=psu[:, 0:512]), reads=[pku], writes=[KD("un")])
                    for i in range(16):
                        o_ = P.op("sp", lambda e, i=i: e.dma_start(out=dr["s_conv_b"][i, 22:30, :], in_=un[8 * i:8 * i + 8, :]), reads=[KD("un")], is_dma=True)
                        P.final_waits.append(o_.idx)
                ps1, pk1 = self.psum()
                ps2, pk2 = self.psum()
                for c in range(4):
                    P.op("pe", lambda e, c=c: e.matmul(ps1[:, 0:T], lhsT=self.ones_bf[:], rhs=ccb[:, c, :], start=(c == 0), stop=(c == 3)), reads=[KD("ccb", c), "ones_bf"], writes=[pk1])
                for c in range(4):
                    P.op("pe", lambda e, c=c: e.matmul(ps2[:, 0:T], lhsT=self.ones_bf[:], rhs=ccs[:, c, :], start=(c == 0), stop=(c == 3)), reads=[KD("ccs", c), "ones_bf"], writes=[pk2])
                P.op("act", lambda e: e.activation(out=mean[:], in_=ps1[:, 0:T], func=AF.Copy, scale=1.0 / 512.0), reads=[pk1], writes=[KD("mean")])
                P.op("dve", lambda e: e.tensor_tensor(out=var[:], in0=mean[:], in1=mean[:], op=ALU.mult), reads=[KD("mean")], writes=[KD("var")])
                P.op("dve", lambda e: e.scalar_tensor_tensor(out=var[:], in0=ps2[:, 0:T], scalar=1.0 / 512.0, in1=var[:], op0=ALU.mult, op1=ALU.subtract), reads=[pk2, KD("var")], writes=[KD("var")])
                P.op("act", lambda e: e.activation(out=var[:], in_=var[:], func=AF.Sqrt, bias=self.eps_t[:], scale=1.0), reads=[KD("var"), "eps"], writes=[KD("var")])
                P.op("dve", lambda e: e.reciprocal(out=var[:], in_=var[:]), reads=[KD("var")], writes=[KD("var")])
                for c in range(4):
                    P.op("dve", lambda e, c=c: e.tensor_tensor(out=cc[:, c, :], in0=cc[:, c, :], in1=mean[:], op=ALU.subtract), reads=[KD("cc", c), KD("mean")], writes=[KD("cc", c)])
                    P.op("dve", lambda e, c=c: e.tensor_tensor(out=cc[:, c, :], in0=cc[:, c, :], in1=var[:], op=ALU.mult), reads=[KD("cc", c), KD("var")], writes=[KD("cc", c)])
                    P.op("act", lambda e, c=c: e.activation(out=cat[:, 4 + c, :], in_=cc[:, c, :], func=AF.Silu, bias=self.cbp[:, c, 33:34], scale=self.cbp[:, c, 32:33]),
                         reads=[KD("cc", c), "cbp"], writes=[K("cat", 4 + c)])
            for half in range(2):
                w, wk = self.next_w("EO%d" % half)
                for oc in range(4):
                    o = half * 4 + oc
                    ps, pk = self.psum()
                    for k in range(8):
                        P.op("pe", lambda e, ps=ps, k=k, oc=oc, w=w: e.matmul(ps[:, 0:T], lhsT=w[:, k, oc * 128:(oc + 1) * 128], rhs=cat[:, k, :], start=(k == 0), stop=(k == 7)),
                             reads=[wk, K("cat", k)], writes=[pk])
                    P.op("act", lambda e, ps=ps, o=o: e.copy(out=self.yT[:, o, 0:T], in_=ps[:, 0:T]), reads=[pk], writes=["yT"])
        self.postnorm_residual(2, T)

    def build_bias_tables(self):
        P, dr, nc = self.P, self.dram, self.nc
        self.LU = nc.dram_tensor("LU", [3, 4, 640], F32, kind="Internal").ap()
        self.LD = nc.dram_tensor("LD", [4, 4, 1024], F32, kind="Internal").ap()
        dil = (1, 4, 16)
        with self.phase("bias") as ph:
            K = ph.k
            rb = ph.buf("rb", [32, 12])
            oh = ph.buf("oh", [32, 3, 129])
            ebv = ph.buf("ebv", [4, 3, 129])
            zt = ph.buf("zt", [4, 1024])
            tp = [ph.buf("tp%d" % i, [128, 512]) for i in range(2)]
            P.op("sp", lambda e: e.dma_start(out=rb[:], in_=dr["rel_bias"]), writes=[K("rb")], is_dma=True)
            P.op("sp", lambda e: e.dma_start(out=oh[:], in_=dr["c_onehot"]), writes=[K("oh")], is_dma=True)
            P.op("pool", lambda e: e.memset(zt[:], 0.0), writes=[K("zt")])
            for g in range(3):
                ps, pk = self.psum()
                P.op("pe", lambda e, ps=ps, g=g: e.matmul(ps[0:4, 0:129], lhsT=rb[0:32, 4 * g:4 * g + 4], rhs=oh[0:32, g, :], start=True, stop=True),
                     reads=[K("rb"), K("oh")], writes=[pk])
                P.op("act", lambda e, ps=ps, g=g: e.activation(out=ebv[0:4, g, :], in_=ps[0:4, 0:129], func=AF.Exp), reads=[pk], writes=[K("ebv")])
                P.op("sp", lambda e, g=g: e.dma_start(out=self.LU[g], in_=zt[:, 0:640]), reads=[K("zt")], writes=[("LU", g)], is_dma=True)
                P.op("sp", lambda e, g=g: e.dma_start(out=self.LD[g], in_=zt[:, 0:1024]), reads=[K("zt")], writes=[("LD", g)], is_dma=True)
                P.op("sp", lambda e, g=g: e.dma_start(out=self.LU[g, :, 128:257], in_=ebv[0:4, g, :]), reads=[K("ebv")], writes=[("LU", g)], is_dma=True)
                nj = 129 if g < 2 else 33
                ldv = self.LD[g]
                dst = bass.AP(ldv.tensor, ldv.offset + 8, [[1024, 4], [dil[g], nj]])
                P.op("sp", lambda e, g=g, dst=dst, nj=nj: e.dma_start(out=dst, in_=ebv[0:4, g, 0:nj], allow_slow_non_contiguous=True),
                     reads=[K("ebv")], writes=[("LD", g)], is_dma=True)
            P.op("sp", lambda e: e.dma_start(out=self.LD[3], in_=zt[:, 0:1024]), reads=[K("zt")], writes=[("LD", 3)], is_dma=True)
            ldv = self.LD[3]
            dst = bass.AP(ldv.tensor, ldv.offset + 128, [[1024, 4], [4, 129]])
            P.op("sp", lambda e, dst=dst: e.dma_start(out=dst, in_=ebv[0:4, 2, :], allow_slow_non_contiguous=True), reads=[K("ebv")], writes=[("LD", 3)], is_dma=True)
            ti = 0

            def finish(t, tk, dst_ap, dst_key, ncols):
                ps, pk = self.psum()
                P.op("pe", lambda e, ps=ps, t=t: e.matmul(ps[:, 0:ncols], lhsT=self.Jrev[:], rhs=t[:, 0:ncols], start=True, stop=True), reads=[tk, "Jrev"], writes=[pk])
                P.op("act", lambda e, ps=ps: e.copy(out=dst_ap, in_=ps[:, 0:ncols]), reads=[pk], writes=[dst_key])

            for g in range(2):
                for wv in range(2):
                    t, tk = tp[ti % 2], K("tp%d" % (ti % 2))
                    ti += 1
                    lv = self.LU[g]
                    src = bass.AP(lv.tensor, lv.offset + (1 if wv == 0 else 129), [[1, 128], [640, 4], [1, 128]])
                    P.op("sp", lambda e, t=t, src=src: e.dma_start(out=t[:].rearrange("p (h q) -> p h q", h=4), in_=src), reads=[("LU", g)], writes=[tk], is_dma=True)
                    finish(t, tk, self.tabs[:, 2 * g + wv, :], ("tabs", 2 * g + wv), 512)
            for dG in range(4):
                t, tk = tp[ti % 2], K("tp%d" % (ti % 2))
                ti += 1
                lv = self.LD[3]
                src = bass.AP(lv.tensor, lv.offset + 1 + 128 * dG, [[1, 128], [1024, 4], [1, 128]])
                P.op("sp", lambda e, t=t, src=src: e.dma_start(out=t[:].rearrange("p (h q) -> p h q", h=4), in_=src), reads=[("LD", 3)], writes=[tk], is_dma=True)
                finish(t, tk, self.tabs[:, 4 + dG, :], ("tabs", 4 + dG), 512)
            for g in range(3):
                t, tk = tp[ti % 2], K("tp%d" % (ti % 2))
                ti += 1
                P.op("pool", lambda e, t=t: e.memset(t[:], 0.0), writes=[tk])
                lv = self.LD[g]
                for b in range(16):
                    B_ = 15 - b
                    src = bass.AP(lv.tensor, lv.offset + 1, [[1, 8], [1024, 4], [1, 8]])
                    dstv = t[8 * B_:8 * B_ + 8, :].rearrange("p (h q) -> p h q", h=4)[:, :, 8 * b:8 * b + 8]
                    P.op("sp", lambda e, dstv=dstv, src=src: e.dma_start(out=dstv, in_=src), reads=[("LD", g)], writes=[tk], is_dma=True, multi=True)
                finish(t, tk, self.tabn[:, g, :], ("tabn", g), 512)
            t, tk = tp[ti % 2], K("tp%d" % (ti % 2))
            ti += 1
            P.op("pool", lambda e, t=t: e.memset(t[:], 0.0), writes=[tk])
            tv = t[:, 0:416].rearrange("p (h q) -> p h q", h=4)
            lv = self.LU[0]
            src = bass.AP(lv.tensor, lv.offset + 129, [[1, 128], [640, 4], [1, 8]])
            P.op("sp", lambda e, src=src: e.dma_start(out=tv[:, :, 0:8], in_=src), reads=[("LU", 0)], writes=[tk], is_dma=True, multi=True)
            lv = self.LD[1]
            for a in range(4):
                src = bass.AP(lv.tensor, lv.offset + 8 + 385 - 128 * a, [[1, 128], [1024, 4], [1, 8]])
                P.op("sp", lambda e, src=src, a=a: e.dma_start(out=tv[:, :, 8 + 8 * a:16 + 8 * a], in_=src), reads=[("LD", 1)], writes=[tk], is_dma=True, multi=True)
            lv = self.LU[2]
            for s in range(8):
                src = bass.AP(lv.tensor, lv.offset + 129, [[1, 128], [640, 4], [1, 1]])
                P.op("sp", lambda e, src=src, s=s: e.dma_start(out=tv[:, :, 40 + 9 * s:41 + 9 * s], in_=src, allow_slow_non_contiguous=True), reads=[("LU", 2)], writes=[tk], is_dma=True, multi=True)
            finish(t, tk, self.tabc[:, 0:416], "tabc", 416)

    def odd_mixer(self, kind, g, T):
        if "stubodd" in DBG:
            for n in ["O0", "O1", "O2", "O4", "O3", "O5", "O6", "OO0", "OO1"]:
                self.next_w(n)
            return
        P, dr = self.P, self.dram
        G = g
        nt = T // 128
        isP = kind == "P"
        self.prenorm(1, T)
        hT = self.hT

        def tok(base, off, dims):
            return bass.AP(base.tensor, base.offset + off, [list(base.ap[0])] + [list(d) for d in dims])

        with self.phase("od_%s%d" % (kind, g)) as ph:
            K = ph.k
            qT = ph.buf("qT", [128, 6, T], BF16)
            catO = ph.buf("catO", [128, 4, T], BF16)
            uT = ph.buf("uT", [128, 2, T])
            esb = [ph.buf("esb%d" % i, [128, 512]) for i in range(2)]
            pT = [ph.buf("pT%d" % i, [128, 4, 128], BF16) for i in range(2)]
            kvs = [ph.buf("kvs%d" % i, [128, 512]) for i in range(2)]
            if not isP:
                kTn = ph.buf("kTn", [128, 6, 128], BF16)
                vnew = ph.buf("vnew", [128, 3, 256], BF16)
            for bn, c0, ncn in (("O0", 0, 4), ("O1", 4, 2)):
                w, wk = self.next_w(bn)
                for ci in range(ncn):
                    cg = c0 + ci
                    ps, pk = self.psum()
                    for k in range(8):
                        P.op("pe", lambda e, ps=ps, k=k, ci=ci, w=w: e.matmul(ps[:, 0:T], lhsT=w[:, k, ci * 128:(ci + 1) * 128], rhs=hT[:, k, 0:T], start=(k == 0), stop=(k == 7)),
                             reads=[wk, ("hT", k)], writes=[pk])
                    P.op("act", lambda e, ps=ps, cg=cg: e.activation(out=qT[:, cg, :], in_=ps[:, 0:T], func=AF.Copy, scale=0.125), reads=[pk], writes=[K("qT", cg)])
            if ODDSTOP <= 1:
                for n in ["O2", "O4", "O3", "O5", "O6", "OO0", "OO1"]:
                    self.next_w(n)
                return
            def kproj(w, wk, c0, ncn):
                for ci in range(ncn):
                    cg = c0 + ci
                    gg, hp = cg // 2, cg % 2
                    ps, pk = self.psum()
                    for k in range(8):
                        P.op("pe", lambda e, ps=ps, k=k, ci=ci, w=w: e.matmul(ps[:, 0:T], lhsT=w[:, k, ci * 128:(ci + 1) * 128], rhs=hT[:, k, 0:T], start=(k == 0), stop=(k == 7)),
                             reads=[wk, ("hT", k)], writes=[pk])
                    if not isP:
                        dst, dk = kTn[:, cg, :], K("kTn", cg)
                    elif gg == 0:
                        dst, dk = self.kT0[:, hp, 128:640], ("kT0", hp)
                    elif gg == 1:
                        dst, dk = self.kT1[:, hp, G % 2, :], ("kT1", hp, G % 2)
                    else:
                        dst, dk = self.kT2[:, hp, 512 * G:512 * G + 512], ("kT2", hp)
                    P.op("act", lambda e, ps=ps, dst=dst: e.copy(out=dst, in_=ps[:, 0:T]), reads=[pk], writes=[dk])

            self.kvi = 0

            def kv_tile(wkv, wkk, kcol, wvv, wvk, vcol, lhs_cols, need_k, vdst, vkey, out_ap):
                ps, pk = self.psum()
                if need_k:
                    for k in range(8):
                        P.op("pe", lambda e, ps=ps, k=k: e.matmul(ps[:, 0:256], lhsT=lhs_cols(k), rhs=wkv[:, k, kcol:kcol + 256], start=(k == 0), stop=(k == 7)),
                             reads=[wkk, ("hT", k)], writes=[pk])
                for k in range(8):
                    P.op("pe", lambda e, ps=ps, k=k: e.matmul(ps[:, 256:512], lhsT=lhs_cols(k), rhs=wvv[:, k, vcol:vcol + 256], start=(k == 0), stop=(k == 7)),
                         reads=[wvk, ("hT", k)], writes=[pk])
                if not need_k:
                    P.op("act", lambda e, ps=ps: e.copy(out=vdst, in_=ps[:, 256:512]), reads=[pk], writes=[vkey])
                else:
                    st_ = kvs[self.kvi % 2]
                    sk = K("kvs%d" % (self.kvi % 2))
                    self.kvi += 1
                    P.op("dve", lambda e, ps=ps, st_=st_: e.tensor_copy(out=st_[:], in_=ps[:, 0:512]), reads=[pk], writes=[sk])
                    P.op("act", lambda e, st_=st_: e.copy(out=vdst, in_=st_[:, 256:512]), reads=[sk], writes=[vkey])
                    for oa in ([] if "nokvout" in DBG else out_ap):
                        o_ = P.op("sp", lambda e, oa=oa, st_=st_: e.dma_start(out=oa[0], in_=oa[1](st_)), reads=[sk], is_dma=True)
                        P.final_waits.append(o_.idx)

            def sample_outs(gg):
                W_ = (128, 512, 2048)[gg]
                pv = dr[("s_kv128", "s_kv512", "s_kv2048")[gg]]
                return [(pv[i, W_ - 8:W_, :], (lambda s, i=i: s[8 * i:8 * i + 8, :])) for i in range(16)]

            w2, wk2 = self.next_w("O2")
            kproj(w2, wk2, 0, 4)
            w4, wk4 = self.next_w("O4")
            if isP:
                for a in range(4):
                    need = (G == 3 and a == 3)
                    outs = [(dr["p_kv128"], lambda s: s[:])] if need else []
                    kv_tile(w2, wk2, 0, w4, wk4, 0, lambda k, a=a: hT[:, k, a * 128:(a + 1) * 128], need, self.v0[:, a + 1, :], ("v0", a + 1), outs)
                for r in range(4):
                    need = (G == 3)
                    pv = dr["p_kv512"]
                    oa = bass.AP(pv.tensor, pv.offset + r * 512, [[4 * 512, 128], [1, 512]])
                    outs = [(oa, lambda s: s[:])] if need else []
                    kv_tile(w2, wk2, 256, w4, wk4, 256, (lambda k, r=r: hT[:, k, r * 128:(r + 1) * 128]) if "nostride" in DBG else (lambda k, r=r: tok(hT[:, k, :], r, [[4, 128]])), need, self.v1[:, G % 2, r, :], ("v1", G % 2, r), outs)
            else:
                for gg in range(2):
                    kv_tile(w2, wk2, 256 * gg, w4, wk4, 256 * gg, lambda k: hT[:, k, 0:128], True, vnew[:, gg, :], K("vnew", gg), sample_outs(gg))
            if ODDSTOP <= 2:
                for n in ["O3", "O5", "O6", "OO0", "OO1"]:
                    self.next_w(n)
                return
            w3, wk3 = self.next_w("O3")
            kproj(w3, wk3, 4, 2)
            w5, wk5 = self.next_w("O5")
            if isP:
                for rq in range(4):
                    pv = dr["p_kv2048"]
                    oa = bass.AP(pv.tensor, pv.offset + (512 * G + rq) * 512, [[4 * 512, 128], [1, 512]])
                    kv_tile(w3, wk3, 0, w5, wk5, 0, lambda k, rq=rq: tok(hT[:, k, :], rq, [[4, 128]]), True, self.v2[:, 4 * G + rq, :], ("v2", 4 * G + rq), [(oa, lambda s: s[:])])
            else:
                kv_tile(w3, wk3, 0, w5, wk5, 0, lambda k: hT[:, k, 0:128], True, vnew[:, 2, :], K("vnew", 2), sample_outs(2))
            if ODDSTOP <= 3:
                for n in ["O6", "OO0", "OO1"]:
                    self.next_w(n)
                return
            w6, wk6 = self.next_w("O6")
            for c in range(2):
                ps, pk = self.psum()
                for k in range(8):
                    P.op("pe", lambda e, ps=ps, k=k, c=c: e.matmul(ps[:, 0:T], lhsT=w6[:, k, c * 128:(c + 1) * 128], rhs=hT[:, k, 0:T], start=(k == 0), stop=(k == 7)),
                         reads=[wk6, ("hT", k)], writes=[pk])
                P.op("act", lambda e, ps=ps, c=c: e.activation(out=uT[:, c, :], in_=ps[:, 0:T], func=AF.Gelu_apprx_tanh), reads=[pk], writes=[K("uT", c)])
            with self.phase("odG_%s%d" % (kind, g)) as pg:
                KG = pg.k
                gv = pg.buf("gv", [128, 256])
                st6 = pg.buf("st6", [128, 6])
                mv = pg.buf("mv", [128, 2])
                rs_ = pg.buf("rs_", [128, 1])
                vn = pg.buf("vn", [128, 256])
                vnb = pg.buf("vnb", [128, 256], BF16)
                tmpm = pg.buf("tmpm", [128, 256])
                wsT = self.wsT if isP else self.wsTs
                sgub = self.sgub if isP else self.sgubs
                for t in range(nt):
                    ps, pk = self.psum()
                    for k in range(8):
                        P.op("pe", lambda e, ps=ps, k=k, t=t: e.matmul(ps[:, 0:256], lhsT=hT[:, k, t * 128:(t + 1) * 128], rhs=w6[:, k, 256:512], start=(k == 0), stop=(k == 7)),
                             reads=[wk6, ("hT", k)], writes=[pk])
                    P.op("act", lambda e, ps=ps: e.activation(out=gv[:], in_=ps[:, 0:256], func=AF.Gelu_apprx_tanh), reads=[pk], writes=[KG("gv")])
                    P.op("dve", lambda e: e.bn_stats(out=st6[:], in_=gv[:]), reads=[KG("gv")], writes=[KG("st6")])
                    P.op("dve", lambda e: e.bn_aggr(out=mv[:], in_=st6[:]), reads=[KG("st6")], writes=[KG("mv")])
                    P.op("act", lambda e: e.activation(out=rs_[:], in_=mv[:, 1:2], func=AF.Sqrt, bias=self.eps_t[:], scale=1.0), reads=[KG("mv"), "eps"], writes=[KG("rs_")])
                    P.op("dve", lambda e: e.reciprocal(out=rs_[:], in_=rs_[:]), reads=[KG("rs_")], writes=[KG("rs_")])
                    P.op("dve", lambda e: e.tensor_scalar(out=vn[:], in0=gv[:], scalar1=mv[:, 0:1], scalar2=rs_[:, 0:1], op0=ALU.subtract, op1=ALU.mult),
                         reads=[KG("gv"), KG("mv"), KG("rs_")], writes=[KG("vn")])
                    P.op("dve", lambda e: e.tensor_tensor(out=vn[:], in0=vn[:], in1=self.lngb[:, 0, :], op=ALU.mult), reads=[KG("vn"), "lngb"], writes=[KG("vn")])
                    P.op("dve", lambda e: e.tensor_tensor(out=vn[:], in0=vn[:], in1=self.lngb[:, 1, :], op=ALU.add), reads=[KG("vn"), "lngb"], writes=[KG("vn")])
                    P.op("act", lambda e: e.copy(out=vnb[:], in_=vn[:]), reads=[KG("vn")], writes=[KG("vnb")])
                    if not isP:
                        o_ = P.op("sp", lambda e: e.dma_start(out=dr["s_sgu_v"], in_=vn[:]), reads=[KG("vn")], is_dma=True)
                        P.final_waits.append(o_.idx)
                    psm, pkm = self.psum()
                    for gq in range(4):
                        c, po = gq // 2, (gq % 2) * 64
                        P.op("pe", lambda e, psm=psm, gq=gq, c=c, po=po: e.matmul(psm[po:po + 64, c * 128:(c + 1) * 128], lhsT=vnb[:, gq * 64:(gq + 1) * 64], rhs=wsT[:, gq, :],
                                                                                 start=True, stop=True),
                             reads=[KG("vnb"), "wsT"], writes=[pkm])
                    P.op("dve", lambda e, psm=psm: e.tensor_tensor(out=tmpm[:], in0=psm[:, 0:256], in1=sgub[:].rearrange("p c t -> p (c t)"), op=ALU.add), reads=[pkm, "sgub"], writes=[KG("tmpm")])
                    P.op("dve", lambda e, t=t: e.tensor_tensor(out=catO[:, 2:4, t * 128:(t + 1) * 128], in0=tmpm[:].rearrange("p (c t) -> p c t", c=2), in1=uT[:, :, t * 128:(t + 1) * 128], op=ALU.mult),
                         reads=[KG("tmpm"), K("uT", 0), K("uT", 1)], writes=[K("catO", 2), K("catO", 3)])
            if ODDSTOP <= 4:
                for n in ["OO0", "OO1"]:
                    self.next_w(n)
                return
            acc = ph.buf("acc", [128, 4, T])
            P.op("pool", lambda e: e.memset(acc[:], 0.0), writes=[K("acc")])
            o_ps = d_ps = okeys = dkeys = None
            self.pti = 0

            pend_pv = [None]

            def att_pair(kT_fn, q_cg, q_pat, v_fn, vkey, kkeys, tab, tabkey, ncols=128, col0=0, out_pat=None):
                i = self.pti
                self.pti += 1
                es, ek = esb[i % 2], K("esb%d" % (i % 2))
                pt, ptk = pT[i % 2], K("pT%d" % (i % 2))
                pss = [self.psum(), self.psum()]
                for h in range(4):
                    hp, po = h // 2, (h % 2) * 64
                    ps, pk = pss[h % 2]
                    rhs = tok(qT[po:po + 64, q_cg * 2 + hp, :], q_pat[0], q_pat[1])
                    P.op("pe", lambda e, ps=ps, hp=hp, rhs=rhs, l=kT_fn(h): e.matmul(ps[:, hp * 128:hp * 128 + ncols], lhsT=l, rhs=rhs, start=True, stop=True, skip_group_check=True),
                         reads=kkeys + [K("qT", q_cg * 2 + hp)], writes=[pk])
                if pend_pv[0] is not None:
                    pend_pv[0]()
                    pend_pv[0] = None
                for par in range(2):
                    ps, pk = pss[par]
                    esv = es[:].rearrange("p (hp par q) -> p hp par q", hp=2, par=2)[:, :, par, :]
                    P.op("act", lambda e, ps=ps, esv=esv: e.activation(out=esv, in_=ps[:, 0:256].rearrange("p (hp q) -> p hp q", hp=2), func=AF.Exp), reads=[pk], writes=[ek])
                P.op("dve", lambda e, es=es, pt=pt, tab=tab: e.tensor_tensor(out=pt[:].rearrange("p h q -> p (h q)"), in0=es[:], in1=tab, op=ALU.mult), reads=[ek, tabkey], writes=[ptk])
                op_ = out_pat if out_pat is not None else q_pat
                pend_pv[0] = lambda: pv_part(pt, ptk, v_fn, vkey, ncols, op_)

            def flush_pv():
                if pend_pv[0] is not None:
                    pend_pv[0]()
                    pend_pv[0] = None

            def pv_part(pt, ptk, v_fn, vkey, ncols, op_):
                pv, pvk = self.psum()
                for h in range(4):
                    hp, po = h // 2, (h % 2) * 64
                    P.op("pe", lambda e, pv=pv, h=h, hp=hp, po=po, pt=pt, l=v_fn(h): e.matmul(pv[po:po + 64, hp * 128:hp * 128 + ncols], lhsT=l, rhs=pt[:, h, 0:ncols], start=True, stop=True, skip_group_check=True),
                         reads=[vkey, ptk], writes=[pvk])
                    P.op("pe", lambda e, pv=pv, h=h, hp=hp, po=po, pt=pt: e.matmul(pv[po:po + 64, (2 + hp) * 128:(2 + hp) * 128 + ncols], lhsT=self.ones_bf[:, 0:64], rhs=pt[:, h, 0:ncols], start=True, stop=True, skip_group_check=True),
                         reads=["ones_bf", ptk], writes=[pvk])
                av = acc[:, :, :]
                accv = bass.AP(av.tensor, av.offset + op_[0], [list(av.ap[0]), [T, 4]] + [list(d) for d in op_[1]])
                P.op("dve", lambda e, pv=pv, accv=accv: e.tensor_tensor(out=accv, in0=pv[:, 0:512].rearrange("p (j q) -> p j q", j=4), in1=accv, op=ALU.add),
                     reads=[pvk, K("acc")], writes=[K("acc")])

            if "noatt" in DBG:
                P.op("pool", lambda e: e.memset(acc[:], 1.0), writes=[K("acc")])
            elif isP:
                for a in range(4):
                    for wv, ap in ((0, a), (1, a - 1)):
                        if G == 0 and ap < 0:
                            continue
                        att_pair(lambda h, ap=ap: self.kT0[(h % 2) * 64:(h % 2) * 64 + 64, h // 2, 128 + 128 * ap:256 + 128 * ap], 0, (128 * a, [[1, 128]]),
                                 lambda h, ap=ap: self.v0[:, ap + 1, h * 64:(h + 1) * 64], ("v0", ap + 1), [("kT0", 0), ("kT0", 1)], self.tabs[:, wv, :], ("tabs", wv))
                for r in range(4):
                    for wv, Gp in ((0, G), (1, G - 1)):
                        if Gp < 0:
                            continue
                        att_pair(lambda h, r=r, Gp=Gp: tok(self.kT1[(h % 2) * 64:(h % 2) * 64 + 64, h // 2, Gp % 2, :], r, [[4, 128]]), 1, (r, [[4, 128]]),
                                 lambda h, r=r, Gp=Gp: self.v1[:, Gp % 2, r, h * 64:(h + 1) * 64], ("v1", Gp % 2, r), [("kT1", 0, Gp % 2), ("kT1", 1, Gp % 2)],
                                 self.tabs[:, 2 + wv, :], ("tabs", 2 + wv))
                for rq in range(4):
                    for Gp in range(G + 1):
                        att_pair(lambda h, rq=rq, Gp=Gp: tok(self.kT2[(h % 2) * 64:(h % 2) * 64 + 64, h // 2, :], 512 * Gp + rq, [[4, 128]]), 2,
                                 (rq, [[4, 128]]), lambda h, rq=rq, Gp=Gp: self.v2[:, 4 * Gp + rq, h * 64:(h + 1) * 64], ("v2", 4 * Gp + rq),
                                 [("kT2", 0), ("kT2", 1)], self.tabs[:, 4 + G - Gp, :], ("tabs", 4 + G - Gp))
                flush_pv()
                for hp in range(2):
                    P.op("pool", lambda e, hp=hp: e.tensor_copy(out=self.kT0[:, hp, 0:128], in_=self.kT0[:, hp, 512:640]), reads=[("kT0", hp)], writes=[("kT0", hp)])
                P.op("pool", lambda e: e.tensor_copy(out=self.v0[:, 0, :], in_=self.v0[:, 4, :]), reads=[("v0", 4)], writes=[("v0", 0)])
            else:
                for gg in range(3):
                    att_pair(lambda h, gg=gg: kTn[(h % 2) * 64:(h % 2) * 64 + 64, gg * 2 + h // 2, :], gg, (0, [[1, 128]]),
                             lambda h, gg=gg: vnew[:, gg, h * 64:(h + 1) * 64], K("vnew", gg), [K("kTn", gg * 2), K("kTn", gg * 2 + 1)], self.tabn[:, gg, :], ("tabn", gg))
                flush_pv()
                if "nocache" not in DBG:
                    self.sample_cache_attention(ph, qT, acc, esb, pT)
            with self.phase("odN_%s%d" % (kind, g)) as pn:
                rd = pn.buf("rd", [128, 2, T])
                P.op("dve", lambda e: e.reciprocal(out=rd[:], in_=acc[:, 2:4, :]), reads=[K("acc")], writes=[pn.k("rd")])
                P.op("dve", lambda e: e.tensor_tensor(out=catO[:, 0:2, :], in0=acc[:, 0:2, :], in1=rd[:], op=ALU.mult), reads=[K("acc"), pn.k("rd")], writes=[K("catO", 0), K("catO", 1)])
            for half in range(2):
                w, wk = self.next_w("OO%d" % half)
                for oc in range(4):
                    o = half * 4 + oc
                    ps, pk = self.psum()
                    for k in range(4):
                        P.op("pe", lambda e, ps=ps, k=k, oc=oc, w=w: e.matmul(ps[:, 0:T], lhsT=w[:, k, oc * 128:(oc + 1) * 128], rhs=catO[:, k, :], start=(k == 0), stop=(k == 3)),
                             reads=[wk, K("catO", k)], writes=[pk])
                    P.op("act", lambda e, ps=ps, o=o: e.copy(out=self.yT[:, o, 0:T], in_=ps[:, 0:T]), reads=[pk], writes=["yT"])
        self.postnorm_residual(3, T)

    def sample_cache_attention(self, ph, qT, acc, esb, pT):
        P, dr = self.P, self.dram
        K = ph.k
        with self.phase("odS") as pc:
            KC = pc.k
            stg = [pc.buf("cst%d" % i, [128, 13, 512]) for i in range(1)]
            kTc = [pc.buf("kTc%d" % i, [128, 26, 128], BF16) for i in range(2)]
            vc = [pc.buf("vc%d" % i, [128, 13, 256], BF16) for i in range(1)]
            psum_ = pc.buf("ptsum", [128, 32])
            pend_c = [None]

            def pv_c(b, v_, vk, ptv, ptk):
                if True:
                    psb, pbk = self.psum()
                    for h in range(4):
                        hp, po = h // 2, (h % 2) * 64
                        for tl in range(13):
                            P.op("pe", lambda e, psb=psb, h=h, hp=hp, po=po, tl=tl, v_=v_, ptv=ptv: e.matmul(psb[po:po + 64, 8 * hp:8 * hp + 8], lhsT=v_[:, tl, h * 64:(h + 1) * 64],
                                                                                                           rhs=ptv[:, h, 8 * tl:8 * tl + 8], start=(tl == 0), stop=(tl == 12), skip_group_check=True),
                                 reads=[vk, ptk], writes=[pbk])
                        P.op("pe", lambda e, psb=psb, h=h, hp=hp, po=po: e.matmul(psb[po:po + 64, 16 + 8 * hp:24 + 8 * hp], lhsT=self.ones_f[:, 0:64], rhs=psum_[:, 8 * h:8 * h + 8],
                                                                                  start=True, stop=True, skip_group_check=True),
                             reads=["ones_f", KC("ptsum")], writes=[pbk])
                    P.op("dve", lambda e, psb=psb, b=b: e.tensor_tensor(out=acc[:, :, 8 * b:8 * b + 8], in0=psb[:, 0:32].rearrange("p (j i) -> p j i", j=4), in1=acc[:, :, 8 * b:8 * b + 8], op=ALU.add),
                         reads=[pbk, K("acc")], writes=[K("acc")])

            for b in range(16):
                s_, sk = stg[0], KC("cst0")
                kt_, kk = kTc[b % 2], KC("kTc%d" % (b % 2))
                v_, vk = vc[0], KC("vc0")
                P.op("sp", lambda e, s_=s_, b=b: e.dma_start(out=s_[:, 0, :], in_=dr["c128"][b]), writes=[sk], is_dma=True, multi=True)
                P.op("sp", lambda e, s_=s_, b=b: e.dma_start(out=s_[:, 1:5, :], in_=dr["c512"][b].rearrange("(a p) c -> p a c", p=128)), writes=[sk], is_dma=True, multi=True)
                cv = dr["c2048"]
                src = bass.AP(cv.tensor, cv.offset + b * 2048 * 512, [[16 * 512, 128], [512, 8], [1, 512]])
                P.op("sp", lambda e, s_=s_, src=src: e.dma_start(out=s_[:, 5:13, :], in_=src), writes=[sk], is_dma=True, multi=True)
                for j0 in range(0, 26, 4):
                    n = min(4, 26 - j0)
                    ps, pk = self.psum()
                    for j in range(n):
                        tl, hp = (j0 + j) // 2, (j0 + j) % 2
                        P.op("pe", lambda e, ps=ps, j=j, tl=tl, hp=hp, s_=s_: e.transpose(out=ps[:, j * 128:(j + 1) * 128], in_=s_[:, tl, hp * 128:(hp + 1) * 128], identity=self.ident[:]),
                             reads=[sk, "ident"], writes=[pk])
                    eng = "act" if (j0 // 4) % 2 == 0 else "dve"
                    if eng == "act":
                        P.op("act", lambda e, ps=ps, j0=j0, n=n, kt_=kt_: e.copy(out=kt_[:, j0:j0 + n, :], in_=ps[:, 0:n * 128].rearrange("p (j k) -> p j k", k=128)), reads=[pk], writes=[kk])
                    else:
                        P.op("dve", lambda e, ps=ps, j0=j0, n=n, kt_=kt_: e.tensor_copy(out=kt_[:, j0:j0 + n, :], in_=ps[:, 0:n * 128].rearrange("p (j k) -> p j k", k=128)), reads=[pk], writes=[kk])
                i = self.pti
                self.pti += 1
                es, ek = esb[i % 2], K("esb%d" % (i % 2))
                pt, ptk = pT[i % 2], K("pT%d" % (i % 2))
                ptv = pt[:].rearrange("p h q -> p (h q)")[:, 0:416].rearrange("p (h q) -> p h q", h=4)
                pss = [self.psum(), self.psum()]
                for h in range(4):
                    hp, po = h // 2, (h % 2) * 64
                    ps, pk = pss[h % 2]
                    for tl in range(13):
                        gg = 0 if tl == 0 else (1 if tl < 5 else 2)
                        P.op("pe", lambda e, ps=ps, h=h, hp=hp, po=po, tl=tl, gg=gg, kt_=kt_, b=b: e.matmul(ps[:, hp * 104 + 8 * tl:hp * 104 + 8 * tl + 8], lhsT=kt_[po:po + 64, tl * 2 + hp, :],
                                                                                                         rhs=qT[po:po + 64, gg * 2 + hp, 8 * b:8 * b + 8], start=True, stop=True, skip_group_check=True),
                             reads=[kk, K("qT", gg * 2 + hp)], writes=[pk])
                if pend_c[0] is not None:
                    pend_c[0]()
                    pend_c[0] = None
                P.op("pool", lambda e, s_=s_, v_=v_: e.tensor_copy(out=v_[:], in_=s_[:, :, 256:512]), reads=[sk], writes=[vk])
                for par in range(2):
                    ps, pk = pss[par]
                    esv = es[:, 0:416].rearrange("p (hp par q) -> p hp par q", hp=2, par=2)[:, :, par, :]
                    P.op("act", lambda e, ps=ps, esv=esv: e.activation(out=esv, in_=ps[:, 0:208].rearrange("p (hp q) -> p hp q", hp=2), func=AF.Exp), reads=[pk], writes=[ek])
                P.op("dve", lambda e, es=es, pt=pt: e.tensor_tensor(out=pt[:].rearrange("p h q -> p (h q)")[:, 0:416], in0=es[:, 0:416], in1=self.tabc[:, 0:416], op=ALU.mult),
                     reads=[ek, "tabc"], writes=[ptk])
                P.op("dve", lambda e, ptv=ptv: e.tensor_reduce(out=psum_[:].rearrange("p (h i) -> p h i", h=4), in_=ptv.rearrange("p h (t i) -> p h i t", i=8), axis=AX.X, op=ALU.add),
                     reads=[ptk], writes=[KC("ptsum")])
                pend_c[0] = lambda b=b, v_=v_, vk=vk, ptv=ptv, ptk=ptk: pv_c(b, v_, vk, ptv, ptk)
            pend_c[0]()
            pend_c[0] = None


    def load_odd_params(self):
        P, dr = self.P, self.dram
        P.op("sp", lambda e: e.dma_start(out=self.Jrev[:], in_=dr["c_jrev"]), writes=["Jrev"], is_dma=True)
        for r, nm in enumerate(["sgu_ln_g", "sgu_ln_b"]):
            v = dr[nm]
            src = bass.AP(v.tensor, v.offset, [[0, 128], [1, 256]])
            P.op("sp", lambda e, src=src, r=r: e.dma_start(out=self.lngb[:, r, :], in_=src), writes=["lngb"], is_dma=True, multi=True)
        v = dr["sgu_b"]
        for gq in range(4):
            c, po = gq // 2, (gq % 2) * 64
            src = bass.AP(v.tensor, v.offset + gq * 128, [[0, 64], [1, 128]])
            P.op("sp", lambda e, src=src, c=c, po=po: e.dma_start(out=self.sgub[po:po + 64, c, :], in_=src), writes=["sgub"], is_dma=True, multi=True)
            src = bass.AP(v.tensor, v.offset + gq * 128, [[0, 64], [0, 16], [1, 8]])
            P.op("sp", lambda e, src=src, c=c, po=po: e.dma_start(out=self.sgubs[po:po + 64, c, :].rearrange("p (b t) -> p b t", t=8), in_=src), writes=["sgub"], is_dma=True, multi=True)
        with self.phase("sguw") as ph:
            K = ph.k
            wst = ph.buf("wst", [128, 4, 128])
            wss = ph.buf("wss", [128, 4, 128])
            tf = ph.buf("tf", [128, 4, 128])
            P.op("sp", lambda e: e.dma_start(out=wst[:], in_=dr["sgu_w"][0].rearrange("g t s -> t g s")), writes=[K("wst")], is_dma=True)
            ps, pk = self.psum()
            for gq in range(4):
                P.op("pe", lambda e, ps=ps, gq=gq: e.transpose(out=ps[:, gq * 128:(gq + 1) * 128], in_=wst[:, gq, :], identity=self.ident[:]), reads=[K("wst"), "ident"], writes=[pk])
            cm = self.causal_st[:]
            cmb = bass.AP(cm.tensor, cm.offset, [[cm.ap[0][0], 128], [0, 4], [1, 128]])
            P.op("dve", lambda e, ps=ps, cmb=cmb: e.tensor_tensor(out=self.wsT[:], in0=ps[:, 0:512].rearrange("p (g t) -> p g t", g=4), in1=cmb, op=ALU.mult), reads=[pk, "masks"], writes=["wsT"])
            P.op("pool", lambda e: e.memset(wss[:], 0.0), writes=[K("wss")])
            v = dr["sgu_w"]
            for b in range(16):
                for gq in range(4):
                    src = bass.AP(v.tensor, v.offset + gq * 128 * 128, [[1, 8], [128, 8]])
                    P.op("sp", lambda e, src=src, b=b, gq=gq: e.dma_start(out=wss[8 * b:8 * b + 8, gq, 8 * b:8 * b + 8], in_=src, allow_slow_non_contiguous=True), writes=[K("wss")], is_dma=True, multi=True)
            cm = self.causal_s8[:]
            cmb2 = bass.AP(cm.tensor, cm.offset, [[cm.ap[0][0], 128], [0, 4], [1, 128]])
            P.op("dve", lambda e, cmb2=cmb2: e.tensor_tensor(out=self.wsTs[:], in0=wss[:], in1=cmb2, op=ALU.mult), reads=[K("wss"), "masks"], writes=["wsT"])

    def ffn(self, kind, g, T, l):
        P, dr = self.P, self.dram
        self.prenorm(4 + l, T)
        with self.phase("ffn%d_%s%d" % (l, kind, g)) as ph:
            act = ph.buf("act", [128, 22, T], BF16)
            if kind == "P":
                W = 2 + T
                up = [ph.buf("up%d" % i, [128, 2, W]) for i in range(2)]
            else:
                up = [ph.buf("up%d" % i, [128, 2, 16, 10]) for i in range(2)]
                hs = ph.buf("hs", [128, 44, 32])
                ho = ph.buf("ho", [128, 44, 16, 2])
                stg = ph.buf("stg", [32, 5632])
                P.op("sp", lambda e: e.dma_start(out=stg[:], in_=dr["sffn"][l].rearrange("i r c -> (i r) c")), writes=[ph.k("stg")], is_dma=True)
                for c0 in range(0, 44, 16):
                    n = min(16, 44 - c0)
                    ps, pk = self.psum()
                    for i in range(n):
                        c = c0 + i
                        P.op("pe", lambda e, ps=ps, i=i, c=c: e.transpose(out=ps[:, i * 32:(i + 1) * 32], in_=stg[0:32, c * 128:(c + 1) * 128],
                                                                            identity=self.ident[0:32, 0:32]),
                             reads=[ph.k("stg"), "ident"], writes=[pk])
                    P.op("dve", lambda e, ps=ps, c0=c0, n=n: e.tensor_copy(out=hs[:, c0:c0 + n, :], in_=ps[:, 0:n * 32].rearrange("p (c r) -> p c r", r=32)),
                         reads=[pk], writes=[ph.k("hs")])
            c0ts = [ph.buf("c0t%d" % i, [128, 2, T]) for i in range(2)]
            gels = [ph.buf("gel%d" % i, [128, T]) for i in range(2)]
            pending_tail = [None]
            for b in range(11):
                w, wk = self.next_w("U%d_%d" % (l, b))
                for jj in range(2):
                    j = 2 * b + jj
                    u = up[j % 2]
                    uk = ph.k("up%d" % (j % 2))
                    c0t, gel = c0ts[j % 2], gels[j % 2]
                    ck, gk = "c0t%d" % (j % 2), "gel%d" % (j % 2)
                    pss = []
                    for gv in range(2):
                        ps, pk = self.psum()
                        pss.append((ps, pk))
                        for k in range(8):
                            P.op("pe", lambda e, ps=ps, k=k, gv=gv, jj=jj, w=w: e.matmul(ps[:, 0:T], lhsT=w[:, k, gv * 256 + jj * 128: gv * 256 + (jj + 1) * 128],
                                                                                       rhs=self.hT[:, k, 0:T], start=(k == 0), stop=(k == 7)),
                                 reads=[wk] + [("hT", k)], writes=[pk])
                    uks = [ph.k("up%d" % (j % 2), 0), ph.k("up%d" % (j % 2), 1)]
                    if kind == "P":
                        P.op("pool", lambda e, u=u, j=j: e.tensor_copy(out=u[:, :, 0:2], in_=self.hal[l][:, j, :, :]), reads=["hal%d" % l], writes=uks)
                    else:
                        for gv in range(2):
                            P.op("pool", lambda e, u=u, j=j, gv=gv: e.tensor_copy(out=u[:, gv, :, 0:2],
                                                                                in_=hs[:, gv * 22 + j, :].rearrange("p (i r) -> p i r", r=2)),
                                 reads=[ph.k("hs")], writes=[uks[gv]])
                    prow = 4 * l
                    views = []
                    for gv in range(2):
                        ps, pk = pss[gv]
                        if kind == "P":
                            views.append((u[:, gv, 2:2 + T], ps[:, 0:T], c0t[:, gv, :], u[:, gv, 1:1 + T], u[:, gv, 0:T]))
                        else:
                            views.append((u[:, gv, :, 2:10], ps[:, 0:T].rearrange("p (i t) -> p i t", t=8), c0t[:, gv, :].rearrange("p (i t) -> p i t", t=8),
                                          u[:, gv, :, 1:9], u[:, gv, :, 0:8]))
                    for gv in range(2):
                        raw_dst, psv, c_dst, sh1, sh2 = views[gv]
                        pk = pss[gv][1]
                        ch = gv * 22 + j
                        P.op("act", lambda e, raw_dst=raw_dst, psv=psv: e.copy(out=raw_dst, in_=psv), reads=[pk], writes=[uks[gv]])
                        P.op("act", lambda e, c_dst=c_dst, psv=psv, ch=ch: e.activation(out=c_dst, in_=psv, func=AF.Identity,
                                                                                       bias=self.ffnp[:, ch, prow + 3:prow + 4], scale=self.ffnp[:, ch, prow + 2:prow + 3]),
                             reads=[pk, "ffnp"], writes=[ph.k(ck, gv)])
                    for tap in (1, 0):
                        for gv in range(2):
                            raw_dst, psv, c_dst, sh1, sh2 = views[gv]
                            ch = gv * 22 + j
                            sh = sh1 if tap == 1 else sh2
                            P.op("dve", lambda e, c_dst=c_dst, sh=sh, ch=ch, tap=tap: e.scalar_tensor_tensor(out=c_dst, in0=sh, scalar=self.ffnp[:, ch, prow + tap:prow + tap + 1],
                                                                                                           in1=c_dst, op0=ALU.mult, op1=ALU.add),
                                 reads=[uks[gv], "ffnp", ph.k(ck, gv)], writes=[ph.k(ck, gv)])
                    if kind == "P":
                        P.op("pool", lambda e, u=u, j=j: e.tensor_copy(out=self.hal[l][:, j, :, :], in_=u[:, :, T:T + 2]), reads=uks, writes=["hal%d" % l])
                    else:
                        for gv in range(2):
                            P.op("pool", lambda e, u=u, j=j, gv=gv: e.tensor_copy(out=ho[:, gv * 22 + j, :, :], in_=u[:, gv, :, 8:10]),
                                 reads=[uks[gv]], writes=[ph.k("ho")])
                    def tail(j=j, gel=gel, c0t=c0t, ck=ck, gk=gk):
                        P.op("act", lambda e: e.activation(out=gel[:], in_=c0t[:, 0, :], func=AF.Gelu_apprx_tanh), reads=[ph.k(ck, 0)], writes=[ph.k(gk)])
                        P.op("dve", lambda e: e.tensor_tensor(out=act[:, j, :], in0=gel[:], in1=c0t[:, 1, :], op=ALU.mult),
                             reads=[ph.k(gk), ph.k(ck, 1)], writes=[ph.k("act", j)])
                    if pending_tail[0] is not None:
                        pending_tail[0]()
                    pending_tail[0] = tail
            pending_tail[0]()
            pending_tail[0] = None
            for o in range(8):
                w, wk = self.next_w("D%d_%d" % (l, o))
                ps, pk = self.psum()
                for j in range(22):
                    P.op("pe", lambda e, ps=ps, j=j, w=w: e.matmul(ps[:, 0:T], lhsT=w[:, j, :], rhs=act[:, j, :], start=(j == 0), stop=(j == 21)),
                         reads=[wk, ph.k("act", j)], writes=[pk])
                P.op("act", lambda e, ps=ps, o=o: e.copy(out=self.yT[:, o, 0:T], in_=ps[:, 0:T]), reads=[pk], writes=["yT"])
            if kind == "P" and g == 3:
                for r in range(2):
                    for gv in range(2):
                        dst = dr["p_ffn"][l, r, gv * DFF:(gv + 1) * DFF].rearrange("(j p) -> p j", p=128)
                        o_ = P.op("sp", lambda e, dst=dst, r=r, gv=gv: e.dma_start(out=dst, in_=self.hal[l][:, :, gv, r], allow_slow_non_contiguous=True),
                                  reads=["hal%d" % l], is_dma=True)
                        P.final_waits.append(o_.idx)
            if kind == "S":
                ost = ph.buf("ost", [32, 5632])
                for c0 in range(0, 44, 4):
                    ps, pk = self.psum()
                    for i in range(4):
                        c = c0 + i
                        P.op("pe", lambda e, ps=ps, i=i, c=c: e.transpose(out=ps[0:32, i * 128:(i + 1) * 128], in_=ho[:, c, :, :].rearrange("p i r -> p (i r)"),
                                                                            identity=self.ident[:]),
                             reads=[ph.k("ho"), "ident"], writes=[pk])
                    P.op("dve", lambda e, ps=ps, c0=c0: e.tensor_copy(out=ost[:, c0 * 128:(c0 + 4) * 128], in_=ps[0:32, 0:512]), reads=[pk], writes=[ph.k("ost")])
                o_ = P.op("sp", lambda e: e.dma_start(out=dr["s_ffn"][l].rearrange("i r c -> (i r) c"), in_=ost[:]), reads=[ph.k("ost")], is_dma=True)
                P.final_waits.append(o_.idx)
        self.postnorm_residual(6 + l, T)


class Phase:
    def __init__(self, kb, tag):
        self.kb, self.tag = kb, tag
        self.stack = contextlib.ExitStack()
        self.names = []
        if not hasattr(kb, "pending_alias"):
            kb.pending_alias = []
        self.inherit_obj = kb.pending_alias

    def k(self, name, *sub):
        n = self.tag + "." + name
        return (n,) + tuple(sub) if sub else n

    def buf(self, name, shape, dt=F32):
        n = self.tag + "." + name
        t = self.stack.enter_context(self.kb.nc.sbuf_tensor(n.replace(".", "_"), list(shape), dt))
        self.names.append(n)
        self.kb.P.alias[n] = list(self.kb.pending_alias)
        return t

    def close(self):
        dead = set(self.kb.P.retire(self.names))
        cur = self.kb.pending_alias
        if cur is not self.inherit_obj:
            dead |= set(cur)
        if not dead:
            dead = set(cur)
        self.kb.pending_alias = sorted(dead)
        self.stack.close()


_CACHE = {}


def _build(stages):
    if stages not in _CACHE:
        kb = KB(stages)
        nc = kb.build()
        _CACHE[stages] = (kb, nc)
    return _CACHE[stages]


def kernel(**inputs):
    stages = "all"
    kb, nc = _build(stages)
    consts = kb.consts
    f32 = lambda a: np.ascontiguousarray(np.asarray(a, dtype=np.float32))
    shared = {}
    for n in WEIGHT_SHAPES:
        shared[n] = f32(inputs[n])
    for n in SMALL_SHAPES:
        shared[n] = f32(inputs[n])
    for n, a in consts.items():
        shared["c_" + n] = a
    in_maps = []
    for c in range(NCORES):
        m = dict(shared)
        b0, b1 = 16 * c, 16 * c + 16
        m["xp"] = f32(inputs["x_prompt"][c])
        m["xs"] = f32(inputs["x_sample"][b0:b1]).reshape(128, 1024)
        m["sgla"] = f32(inputs["state_gla"][0, b0:b1])
        m["scb"] = f32(inputs["state_conv_b"][0, b0:b1])
        m["c128"] = f32(inputs["cache_c_w128"][0, b0:b1]).reshape(16, 128, 512)
        m["c512"] = f32(inputs["cache_c_w512"][0, b0:b1]).reshape(16, 512, 512)
        m["c2048"] = f32(inputs["cache_c_w2048"][0, b0:b1]).reshape(16, 2048, 512)
        m["sffn"] = f32(inputs["state_ffn_conv"][:, b0:b1])
        in_maps.append(m)
    res = run_bass_kernel_spmd(nc, in_maps, core_ids=list(range(NCORES)))
    R = res.results
    cat = lambda n, ax=0: np.concatenate([np.asarray(R[c][n]) for c in range(NCORES)], axis=ax)
    stk = lambda n: np.stack([np.asarray(R[c][n]) for c in range(NCORES)], axis=0)
    y_prompt = stk("y_p")
    y_sample = cat("y_s").reshape(128, 8, 1024)
    p_gla = stk("p_gla")[None]
    p_conv_b = stk("p_conv_b")[None]
    p_kv128 = stk("p_kv128").reshape(1, 8, 128, 2, 4, 64)
    p_kv512 = stk("p_kv512").reshape(1, 8, 512, 2, 4, 64)
    p_kv2048 = stk("p_kv2048").reshape(1, 8, 2048, 2, 4, 64)
    p_ffn = np.stack([np.asarray(R[c]["p_ffn"]) for c in range(NCORES)], axis=1)
    s_gla = cat("s_gla")[None]
    s_conv_b = cat("s_conv_b")[None]
    s_kv128 = cat("s_kv128").reshape(1, 128, 128, 2, 4, 64)
    s_kv512 = cat("s_kv512").reshape(1, 128, 512, 2, 4, 64)
    s_kv2048 = cat("s_kv2048").reshape(1, 128, 2048, 2, 4, 64)
    s_sgu_v = cat("s_sgu_v").reshape(1, 128, 8, 256)
    s_ffn = cat("s_ffn", ax=1)
    outs = (y_prompt, y_sample, p_gla, p_conv_b, p_kv128, p_kv512, p_kv2048, p_ffn,
            s_gla, s_conv_b, s_kv128, s_kv512, s_kv2048, s_sgu_v, s_ffn)
    return tuple(np.ascontiguousarray(o, dtype=np.float32) for o in outs)
```

```python
import contextlib
import math
import os
DBG = set(os.environ.get('KDEBUG', '').split(','))
ODDSTOP = int(os.environ.get('ODDSTOP', '99'))
import numpy as np
import concourse.bass as bass
import concourse.mybir as mybir
from concourse.bass_utils import run_bass_kernel_spmd
from concourse.alu_op_type import AluOpType as ALU

F32 = mybir.dt.float32
BF16 = mybir.dt.bfloat16
AF = mybir.ActivationFunctionType
AX = mybir.AxisListType

NCORES = 8
D = 1024
DFF = 2816
EPS = 1e-6
ENGS = ("pe", "act", "dve", "pool", "sp")
COMPUTE = ("pe", "act", "dve", "pool")


class Op:
    __slots__ = ("eng", "fn", "reads", "writes", "is_dma", "idx", "deps", "has_dependents", "sem", "semval")

    def __init__(self, eng, fn, reads, writes, is_dma):
        self.eng, self.fn, self.reads, self.writes, self.is_dma = eng, fn, reads, writes, is_dma
        self.deps, self.has_dependents, self.sem, self.semval = [], False, None, None


def _bname(k):
    return k if isinstance(k, str) else k[0]


class Prog:
    def __init__(self, nc, n_dma_sems=48, same_engine_sync=True):
        self.nc = nc
        self.ops = []
        self.last_writer = {}
        self.readers = {}
        self.n_dma_sems = n_dma_sems
        self.same_engine_sync = same_engine_sync
        self.final_waits = []
        self.touch = {}
        self.alias = {}
        self.mw = {}

    def retire(self, names):
        deps = set()
        for n in names:
            t = self.touch.get(n)
            if t:
                deps.update(t["c"].values())
                deps.update(t["d"])
        return sorted(deps)

    def op(self, eng, fn, reads=(), writes=(), is_dma=False, multi=False):
        o = Op(eng, fn, tuple(reads), tuple(writes), is_dma)
        o.idx = len(self.ops)
        deps = set()
        for k in o.reads:
            w = self.last_writer.get(k)
            if w is not None:
                deps.add(w)
            deps.update(self.mw.get(k, ()))
        for k in o.writes:
            w = self.last_writer.get(k)
            if w is not None:
                deps.add(w)
            deps.update(self.readers.get(k, {}).values())
            if not multi:
                deps.update(self.mw.get(k, ()))
        for k in o.reads + o.writes:
            n = _bname(k)
            a = self.alias.get(n)
            if a:
                deps.update(a)
            t = self.touch.setdefault(n, {"c": {}, "d": []})
            if is_dma:
                t["d"].append(o.idx)
            else:
                t["c"][eng] = o.idx
        for k in o.reads:
            r = self.readers.setdefault(k, {})
            if is_dma:
                r[("dma", o.idx)] = o.idx
            else:
                r[eng] = o.idx
        for k in o.writes:
            if multi:
                self.mw.setdefault(k, []).append(o.idx)
            else:
                self.last_writer[k] = o.idx
                self.mw[k] = []
                self.readers[k] = {}
        deps.discard(o.idx)
        o.deps = sorted(deps)
        self.ops.append(o)
        return o

    def emit(self, stack):
        nc, ops = self.nc, self.ops
        for o in ops:
            nd = []
            for d in o.deps:
                p = ops[d]
                if (not p.is_dma) and (not o.is_dma) and p.eng == o.eng:
                    if o.eng == "pe" or not self.same_engine_sync:
                        continue
                nd.append(d)
            o.deps = nd
        dma_last = [None] * self.n_dma_sems
        dma_cnt = [0] * self.n_dma_sems
        NSW = 16
        rrs = {"pool": 0, "sp": 0}
        for o in ops:
            if o.is_dma:
                if o.eng == "pool":
                    s = rrs["pool"]
                    rrs["pool"] = (s + 1) % NSW
                else:
                    s = NSW + rrs["sp"]
                    rrs["sp"] = (rrs["sp"] + 1) % (self.n_dma_sems - NSW)
                if dma_last[s] is not None and dma_last[s] not in o.deps:
                    o.deps.append(dma_last[s])
                dma_cnt[s] += 16
                o.sem, o.semval = ("dma", s), dma_cnt[s]
                dma_last[s] = o.idx
        for o in ops:
            for d in o.deps:
                ops[d].has_dependents = True
        for idx in self.final_waits:
            ops[idx].has_dependents = True
        cnt = {e: 0 for e in ENGS}
        pending = {e: [] for e in ENGS}
        for o in ops:
            if o.is_dma:
                continue
            pending[o.eng].append(o)
            if o.has_dependents:
                cnt[o.eng] += 1
                for p in pending[o.eng]:
                    p.sem, p.semval = ("eng", o.eng), cnt[o.eng]
                pending[o.eng] = []
        self.stats = dict(cnt)
        esem = {e: stack.enter_context(nc.semaphore("s_" + e)) for e in COMPUTE}
        dsem = [stack.enter_context(nc.semaphore("d_%d" % i)) for i in range(self.n_dma_sems)]
        block = stack.enter_context(nc.Block())

        def semh(key):
            return esem[key[1]] if key[0] == "eng" else dsem[key[1]]

        per_eng = {e: [o for o in ops if o.eng == e] for e in ENGS}
        finals = [ops[i] for i in self.final_waits]

        def run(ename, eobj):
            seen = {}
            for o in per_eng[ename]:
                need = {}
                for d in o.deps:
                    p = ops[d]
                    if p.sem is None or seen.get(p.sem, 0) >= p.semval:
                        continue
                    if need.get(p.sem, 0) < p.semval:
                        need[p.sem] = p.semval
                for k, v in need.items():
                    eobj.wait_ge(semh(k), v)
                    seen[k] = v
                ins = o.fn(eobj)
                if o.is_dma:
                    ins.then_inc(semh(o.sem), 16)
                elif o.has_dependents:
                    ins.then_inc(semh(o.sem), 1)
            if ename == "sp":
                for p in finals:
                    if seen.get(p.sem, 0) < p.semval:
                        eobj.wait_ge(semh(p.sem), p.semval)
                        seen[p.sem] = p.semval

        @block.tensor
        def _(e):
            run("pe", e)

        @block.scalar
        def _(e):
            run("act", e)

        @block.vector
        def _(e):
            run("dve", e)

        @block.gpsimd
        def _(e):
            run("pool", e)

        @block.sync
        def _(e):
            run("sp", e)


def weight_blocks():
    B = {}
    ev = [("E0", 0, 512), ("E1", 512, 512), ("E2", 1024, 16), ("E3", 1040, 512), ("E4", 1552, 512), ("E5", 2064, 512)]
    for n, c0, nc_ in ev:
        B[n] = ("w_in_even", 0, 8, [(0, c0, nc_)])
    B["EO0"] = ("w_out_even", 0, 8, [(0, 0, 512)])
    B["EO1"] = ("w_out_even", 0, 8, [(0, 512, 512)])
    od = [("O0", 0, 512), ("O1", 512, 256), ("O2", 768, 512), ("O3", 1280, 256), ("O4", 1536, 512), ("O5", 2048, 256),
          ("O6", 2304, 512)]
    for n, c0, nc_ in od:
        B[n] = ("w_in_odd", 0, 8, [(0, c0, nc_)])
    B["OO0"] = ("w_out_odd", 0, 4, [(0, 0, 512)])
    B["OO1"] = ("w_out_odd", 0, 4, [(0, 512, 512)])
    for l in range(2):
        for b in range(11):
            B["U%d_%d" % (l, b)] = ("w_up", l, 8, [(0, 256 * b, 256), (256, DFF + 256 * b, 256)])
        for o in range(8):
            B["D%d_%d" % (l, o)] = ("w_down", l, 22, [(0, 128 * o, 128)])
    return B


def group_wplan():
    p = ["E0", "E2", "E1", "E3", "E4", "E5", "EO0", "EO1"]
    p += ["U0_%d" % b for b in range(11)] + ["D0_%d" % o for o in range(8)]
    p += ["O0", "O1", "O2", "O4", "O3", "O5", "O6", "OO0", "OO1"]
    p += ["U1_%d" % b for b in range(11)] + ["D1_%d" % o for o in range(8)]
    return p


WEIGHT_SHAPES = {
    "w_in_even": [1, 1024, 2576], "w_out_even": [1, 1024, 1024], "w_in_odd": [1, 1024, 2816],
    "w_out_odd": [1, 512, 1024], "w_up": [2, 1024, 5632], "w_down": [2, 2816, 1024],
}
SMALL_SHAPES = {
    "norm_pre_mix": [2, 1024], "norm_post_mix": [2, 1024], "norm_pre_ffn": [2, 1024], "norm_post_ffn": [2, 1024],
    "w_gate2": [1, 16, 256], "b_gate": [1, 256], "gla_norm": [1, 128], "conv_b_w": [1, 31, 512], "conv_b_b": [1, 512],
    "ln_b_g": [1, 512], "ln_b_b": [1, 512], "rel_bias": [32, 12], "sgu_ln_g": [1, 256], "sgu_ln_b": [1, 256],
    "sgu_w": [1, 4, 128, 128], "sgu_b": [1, 4, 128], "ffn_dw_w": [2, 3, 5632], "ffn_dw_b": [2, 5632],
}
CORE_IN = {
    "xp": [2048, 1024], "xs": [128, 1024], "sgla": [16, 4, 64, 128], "scb": [16, 30, 512],
    "c128": [16, 128, 512], "c512": [16, 512, 512], "c2048": [16, 2048, 512], "sffn": [2, 16, 2, 5632],
}
CORE_OUT = {
    "y_p": [2048, 1024], "y_s": [128, 1024], "p_gla": [4, 64, 128], "p_conv_b": [30, 512],
    "p_kv128": [128, 512], "p_kv512": [512, 512], "p_kv2048": [2048, 512], "p_ffn": [2, 2, 5632],
    "s_gla": [16, 4, 64, 128], "s_conv_b": [16, 30, 512], "s_kv128": [16, 128, 512], "s_kv512": [16, 512, 512],
    "s_kv2048": [16, 2048, 512], "s_sgu_v": [128, 256], "s_ffn": [2, 16, 2, 5632],
}


def host_consts():
    c = {}
    c["ident"] = np.eye(128, dtype=np.float32)
    m = np.ones((128, 512), np.float32)
    m[:, 0::128] = 0.0
    c["scanmask_p"] = m
    m = np.ones((128, 128), np.float32)
    m[:, 0::8] = 0.0
    c["scanmask_s"] = m
    s = np.arange(128)[:, None]
    t = np.arange(128)[None, :]
    c["causal_st"] = (s <= t).astype(np.float32)
    c["causal_s8"] = ((s <= t) & (s // 8 == t // 8)).astype(np.float32)
    dil = (1, 4, 16)
    oh = np.zeros((32, 3, 129), np.float32)
    for g in range(3):
        dist = (dil[g] * np.arange(129)).astype(np.int64)
        d32 = np.maximum(dist, 1).astype(np.float32)
        large = 16 + (np.log(d32 / np.float32(16)) / np.float32(math.log(2048 / 16)) * np.float32(16)).astype(np.int32)
        large = np.minimum(large, 31)
        bk = np.where(dist < 16, dist, large)
        oh[bk, g, np.arange(129)] = 1.0
    c["onehot"] = oh
    c["jrev"] = np.ascontiguousarray(np.eye(128, dtype=np.float32)[::-1])
    c["seqmask"] = (np.arange(128)[:, None] // 8 == np.arange(16)[None, :]).astype(np.float32)
    return c


class KB:
    def __init__(self, stages):
        self.stages = stages
        self.nc = bass.Bass("TRN2", target_bir_lowering=False)
        self.st = contextlib.ExitStack()
        self.P = Prog(self.nc)
        self.dram = {}
        self.psi = 0
        self.ps_avail = list(range(8))
        self.wptr = 0
        self.wissued = 0

    def din(self, name, shape, dt=F32):
        self.dram[name] = self.nc.dram_tensor(name, list(shape), dt, kind="ExternalInput").ap()
        return self.dram[name]

    def dout(self, name, shape, dt=F32):
        self.dram[name] = self.nc.dram_tensor(name, list(shape), dt, kind="ExternalOutput").ap()
        return self.dram[name]

    def sb(self, name, shape, dt=F32, stack=None):
        return (stack or self.st).enter_context(self.nc.sbuf_tensor(name, list(shape), dt))

    def psum(self):
        av = self.ps_avail
        i = av[self.psi % len(av)]
        self.psi += 1
        return self.ps[i], ("ps", i)

    def next_w(self, name):
        P = self.P
        assert self.wplan[self.wptr] == name, (self.wplan[self.wptr], name)
        i = self.wptr
        self.wptr += 1
        while self.wissued < min(len(self.wplan), i + self.NSLOT - 1):
            j = self.wissued
            self.precast_upto(j + 6)
            bn = self.wplan[j]
            bi = self.wbidx[bn]
            _, _, kc, parts = self.wblocks[bn]
            ncols = sum(p[2] for p in parts)
            slot = j % self.NSLOT
            dst = self.wslots[slot][:, 0:kc * ncols]
            src = self.wsc[bi, :, 0:kc * ncols]
            P.op("sp", lambda e, dst=dst, src=src: e.dma_start(out=dst, in_=src),
                 reads=[("wsc", bi)], writes=[("wslot", slot)], is_dma=True)
            self.wissued += 1
        slot = i % self.NSLOT
        _, _, kc, parts = self.wblocks[name]
        ncols = sum(p[2] for p in parts)
        view = self.wslots[slot][:, 0:kc * ncols].rearrange("p (k c) -> p k c", k=kc)
        return view, ("wslot", slot)

    def precast_upto(self, jmax):
        P, dr = self.P, self.dram
        nb = len(self.wblocks)
        while self.pc_ptr < min(nb, jmax + 1):
            bn = self.wplan[self.pc_ptr]
            self.pc_ptr += 1
            tn, l, kc, parts = self.wblocks[bn]
            bi = self.wbidx[bn]
            ncols = sum(p[2] for p in parts)
            for (d0, s0, n_) in parts:
                src = dr[tn][l, :, s0:s0 + n_].rearrange("(k p) c -> p k c", p=128)
                dst = self.wsc[bi, :, 0:kc * ncols].rearrange("p (k c) -> p k c", k=kc)[:, :, d0:d0 + n_]
                P.op("pool", lambda e, dst=dst, src=src: e.dma_start(out=dst, in_=src),
                     writes=[("wsc", bi)], is_dma=True, multi=True)

    def build(self):
        nc, P, st = self.nc, self.P, self.st
        for n, s in CORE_IN.items():
            self.din(n, s)
        for n, s in WEIGHT_SHAPES.items():
            self.din(n, s)
        for n, s in SMALL_SHAPES.items():
            self.din(n, s)
        self.consts = host_consts()
        for n, a in self.consts.items():
            self.din("c_" + n, a.shape)
        for n, s in CORE_OUT.items():
            self.dout(n, s)
        dr = self.dram
        self.wblocks = weight_blocks()
        self.wbidx = {n: i for i, n in enumerate(self.wblocks)}
        self.wsc = nc.dram_tensor("wsc", [len(self.wblocks), 128, 4096], BF16, kind="Internal").ap()
        self.NSLOT = 5
        gp = group_wplan()
        self.groups = [("P", g) for g in range(4)] + [("S", 0)]
        self.wplan = []
        for _ in self.groups:
            self.wplan += gp

        self.ps = [st.enter_context(nc.psum_tensor("ps%d" % i, [128, 512], F32)) for i in range(8)]
        self.wslots = [self.sb("wslot%d" % i, [128, 4096], BF16) for i in range(self.NSLOT)]
        self.ident = self.sb("ident", [128, 128])
        self.ones_bf = self.sb("ones_bf", [128, 128], BF16)
        self.gains = self.sb("gains", [128, 8, 8])
        self.ffnp = self.sb("ffnp", [128, 44, 8])
        self.hal = [self.sb("hal%d" % l, [128, 22, 2, 2]) for l in range(2)]
        self.xT = self.sb("xT", [128, 8, 512])
        self.hT = self.sb("hT", [128, 8, 512], BF16)
        self.rstd = self.sb("rstd", [128, 512])
        self.yT = self.sb("yT", [128, 8, 512])
        self.eps_t = self.sb("eps_t", [128, 1])
        self.alloc_persistent()

        P.op("sp", lambda e: e.dma_start(out=self.ident[:], in_=dr["c_ident"]), writes=["ident"], is_dma=True)
        P.op("pool", lambda e: e.memset(self.ones_bf[:], 1.0), writes=["ones_bf"])
        P.op("pool", lambda e: e.memset(self.eps_t[:], EPS), writes=["eps"])
        for l in range(2):
            P.op("pool", lambda e, l=l: e.memset(self.hal[l][:], 0.0), writes=["hal%d" % l])
        self.pc_ptr = 0
        self.load_params()
        if "noodd" not in DBG:
            self.load_odd_params()
        if "notab" not in DBG:
            self.build_bias_tables()
        def cache_copies():
            for nm, cn, W_ in (("s_kv128", "c128", 128), ("s_kv512", "c512", 512), ("s_kv2048", "c2048", 2048)):
                if "nocopy" in DBG:
                    continue
                for i in range(16):
                    o_ = P.op("sp", lambda e, nm=nm, cn=cn, W_=W_, i=i: e.dma_start(out=dr[nm][i, 0:W_ - 8, :], in_=dr[cn][i, 8:W_, :]), is_dma=True)
                    P.final_waits.append(o_.idx)

        for gi, (kind, g) in enumerate(self.groups):
            self.run_group(kind, g)
            if gi == 0:
                cache_copies()

        P.emit(st)
        return nc

    def rows_to_featmajor(self, row_aps, C, dst, dst_key, row0=0):
        P = self.P
        R = len(row_aps)
        with self.phase("stg_%s_%d" % (dst_key, row0)) as ph:
            stg = ph.buf("stg", [R, C])
            skey = ph.k("stg")
            for r, ap in enumerate(row_aps):
                P.op("sp", lambda e, r=r, ap=ap: e.dma_start(out=stg[r:r + 1, :], in_=ap.rearrange("(o c) -> o c", o=1)),
                     writes=[skey], is_dma=True, multi=True)
            nch = C // 128
            per = max(1, min(512 // R, nch))
            for c0 in range(0, nch, per):
                n = min(per, nch - c0)
                ps, pk = self.psum()
                for i in range(n):
                    c = c0 + i
                    P.op("pe", lambda e, ps=ps, i=i, c=c: e.transpose(out=ps[:, i * R:(i + 1) * R], in_=stg[0:R, c * 128:(c + 1) * 128],
                                                                        identity=self.ident[0:R, 0:R]),
                         reads=[skey, "ident"], writes=[pk])
                src = ps[:, 0:n * R].rearrange("p (c r) -> p c r", r=R)
                d = dst[:, c0:c0 + n, row0:row0 + R]
                P.op("dve", lambda e, d=d, src=src: e.tensor_copy(out=d, in_=src), reads=[pk], writes=[dst_key])

    def alloc_persistent(self):
        self.one_t = self.sb("one_t", [128, 1])
        self.negb = self.sb("negb", [128, 2, 1])
        self.gn = self.sb("gn", [128, 1, 1])
        self.cbp = self.sb("cbp", [128, 4, 34])
        self.wg2f = self.sb("wg2f", [16, 256])
        self.wg2 = self.sb("wg2", [16, 256], BF16)
        self.scanmask_p = self.sb("scanmask_p", [128, 512])
        self.scanmask_s = self.sb("scanmask_s", [128, 128])
        self.causal_st = self.sb("causal_st", [128, 128])
        self.causal_s8 = self.sb("causal_s8", [128, 128])
        self.seqmask = self.sb("seqmask", [128, 16], BF16)
        self.seqmaskf = self.sb("seqmaskf", [128, 16])
        self.S = self.sb("S", [128, 2, 128])
        self.Sbf = self.sb("Sbf", [128, 2, 128], BF16)
        self.uhalo = self.sb("uhalo", [128, 4, 30])
        self.Jrev = self.sb("Jrev", [128, 128])
        self.ones_f = self.sb("ones_f", [128, 64])
        self.lngb = self.sb("lngb", [128, 2, 256])
        self.sgub = self.sb("sgub", [128, 2, 128])
        self.sgubs = self.sb("sgubs", [128, 2, 128])
        self.wsT = self.sb("wsT", [128, 4, 128], BF16)
        self.wsTs = self.sb("wsTs", [128, 4, 128], BF16)
        self.tabn = self.sb("tabn", [128, 3, 512], BF16)
        self.tabc = self.sb("tabc", [128, 416], BF16)
        self.tabs = self.sb("tabs", [128, 8, 512], BF16)
        self.kT0 = self.sb("kT0", [128, 2, 640], BF16)
        self.kT1 = self.sb("kT1", [128, 2, 2, 512], BF16)
        self.kT2 = self.sb("kT2", [128, 2, 2048], BF16)
        self.v0 = self.sb("v0", [128, 5, 256], BF16)
        self.v1 = self.sb("v1", [128, 2, 4, 256], BF16)
        self.v2 = self.sb("v2", [128, 16, 256], BF16)

    def load_params(self):
        dr = self.dram
        P = self.P
        P.op("pool", lambda e: e.memset(self.one_t[:], 1.0), writes=["one_t"])
        P.op("pool", lambda e: e.memset(self.ones_f[:], 1.0), writes=["ones_f"])
        P.op("pool", lambda e: e.memset(self.S[:], 0.0), writes=[("S", h) for h in range(4)])
        P.op("pool", lambda e: e.memset(self.Sbf[:], 0.0), writes=[("Sbf", h) for h in range(4)])
        P.op("pool", lambda e: e.memset(self.uhalo[:], 0.0), writes=["uhalo"])
        for nm, t in [("scanmask_p", self.scanmask_p), ("scanmask_s", self.scanmask_s)]:
            P.op("sp", lambda e, nm=nm, t=t: e.dma_start(out=t[:], in_=dr["c_" + nm]), writes=["scanmask"], is_dma=True, multi=True)
        for nm, t in [("causal_st", self.causal_st), ("causal_s8", self.causal_s8), ("seqmask", self.seqmaskf)]:
            P.op("sp", lambda e, nm=nm, t=t: e.dma_start(out=t[:], in_=dr["c_" + nm]), writes=["masks" if nm != "seqmask" else "seqmaskf"], is_dma=True, multi=True)
        P.op("dve", lambda e: e.tensor_copy(out=self.seqmask[:], in_=self.seqmaskf[:]), reads=["seqmaskf"], writes=["masks"])
        P.op("sp", lambda e: e.dma_start(out=self.wg2f[:], in_=dr["w_gate2"][0]), writes=["wg2f"], is_dma=True)
        P.op("dve", lambda e: e.tensor_copy(out=self.wg2[:], in_=self.wg2f[:]), reads=["wg2f"], writes=["wg2"])
        self.rows_to_featmajor([dr["b_gate"][0]], 256, self.negb, "negb")
        P.op("dve", lambda e: e.tensor_scalar(out=self.negb[:], in0=self.negb[:], scalar1=-1.0, scalar2=None, op0=ALU.mult), reads=["negb"], writes=["negb"])
        self.rows_to_featmajor([dr["gla_norm"][0]], 128, self.gn, "gn")
        rows = [dr["conv_b_w"][0, j] for j in range(31)] + [dr["conv_b_b"][0], dr["ln_b_g"][0], dr["ln_b_b"][0]]
        self.rows_to_featmajor(rows, 512, self.cbp, "cbp")
        rows = []
        for n in ["norm_pre_mix", "norm_post_mix", "norm_pre_ffn", "norm_post_ffn"]:
            rows += [dr[n][0], dr[n][1]]
        self.rows_to_featmajor(rows, 1024, self.gains, "gains")
        rows = []
        for l in range(2):
            rows += [dr["ffn_dw_w"][l, 0], dr["ffn_dw_w"][l, 1], dr["ffn_dw_w"][l, 2], dr["ffn_dw_b"][l]]
        self.rows_to_featmajor(rows, 5632, self.ffnp, "ffnp")

    @contextlib.contextmanager
    def phase(self, tag):
        ph = Phase(self, tag)
        try:
            yield ph
        finally:
            ph.close()

    def ssq_rstd(self, src, src_key, T, scale_n, sq, sq_keys):
        P = self.P
        rstd = self.rstd
        P.op("act", lambda e: e.activation(out=sq[:, :, 0:T], in_=src[:, :, 0:T], func=AF.Square), reads=[src_key], writes=sq_keys)
        ps, pk = self.psum()
        for k in range(8):
            P.op("pe", lambda e, k=k, ps=ps: e.matmul(ps[:, 0:T], lhsT=self.ones_bf[:], rhs=sq[:, k, 0:T], start=(k == 0), stop=(k == 7)),
                 reads=sq_keys + ["ones_bf"], writes=[pk])
        P.op("act", lambda e, ps=ps: e.activation(out=rstd[:, 0:T], in_=ps[:, 0:T], func=AF.Ln, bias=self.eps_t[:], scale=1.0 / scale_n),
             reads=[pk, "eps"], writes=["rstd"])
        P.op("act", lambda e: e.activation(out=rstd[:, 0:T], in_=rstd[:, 0:T], func=AF.Exp, scale=-0.5), reads=["rstd"], writes=["rstd"])

    def prenorm(self, grow, T):
        P = self.P
        sqv = self.yT[:, 0:4, :].bitcast(BF16).rearrange("p a (b c) -> p (a b) c", b=2)
        self.ssq_rstd(self.xT, "xT", T, 1024.0, sqv, ["yT"])
        for k in range(8):
            P.op("dve", lambda e, k=k: e.scalar_tensor_tensor(out=self.hT[:, k, 0:T], in0=self.xT[:, k, 0:T], scalar=self.gains[:, k, grow:grow + 1],
                                                               in1=self.rstd[:, 0:T], op0=ALU.mult, op1=ALU.mult),
                 reads=["xT", "gains", "rstd"], writes=[("hT", k)])

    def postnorm_residual(self, grow, T):
        P = self.P
        self.ssq_rstd(self.yT, "yT", T, 1024.0, self.hT, [("hT", k) for k in range(8)])
        for k in range(8):
            P.op("dve", lambda e, k=k: e.tensor_tensor(out=self.yT[:, k, 0:T], in0=self.yT[:, k, 0:T], in1=self.rstd[:, 0:T], op=ALU.mult),
                 reads=["yT", "rstd"], writes=[("yTs", k)])
        for k in range(8):
            P.op("dve", lambda e, k=k: e.scalar_tensor_tensor(out=self.xT[:, k, 0:T], in0=self.yT[:, k, 0:T], scalar=self.gains[:, k, grow:grow + 1],
                                                               in1=self.xT[:, k, 0:T], op0=ALU.mult, op1=ALU.add),
                 reads=[("yTs", k), "yT", "gains", "xT"], writes=["xT"])

    def run_group(self, kind, g):
        P, dr = self.P, self.dram
        T = 512 if kind == "P" else 128
        nt = T // 128
        with self.phase("xin_%s%d" % (kind, g)) as ph:
            xin = ph.buf("xin", [128, nt, 1024])
            src = (dr["xp"][g * 512:(g + 1) * 512, :] if kind == "P" else dr["xs"]).rearrange("(t p) f -> p t f", p=128)
            P.op("sp", lambda e, src=src, xin=xin: e.dma_start(out=xin[:], in_=src), writes=[ph.k("xin")], is_dma=True)
            for k in range(8):
                ps, pk = self.psum()
                for t in range(nt):
                    P.op("pe", lambda e, ps=ps, t=t, k=k: e.transpose(out=ps[:, t * 128:(t + 1) * 128], in_=xin[:, t, k * 128:(k + 1) * 128],
                                                                        identity=self.ident[:]),
                         reads=[ph.k("xin"), "ident"], writes=[pk])
                P.op("act", lambda e, ps=ps, k=k: e.copy(out=self.xT[:, k, 0:T], in_=ps[:, 0:T]), reads=[pk], writes=["xT"])
        for layer in range(2):
            if layer == 0:
                self.even_mixer(kind, g, T)
            else:
                self.odd_mixer(kind, g, T)
            self.ffn(kind, g, T, layer)
        with self.phase("yout_%s%d" % (kind, g)) as ph:
            yo = ph.buf("yo", [128, nt, 1024])
            for t in range(nt):
                for k0 in range(0, 8, 4):
                    ps, pk = self.psum()
                    for k in range(k0, k0 + 4):
                        P.op("pe", lambda e, ps=ps, t=t, k=k, k0=k0: e.transpose(out=ps[:, (k - k0) * 128:(k - k0 + 1) * 128],
                                                                                  in_=self.xT[:, k, t * 128:(t + 1) * 128], identity=self.ident[:]),
                             reads=["xT", "ident"], writes=[pk])
                    P.op("act", lambda e, ps=ps, t=t, k0=k0: e.copy(out=yo[:, t, k0 * 128:(k0 + 4) * 128], in_=ps[:, 0:512]),
                         reads=[pk], writes=[ph.k("yo")])
            dst = (dr["y_p"][g * 512:(g + 1) * 512, :] if kind == "P" else dr["y_s"]).rearrange("(t p) f -> p t f", p=128)
            o = P.op("sp", lambda e, dst=dst, yo=yo: e.dma_start(out=dst, in_=yo[:]), reads=[ph.k("yo")], is_dma=True)
            P.final_waits.append(o.idx)

    def even_mixer(self, kind, g, T):
        P, dr = self.P, self.dram
        nt = T // 128
        isP = kind == "P"
        self.prenorm(0, T)
        hT = self.hT
        hreads = [("hT", k) for k in range(8)]

        def proj_fm(w, wk, col0, M, dst_fn, dst_key, func=None, psrows=128):
            ps, pk = self.psum()
            for k in range(8):
                P.op("pe", lambda e, ps=ps, k=k: e.matmul(ps[0:M, 0:T], lhsT=w[:, k, col0:col0 + M], rhs=hT[:, k, 0:T], start=(k == 0), stop=(k == 7)),
                     reads=[wk, ("hT", k)], writes=[pk])
            if func is None:
                P.op("act", lambda e, ps=ps: e.copy(out=dst_fn, in_=ps[0:M, 0:T]), reads=[pk], writes=[dst_key])
            else:
                P.op("act", lambda e, ps=ps: e.activation(out=dst_fn, in_=ps[0:M, 0:T], func=func), reads=[pk], writes=[dst_key])

        with self.phase("ev_%s%d" % (kind, g)) as ph:
            K = ph.k
            qt = ph.buf("qt", [128, 2, T], BF16)
            kt = ph.buf("kt", [128, 2, T], BF16)
            ebl_t = ph.buf("ebl", [128, 2, 16])
            khtok = ph.buf("khtok", [128, nt, 256], BF16)
            vtok = ph.buf("vtok", [128, nt, 512], BF16)
            sr = ph.buf("sr", [128, 4, T], BF16)
            cat = ph.buf("cat", [128, 8, T], BF16)
            oT = ph.buf("oT", [128, 4, T])
            if not isP:
                Ss = ph.buf("Ss", [128, 2, 16, 128])
                Ssb = ph.buf("Ssb", [128, 2, 16, 128], BF16)
                src = dr["sgla"].rearrange("i (c two) d v -> (two d) c i v", two=2)
                for c in range(2):
                    P.op("sp", lambda e, c=c, src=src: e.dma_start(out=Ss[:, c, :, :], in_=src[:, c, :, :]), writes=[K("Ss")], is_dma=True, multi=True)
                P.op("pool", lambda e: e.tensor_copy(out=Ssb[:], in_=Ss[:]), reads=[K("Ss")], writes=[K("Ssb")])
            with self.phase("evA_%s%d" % (kind, g)) as pa:
                KA = pa.k
                qk = pa.buf("qk", [128, 4, T])
                glr = pa.buf("glr", [16, T], BF16)
                la = pa.buf("la", [128, 2, T])
                enb = pa.buf("enb", [128, 2, T])
                eb = pa.buf("eb", [128, 2, T])
                kh = pa.buf("kh", [128, 2, T])
                w, wk = self.next_w("E0")
                for c in range(4):
                    proj_fm(w, wk, c * 128, 128, qk[:, c, 0:T], KA("qk", c))
                w, wk = self.next_w("E2")
                proj_fm(w, wk, 0, 16, glr[0:16, 0:T], KA("glr"))
                for c in range(2):
                    ps, pk = self.psum()
                    P.op("pe", lambda e, ps=ps, c=c: e.matmul(ps[:, 0:T], lhsT=self.wg2[0:16, c * 128:(c + 1) * 128], rhs=glr[0:16, 0:T], start=True, stop=True),
                         reads=["wg2", KA("glr")], writes=[pk])
                    P.op("act", lambda e, ps=ps, c=c: e.activation(out=la[:, c, :], in_=ps[:, 0:T], func=AF.Exp, bias=self.negb[:, c, :], scale=-1.0),
                         reads=[pk, "negb"], writes=[KA("la", c)])
                    P.op("act", lambda e, c=c: e.activation(out=la[:, c, :], in_=la[:, c, :], func=AF.Ln, bias=self.one_t[:], scale=1.0),
                         reads=[KA("la", c), "one_t"], writes=[KA("la", c)])
                    sm = self.scanmask_p if isP else self.scanmask_s
                    P.op("dve", lambda e, c=c, sm=sm: e.tensor_tensor_scan(out=la[:, c, :], data0=sm[:, 0:T], data1=la[:, c, :], initial=0.0,
                                                                          op0=ALU.mult, op1=ALU.add),
                         reads=[KA("la", c), "scanmask"], writes=[KA("la", c)])
                    P.op("act", lambda e, c=c: e.activation(out=eb[:, c, :], in_=la[:, c, :], func=AF.Exp, scale=-1.0 / 16.0), reads=[KA("la", c)], writes=[KA("eb", c)])
                    nseg, seg = (nt, 128) if isP else (16, 8)
                    ebv0 = eb[:, c, :]
                    ends = bass.AP(ebv0.tensor, ebv0.offset + seg - 1, [[ebv0.ap[0][0], 128], [seg, nseg]])
                    P.op("pool", lambda e, c=c, ends=ends, nseg=nseg: e.tensor_copy(out=ebl_t[:, c, 0:nseg], in_=ends), reads=[KA("eb", c)], writes=[K("ebl", c)])
                    P.op("act", lambda e, c=c: e.activation(out=enb[:, c, :], in_=la[:, c, :], func=AF.Exp, scale=1.0 / 16.0), reads=[KA("la", c)], writes=[KA("enb", c)])
                    P.op("dve", lambda e, c=c: e.scalar_tensor_tensor(out=qt[:, c, :], in0=qk[:, c, :], scalar=0.125, in1=eb[:, c, :], op0=ALU.mult, op1=ALU.mult),
                         reads=[KA("qk", c), KA("eb", c)], writes=[K("qt", c)])
                    P.op("dve", lambda e, c=c: e.tensor_tensor(out=kt[:, c, :], in0=qk[:, 2 + c, :], in1=enb[:, c, :], op=ALU.mult),
                         reads=[KA("qk", 2 + c), KA("enb", c)], writes=[K("kt", c)])
                    if isP:
                        for t in range(nt):
                            te = (t + 1) * 128
                            P.op("dve", lambda e, c=c, t=t, te=te: e.scalar_tensor_tensor(out=kh[:, c, t * 128:te], in0=qk[:, 2 + c, t * 128:te], scalar=eb[:, c, te - 1:te],
                                                                                         in1=enb[:, c, t * 128:te], op0=ALU.mult, op1=ALU.mult),
                                 reads=[KA("qk", 2 + c), KA("eb", c), KA("enb", c)], writes=[KA("kh", c)])
                    else:
                        P.op("dve", lambda e, c=c: e.tensor_tensor(out=kh[:, c, :], in0=qk[:, 2 + c, :], in1=enb[:, c, :], op=ALU.mult),
                             reads=[KA("qk", 2 + c), KA("enb", c)], writes=[KA("kh", c)])
                        ebv = eb[:, c, :]
                        ebl = bass.AP(ebv.tensor, ebv.offset + 7, [[ebv.ap[0][0], 128], [8, 16], [0, 8]])
                        P.op("dve", lambda e, c=c, ebl=ebl: e.tensor_tensor(out=kh[:, c, :].rearrange("p (i t) -> p i t", t=8), in0=kh[:, c, :].rearrange("p (i t) -> p i t", t=8),
                                                                           in1=ebl, op=ALU.mult),
                             reads=[KA("kh", c), KA("eb", c)], writes=[KA("kh", c)])
                pairs = [(t, c) for t in range(nt) for c in range(2)]
                for p0 in range(0, len(pairs), 4):
                    ps, pk = self.psum()
                    grp = pairs[p0:p0 + 4]
                    for i, (t, c) in enumerate(grp):
                        P.op("pe", lambda e, ps=ps, i=i, t=t, c=c: e.transpose(out=ps[:, i * 128:(i + 1) * 128], in_=kh[:, c, t * 128:(t + 1) * 128], identity=self.ident[:]),
                             reads=[KA("kh", c), "ident"], writes=[pk])
                    t0 = grp[0][0]
                    ntl = len(grp) // 2
                    P.op("act", lambda e, ps=ps, t0=t0, ntl=ntl: e.copy(out=khtok[:, t0:t0 + ntl, :], in_=ps[:, 0:ntl * 256].rearrange("p (t f) -> p t f", f=256)),
                         reads=[pk], writes=[K("khtok")])
                w, wk = self.next_w("E1")
                for t in range(nt):
                    ps, pk = self.psum()
                    for k in range(8):
                        P.op("pe", lambda e, ps=ps, k=k, t=t, w=w: e.matmul(ps[:, 0:512], lhsT=hT[:, k, t * 128:(t + 1) * 128], rhs=w[:, k, 0:512], start=(k == 0), stop=(k == 7)),
                             reads=[wk, ("hT", k)], writes=[pk])
                    P.op("act", lambda e, ps=ps, t=t: e.copy(out=vtok[:, t, :], in_=ps[:, 0:512]), reads=[pk], writes=[K("vtok", t)])
                w, wk = self.next_w("E3")
                for c in range(4):
                    proj_fm(w, wk, c * 128, 128, sr[:, c, 0:T], K("sr", c), func=AF.Silu)
            with self.phase("evB_%s%d" % (kind, g)) as pb:
                KB_ = pb.k
                at = [pb.buf("at%d" % i, [128, 128], BF16) for i in range(2)]
                mask = self.causal_st if isP else self.causal_s8
                if not isP:
                    vexp = [pb.buf("vexp%d" % i, [128, 16, 128], BF16) for i in range(2)]
                    tmpS = pb.buf("tmpS", [128, 4, 128])
                ai = 0
                for t in range(nt):
                    for h in range(4):
                        c, po = h // 2, (h % 2) * 64
                        a = at[ai % 2]
                        ak = KB_("at%d" % (ai % 2))
                        ai += 1
                        psa, pka = self.psum()
                        P.op("pe", lambda e, psa=psa, c=c, po=po, t=t: e.matmul(psa[:, 0:128], lhsT=kt[po:po + 64, c, t * 128:(t + 1) * 128], rhs=qt[po:po + 64, c, t * 128:(t + 1) * 128],
                                                                                 start=True, stop=True),
                             reads=[K("kt", c), K("qt", c)], writes=[pka])
                        P.op("dve", lambda e, psa=psa, a=a: e.tensor_tensor(out=a[:], in0=psa[:, 0:128], in1=mask[:], op=ALU.mult), reads=[pka, "masks"], writes=[ak])
                        pso, pko = self.psum()
                        P.op("pe", lambda e, pso=pso, a=a, t=t, h=h: e.matmul(pso[:, 0:128], lhsT=vtok[:, t, h * 128:(h + 1) * 128], rhs=a[:], start=True, stop=(not isP), skip_group_check=(not isP)),
                             reads=[K("vtok", t), ak], writes=[pko])
                        if isP:
                            P.op("pe", lambda e, pso=pso, c=c, po=po, t=t: e.matmul(pso[:, 0:128], lhsT=self.Sbf[po:po + 64, c, :], rhs=qt[po:po + 64, c, t * 128:(t + 1) * 128],
                                                                                     start=False, stop=True),
                                 reads=[("Sbf", h), K("qt", c)], writes=[pko])
                        else:
                            for i in range(16):
                                P.op("pe", lambda e, pso=pso, c=c, po=po, i=i: e.matmul(pso[:, 8 * i:8 * i + 8], lhsT=Ssb[po:po + 64, c, i, :], rhs=qt[po:po + 64, c, 8 * i:8 * i + 8],
                                                                                         start=False, stop=True, skip_group_check=True),
                                     reads=[K("Ssb"), K("qt", c)], writes=[pko])
                        P.op("act", lambda e, pso=pso, h=h, t=t: e.copy(out=oT[:, h, t * 128:(t + 1) * 128], in_=pso[:, 0:128]), reads=[pko], writes=[K("oT", h)])
                        if isP:
                            pss, pks = self.psum()
                            P.op("pe", lambda e, pss=pss, c=c, po=po, t=t, h=h: e.matmul(pss[po:po + 64, 0:128], lhsT=khtok[:, t, c * 128 + po:c * 128 + po + 64],
                                                                                          rhs=vtok[:, t, h * 128:(h + 1) * 128], start=True, stop=True),
                                 reads=[K("khtok"), K("vtok", t)], writes=[pks])
                            te = (t + 1) * 128
                            P.op("dve", lambda e, pss=pss, c=c, po=po, t=t: e.scalar_tensor_tensor(out=self.S[po:po + 64, c, :], in0=self.S[po:po + 64, c, :], scalar=ebl_t[po:po + 64, c, t:t + 1],
                                                                                                     in1=pss[po:po + 64, 0:128], op0=ALU.mult, op1=ALU.add),
                                 reads=[pks, ("S", h), K("ebl", c)], writes=[("S", h)])
                            P.op("act", lambda e, c=c, po=po: e.copy(out=self.Sbf[po:po + 64, c, :], in_=self.S[po:po + 64, c, :]), reads=[("S", h)], writes=[("Sbf", h)])
                        else:
                            vx = vexp[h % 2]
                            vk = KB_("vexp%d" % (h % 2))
                            vv = vtok[:, 0, h * 128:(h + 1) * 128]
                            vb = bass.AP(vv.tensor, vv.offset, [[vv.ap[0][0], 128], [0, 16], [1, 128]])
                            sm = self.seqmask[:]
                            smb = bass.AP(sm.tensor, sm.offset, [[sm.ap[0][0], 128], [1, 16], [0, 128]])
                            P.op("dve", lambda e, vx=vx, vb=vb, smb=smb: e.tensor_tensor(out=vx[:], in0=vb, in1=smb, op=ALU.mult), reads=[K("vtok", 0), "masks"], writes=[vk])
                            for blk in range(4):
                                pss, pks = self.psum()
                                P.op("pe", lambda e, pss=pss, c=c, po=po, vx=vx, blk=blk: e.matmul(pss[po:po + 64, 0:512], lhsT=khtok[:, 0, c * 128 + po:c * 128 + po + 64],
                                                                                                rhs=vx[:, 4 * blk:4 * blk + 4, :].rearrange("p i v -> p (i v)"), start=True, stop=True),
                                     reads=[K("khtok"), vk], writes=[pks])
                                ebv = ebl_t[po:po + 64, c, :]
                                ebl = bass.AP(ebv.tensor, ebv.offset + 4 * blk, [[ebv.ap[0][0], 64], [1, 4], [0, 128]])
                                P.op("dve", lambda e, c=c, po=po, blk=blk, ebl=ebl: e.tensor_tensor(out=tmpS[po:po + 64, :, :], in0=Ss[po:po + 64, c, 4 * blk:4 * blk + 4, :], in1=ebl, op=ALU.mult),
                                     reads=[K("Ss"), K("ebl", c)], writes=[KB_("tmpS", po)])
                                P.op("dve", lambda e, pss=pss, c=c, po=po, blk=blk: e.tensor_tensor(out=Ss[po:po + 64, c, 4 * blk:4 * blk + 4, :], in0=tmpS[po:po + 64, :, :],
                                                                                                  in1=pss[po:po + 64, 0:512].rearrange("p (i v) -> p i v", v=128), op=ALU.add),
                                     reads=[KB_("tmpS", po), pks], writes=[K("Ss")])
                if not isP:
                    dst = dr["s_gla"].rearrange("i (c two) d v -> (two d) c i v", two=2)
                    for c in range(2):
                        o_ = P.op("sp", lambda e, c=c, dst=dst: e.dma_start(out=dst[:, c, :, :], in_=Ss[:, c, :, :]), reads=[K("Ss")], is_dma=True)
                        P.final_waits.append(o_.idx)
                elif g == 3:
                    dst = dr["p_gla"].rearrange("(c two) d v -> (two d) c v", two=2)
                    o_ = P.op("sp", lambda e, dst=dst: e.dma_start(out=dst, in_=self.S[:]), reads=[("S", h) for h in range(4)], is_dma=True)
                    P.final_waits.append(o_.idx)
            with self.phase("evC_%s%d" % (kind, g)) as pc:
                KC_ = pc.k
                sqo = pc.buf("sqo", [128, T], BF16)
                rs = pc.buf("rs", [128, T])
                tmp = pc.buf("tmp", [128, T])
                for h in range(4):
                    P.op("act", lambda e, h=h: e.activation(out=sqo[:], in_=oT[:, h, :], func=AF.Square), reads=[K("oT", h)], writes=[KC_("sqo")])
                    ps, pk = self.psum()
                    P.op("pe", lambda e, ps=ps: e.matmul(ps[:, 0:T], lhsT=self.ones_bf[:], rhs=sqo[:], start=True, stop=True), reads=[KC_("sqo"), "ones_bf"], writes=[pk])
                    P.op("act", lambda e, ps=ps: e.activation(out=rs[:], in_=ps[:, 0:T], func=AF.Sqrt, bias=self.eps_t[:], scale=1.0 / 128.0), reads=[pk, "eps"], writes=[KC_("rs")])
                    P.op("dve", lambda e: e.reciprocal(out=rs[:], in_=rs[:]), reads=[KC_("rs")], writes=[KC_("rs")])
                    P.op("dve", lambda e, h=h: e.scalar_tensor_tensor(out=tmp[:], in0=oT[:, h, :], scalar=self.gn[:, 0, :], in1=rs[:], op0=ALU.mult, op1=ALU.mult),
                         reads=[K("oT", h), "gn", KC_("rs")], writes=[KC_("tmp")])
                    P.op("dve", lambda e, h=h: e.tensor_tensor(out=cat[:, h, :], in0=tmp[:], in1=sr[:, h, :], op=ALU.mult), reads=[KC_("tmp"), K("sr", h)], writes=[K("cat", h)])
            with self.phase("evD_%s%d" % (kind, g)) as pd:
                KD = pd.k
                if isP:
                    ub = pd.buf("ub", [128, 4, 30 + T])
                else:
                    ub = pd.buf("ub", [128, 4, 16, 38])
                    stg = pd.buf("stg", [120, 4, 512])
                    P.op("sp", lambda e: e.dma_start(out=stg[:], in_=dr["scb"].rearrange("(q a) j c -> (a j) q c", a=4)), writes=[KD("stg")], is_dma=True)
                    for q in range(4):
                        ps, pk = self.psum()
                        for c in range(4):
                            P.op("pe", lambda e, ps=ps, q=q, c=c: e.transpose(out=ps[:, c * 120:(c + 1) * 120], in_=stg[0:120, q, c * 128:(c + 1) * 128], identity=self.ident[0:120, 0:120]),
                                 reads=[KD("stg"), "ident"], writes=[pk])
                        P.op("act", lambda e, ps=ps, q=q: e.copy(out=ub[:, :, 4 * q:4 * q + 4, 0:30], in_=ps[:, 0:480].rearrange("p (c a j) -> p c a j", c=4, a=4)),
                             reads=[pk], writes=[KD("ub")])
                    o_ = P.op("sp", lambda e: e.dma_start(out=dr["s_conv_b"][:, 0:22, :], in_=dr["scb"][:, 8:30, :]), is_dma=True)
                    P.final_waits.append(o_.idx)
                sg = pd.buf("sg", [128, T])
                cc = pd.buf("cc", [128, 4, T])
                ccb = pd.buf("ccb", [128, 4, T], BF16)
                ccs = pd.buf("ccs", [128, 4, T], BF16)
                mean = pd.buf("mean", [128, T])
                var = pd.buf("var", [128, T])
                w4, wk4 = self.next_w("E4")
                w5, wk5 = self.next_w("E5")
                if isP:
                    P.op("pool", lambda e: e.tensor_copy(out=ub[:, :, 0:30], in_=self.uhalo[:]), reads=["uhalo"], writes=[KD("ub")])
                for c in range(4):
                    psa, pka = self.psum()
                    psg, pkg = self.psum()
                    for k in range(8):
                        P.op("pe", lambda e, psa=psa, k=k, c=c: e.matmul(psa[:, 0:T], lhsT=w4[:, k, c * 128:(c + 1) * 128], rhs=hT[:, k, 0:T], start=(k == 0), stop=(k == 7)),
                             reads=[wk4, ("hT", k)], writes=[pka])
                    for k in range(8):
                        P.op("pe", lambda e, psg=psg, k=k, c=c: e.matmul(psg[:, 0:T], lhsT=w5[:, k, c * 128:(c + 1) * 128], rhs=hT[:, k, 0:T], start=(k == 0), stop=(k == 7)),
                             reads=[wk5, ("hT", k)], writes=[pkg])
                    P.op("act", lambda e, psg=psg: e.activation(out=sg[:], in_=psg[:, 0:T], func=AF.Sigmoid), reads=[pkg], writes=[KD("sg")])
                    if isP:
                        udst, pv, sv = ub[:, c, 30:30 + T], psa[:, 0:T], sg[:]
                    else:
                        udst = ub[:, c, :, 30:38]
                        pv = psa[:, 0:T].rearrange("p (i t) -> p i t", t=8)
                        sv = sg[:].rearrange("p (i t) -> p i t", t=8)
                    P.op("dve", lambda e, udst=udst, pv=pv, sv=sv: e.tensor_tensor(out=udst, in0=pv, in1=sv, op=ALU.mult), reads=[pka, KD("sg")], writes=[KD("ub")])
                if isP:
                    P.op("pool", lambda e: e.tensor_copy(out=self.uhalo[:], in_=ub[:, :, T:T + 30]), reads=[KD("ub")], writes=["uhalo"])
                    if g == 3:
                        dst = dr["p_conv_b"].rearrange("j (c p) -> p c j", p=128)
                        for c in range(4):
                            o_ = P.op("sp", lambda e, c=c, dst=dst: e.dma_start(out=dst[:, c, :], in_=self.uhalo[:, c, :], allow_slow_non_contiguous=True), reads=["uhalo"], is_dma=True)
                            P.final_waits.append(o_.idx)
                def uview(c, j):
                    return ub[:, c, j:j + T] if isP else ub[:, c, :, j:j + 8]
                cvs = [cc[:, c, :] if isP else cc[:, c, :].rearrange("p (i t) -> p i t", t=8) for c in range(4)]
                for c in range(4):
                    P.op("dve", lambda e, c=c, cv=cvs[c], u0=uview(c, 0): e.tensor_scalar(out=cv, in0=u0, scalar1=self.cbp[:, c, 0:1], scalar2=self.cbp[:, c, 31:32], op0=ALU.mult, op1=ALU.add),
                         reads=[KD("ub"), "cbp"], writes=[KD("cc", c)])
                for j in range(1, 31):
                    for c in range(4):
                        P.op("dve", lambda e, c=c, cv=cvs[c], uj=uview(c, j), j=j: e.scalar_tensor_tensor(out=cv, in0=uj, scalar=self.cbp[:, c, j:j + 1], in1=cv, op0=ALU.mult, op1=ALU.add),
                             reads=[KD("ub"), "cbp", KD("cc", c)], writes=[KD("cc", c)])
                for c in range(4):
                    P.op("act", lambda e, c=c: e.copy(out=ccb[:, c, :], in_=cc[:, c, :]), reads=[KD("cc", c)], writes=[KD("ccb", c)])
                    P.op("act", lambda e, c=c: e.activation(out=ccs[:, c, :], in_=cc[:, c, :], func=AF.Square), reads=[KD("cc", c)], writes=[KD("ccs", c)])
                if not isP:
                    un = pd.buf("un", [128, 512])
                    psu, pku = self.psum()
                    uc = pd.buf("uc", [128, 4, 128])
                    P.op("pool", lambda e: e.tensor_copy(out=uc[:].rearrange("p c (i t) -> p c i t", t=8), in_=ub[:, :, :, 30:38]), reads=[KD("ub")], writes=[KD("uc")])
                    for c in range(4):
                        P.op("pe", lambda e, c=c: e.transpose(out=psu[:, c * 128:(c + 1) * 128], in_=uc[:, c, :], identity=self.ident[:]), reads=[KD("uc"), "ident"], writes=[pku])
                    P.op("act", lambda e: e.copy(out=un[:], in_=psu[:, 0:512]), reads=[pku], writes=[KD("un")])
                    for i in range(16):
                        o_ = P.op("sp", lambda e, i=i: e.dma_start(out=dr["s_conv_b"][i, 22:30, :], in_=un[8 * i:8 * i + 8, :]), reads=[KD("un")], is_dma=True)
                        P.final_waits.append(o_.idx)
                ps1, pk1 = self.psum()
                ps2, pk2 = self.psum()
                for c in range(4):
                    P.op("pe", lambda e, c=c: e.matmul(ps1[:, 0:T], lhsT=self.ones_bf[:], rhs=ccb[:, c, :], start=(c == 0), stop=(c == 3)), reads=[KD("ccb", c), "ones_bf"], writes=[pk1])
                for c in range(4):
                    P.op("pe", lambda e, c=c: e.matmul(ps2[:, 0:T], lhsT=self.ones_bf[:], rhs=ccs[:, c, :], start=(c == 0), stop=(c == 3)), reads=[KD("ccs", c), "ones_bf"], writes=[pk2])
                P.op("act", lambda e: e.activation(out=mean[:], in_=ps1[:, 0:T], func=AF.Copy, scale=1.0 / 512.0), reads=[pk1], writes=[KD("mean")])
                P.op("dve", lambda e: e.tensor_tensor(out=var[:], in0=mean[:], in1=mean[:], op=ALU.mult), reads=[KD("mean")], writes=[KD("var")])
                P.op("dve", lambda e: e.scalar_tensor_tensor(out=var[:], in0=ps2[:, 0:T], scalar=1.0 / 512.0, in1=var[:], op0=ALU.mult, op1=ALU.subtract), reads=[pk2, KD("var")], writes=[KD("var")])
                P.op("act", lambda e: e.activation(out=var[:], in_=var[:], func=AF.Sqrt, bias=self.eps_t[:], scale=1.0), reads=[KD("var"), "eps"], writes=[KD("var")])
                P.op("dve", lambda e: e.reciprocal(out=var[:], in_=var[:]), reads=[KD("var")], writes=[KD("var")])
                for c in range(4):
                    P.op("dve", lambda e, c=c: e.tensor_tensor(out=cc[:, c, :], in0=cc[:, c, :], in1=mean[:], op=ALU.subtract), reads=[KD("cc", c), KD("mean")], writes=[KD("cc", c)])
                    P.op("dve", lambda e, c=c: e.tensor_tensor(out=cc[:, c, :], in0=cc[:, c, :], in1=var[:], op=ALU.mult), reads=[KD("cc", c), KD("var")], writes=[KD("cc", c)])
                    P.op("act", lambda e, c=c: e.activation(out=cat[:, 4 + c, :], in_=cc[:, c, :], func=AF.Silu, bias=self.cbp[:, c, 33:34], scale=self.cbp[:, c, 32:33]),
                         reads=[KD("cc", c), "cbp"], writes=[K("cat", 4 + c)])
            for half in range(2):
                w, wk = self.next_w("EO%d" % half)
                for oc in range(4):
                    o = half * 4 + oc
                    ps, pk = self.psum()
                    for k in range(8):
                        P.op("pe", lambda e, ps=ps, k=k, oc=oc, w=w: e.matmul(ps[:, 0:T], lhsT=w[:, k, oc * 128:(oc + 1) * 128], rhs=cat[:, k, :], start=(k == 0), stop=(k == 7)),
                             reads=[wk, K("cat", k)], writes=[pk])
                    P.op("act", lambda e, ps=ps, o=o: e.copy(out=self.yT[:, o, 0:T], in_=ps[:, 0:T]), reads=[pk], writes=["yT"])
        self.postnorm_residual(2, T)

    def build_bias_tables(self):
        P, dr, nc = self.P, self.dram, self.nc
        self.LU = nc.dram_tensor("LU", [3, 4, 640], F32, kind="Internal").ap()
        self.LD = nc.dram_tensor("LD", [4, 4, 1024], F32, kind="Internal").ap()
        dil = (1, 4, 16)
        with self.phase("bias") as ph:
            K = ph.k
            rb = ph.buf("rb", [32, 12])
            oh = ph.buf("oh", [32, 3, 129])
            ebv = ph.buf("ebv", [4, 3, 129])
            zt = ph.buf("zt", [4, 1024])
            tp = [ph.buf("tp%d" % i, [128, 512]) for i in range(2)]
            P.op("sp", lambda e: e.dma_start(out=rb[:], in_=dr["rel_bias"]), writes=[K("rb")], is_dma=True)
            P.op("sp", lambda e: e.dma_start(out=oh[:], in_=dr["c_onehot"]), writes=[K("oh")], is_dma=True)
            P.op("pool", lambda e: e.memset(zt[:], 0.0), writes=[K("zt")])
            for g in range(3):
                ps, pk = self.psum()
                P.op("pe", lambda e, ps=ps, g=g: e.matmul(ps[0:4, 0:129], lhsT=rb[0:32, 4 * g:4 * g + 4], rhs=oh[0:32, g, :], start=True, stop=True),
                     reads=[K("rb"), K("oh")], writes=[pk])
                P.op("act", lambda e, ps=ps, g=g: e.activation(out=ebv[0:4, g, :], in_=ps[0:4, 0:129], func=AF.Exp), reads=[pk], writes=[K("ebv")])
                P.op("sp", lambda e, g=g: e.dma_start(out=self.LU[g], in_=zt[:, 0:640]), reads=[K("zt")], writes=[("LU", g)], is_dma=True)
                P.op("sp", lambda e, g=g: e.dma_start(out=self.LD[g], in_=zt[:, 0:1024]), reads=[K("zt")], writes=[("LD", g)], is_dma=True)
                P.op("sp", lambda e, g=g: e.dma_start(out=self.LU[g, :, 128:257], in_=ebv[0:4, g, :]), reads=[K("ebv")], writes=[("LU", g)], is_dma=True)
                nj = 129 if g < 2 else 33
                ldv = self.LD[g]
                dst = bass.AP(ldv.tensor, ldv.offset + 8, [[1024, 4], [dil[g], nj]])
                P.op("sp", lambda e, g=g, dst=dst, nj=nj: e.dma_start(out=dst, in_=ebv[0:4, g, 0:nj], allow_slow_non_contiguous=True),
                     reads=[K("ebv")], writes=[("LD", g)], is_dma=True)
            P.op("sp", lambda e: e.dma_start(out=self.LD[3], in_=zt[:, 0:1024]), reads=[K("zt")], writes=[("LD", 3)], is_dma=True)
            ldv = self.LD[3]
            dst = bass.AP(ldv.tensor, ldv.offset + 128, [[1024, 4], [4, 129]])
            P.op("sp", lambda e, dst=dst: e.dma_start(out=dst, in_=ebv[0:4, 2, :], allow_slow_non_contiguous=True), reads=[K("ebv")], writes=[("LD", 3)], is_dma=True)
            ti = 0

            def finish(t, tk, dst_ap, dst_key, ncols):
                ps, pk = self.psum()
                P.op("pe", lambda e, ps=ps, t=t: e.matmul(ps[:, 0:ncols], lhsT=self.Jrev[:], rhs=t[:, 0:ncols], start=True, stop=True), reads=[tk, "Jrev"], writes=[pk])
                P.op("act", lambda e, ps=ps: e.copy(out=dst_ap, in_=ps[:, 0:ncols]), reads=[pk], writes=[dst_key])

            for g in range(2):
                for wv in range(2):
                    t, tk = tp[ti % 2], K("tp%d" % (ti % 2))
                    ti += 1
                    lv = self.LU[g]
                    src = bass.AP(lv.tensor, lv.offset + (1 if wv == 0 else 129), [[1, 128], [640, 4], [1, 128]])
                    P.op("sp", lambda e, t=t, src=src: e.dma_start(out=t[:].rearrange("p (h q) -> p h q", h=4), in_=src), reads=[("LU", g)], writes=[tk], is_dma=True)
                    finish(t, tk, self.tabs[:, 2 * g + wv, :], ("tabs", 2 * g + wv), 512)
            for dG in range(4):
                t, tk = tp[ti % 2], K("tp%d" % (ti % 2))
                ti += 1
                lv = self.LD[3]
                src = bass.AP(lv.tensor, lv.offset + 1 + 128 * dG, [[1, 128], [1024, 4], [1, 128]])
                P.op("sp", lambda e, t=t, src=src: e.dma_start(out=t[:].rearrange("p (h q) -> p h q", h=4), in_=src), reads=[("LD", 3)], writes=[tk], is_dma=True)
                finish(t, tk, self.tabs[:, 4 + dG, :], ("tabs", 4 + dG), 512)
            for g in range(3):
                t, tk = tp[ti % 2], K("tp%d" % (ti % 2))
                ti += 1
                P.op("pool", lambda e, t=t: e.memset(t[:], 0.0), writes=[tk])
                lv = self.LD[g]
                for b in range(16):
                    B_ = 15 - b
                    src = bass.AP(lv.tensor, lv.offset + 1, [[1, 8], [1024, 4], [1, 8]])
                    dstv = t[8 * B_:8 * B_ + 8, :].rearrange("p (h q) -> p h q", h=4)[:, :, 8 * b:8 * b + 8]
                    P.op("sp", lambda e, dstv=dstv, src=src: e.dma_start(out=dstv, in_=src), reads=[("LD", g)], writes=[tk], is_dma=True, multi=True)
                finish(t, tk, self.tabn[:, g, :], ("tabn", g), 512)
            t, tk = tp[ti % 2], K("tp%d" % (ti % 2))
            ti += 1
            P.op("pool", lambda e, t=t: e.memset(t[:], 0.0), writes=[tk])
            tv = t[:, 0:416].rearrange("p (h q) -> p h q", h=4)
            lv = self.LU[0]
            src = bass.AP(lv.tensor, lv.offset + 129, [[1, 128], [640, 4], [1, 8]])
            P.op("sp", lambda e, src=src: e.dma_start(out=tv[:, :, 0:8], in_=src), reads=[("LU", 0)], writes=[tk], is_dma=True, multi=True)
            lv = self.LD[1]
            for a in range(4):
                src = bass.AP(lv.tensor, lv.offset + 8 + 385 - 128 * a, [[1, 128], [1024, 4], [1, 8]])
                P.op("sp", lambda e, src=src, a=a: e.dma_start(out=tv[:, :, 8 + 8 * a:16 + 8 * a], in_=src), reads=[("LD", 1)], writes=[tk], is_dma=True, multi=True)
            lv = self.LU[2]
            for s in range(8):
                src = bass.AP(lv.tensor, lv.offset + 129, [[1, 128], [640, 4], [1, 1]])
                P.op("sp", lambda e, src=src, s=s: e.dma_start(out=tv[:, :, 40 + 9 * s:41 + 9 * s], in_=src, allow_slow_non_contiguous=True), reads=[("LU", 2)], writes=[tk], is_dma=True, multi=True)
            finish(t, tk, self.tabc[:, 0:416], "tabc", 416)

    def odd_mixer(self, kind, g, T):
        if "stubodd" in DBG:
            for n in ["O0", "O1", "O2", "O4", "O3", "O5", "O6", "OO0", "OO1"]:
                self.next_w(n)
            return
        P, dr = self.P, self.dram
        G = g
        nt = T // 128
        isP = kind == "P"
        self.prenorm(1, T)
        hT = self.hT

        def tok(base, off, dims):
            return bass.AP(base.tensor, base.offset + off, [list(base.ap[0])] + [list(d) for d in dims])

        with self.phase("od_%s%d" % (kind, g)) as ph:
            K = ph.k
            qT = ph.buf("qT", [128, 6, T], BF16)
            catO = ph.buf("catO", [128, 4, T], BF16)
            uT = ph.buf("uT", [128, 2, T])
            esb = [ph.buf("esb%d" % i, [128, 512]) for i in range(2)]
            pT = [ph.buf("pT%d" % i, [128, 4, 128], BF16) for i in range(2)]
            kvs = [ph.buf("kvs%d" % i, [128, 512]) for i in range(2)]
            if not isP:
                kTn = ph.buf("kTn", [128, 6, 128], BF16)
                vnew = ph.buf("vnew", [128, 3, 256], BF16)
            for bn, c0, ncn in (("O0", 0, 4), ("O1", 4, 2)):
                w, wk = self.next_w(bn)
                for ci in range(ncn):
                    cg = c0 + ci
                    ps, pk = self.psum()
                    for k in range(8):
                        P.op("pe", lambda e, ps=ps, k=k, ci=ci, w=w: e.matmul(ps[:, 0:T], lhsT=w[:, k, ci * 128:(ci + 1) * 128], rhs=hT[:, k, 0:T], start=(k == 0), stop=(k == 7)),
                             reads=[wk, ("hT", k)], writes=[pk])
                    P.op("act", lambda e, ps=ps, cg=cg: e.activation(out=qT[:, cg, :], in_=ps[:, 0:T], func=AF.Copy, scale=0.125), reads=[pk], writes=[K("qT", cg)])
            if ODDSTOP <= 1:
                for n in ["O2", "O4", "O3", "O5", "O6", "OO0", "OO1"]:
                    self.next_w(n)
                return
            def kproj(w, wk, c0, ncn):
                for ci in range(ncn):
                    cg = c0 + ci
                    gg, hp = cg // 2, cg % 2
                    ps, pk = self.psum()
                    for k in range(8):
                        P.op("pe", lambda e, ps=ps, k=k, ci=ci, w=w: e.matmul(ps[:, 0:T], lhsT=w[:, k, ci * 128:(ci + 1) * 128], rhs=hT[:, k, 0:T], start=(k == 0), stop=(k == 7)),
                             reads=[wk, ("hT", k)], writes=[pk])
                    if not isP:
                        dst, dk = kTn[:, cg, :], K("kTn", cg)
                    elif gg == 0:
                        dst, dk = self.kT0[:, hp, 128:640], ("kT0", hp)
                    elif gg == 1:
                        dst, dk = self.kT1[:, hp, G % 2, :], ("kT1", hp, G % 2)
                    else:
                        dst, dk = self.kT2[:, hp, 512 * G:512 * G + 512], ("kT2", hp)
                    P.op("act", lambda e, ps=ps, dst=dst: e.copy(out=dst, in_=ps[:, 0:T]), reads=[pk], writes=[dk])

            self.kvi = 0

            def kv_tile(wkv, wkk, kcol, wvv, wvk, vcol, lhs_cols, need_k, vdst, vkey, out_ap):
                ps, pk = self.psum()
                if need_k:
                    for k in range(8):
                        P.op("pe", lambda e, ps=ps, k=k: e.matmul(ps[:, 0:256], lhsT=lhs_cols(k), rhs=wkv[:, k, kcol:kcol + 256], start=(k == 0), stop=(k == 7)),
                             reads=[wkk, ("hT", k)], writes=[pk])
                for k in range(8):
                    P.op("pe", lambda e, ps=ps, k=k: e.matmul(ps[:, 256:512], lhsT=lhs_cols(k), rhs=wvv[:, k, vcol:vcol + 256], start=(k == 0), stop=(k == 7)),
                         reads=[wvk, ("hT", k)], writes=[pk])
                if not need_k:
                    P.op("act", lambda e, ps=ps: e.copy(out=vdst, in_=ps[:, 256:512]), reads=[pk], writes=[vkey])
                else:
                    st_ = kvs[self.kvi % 2]
                    sk = K("kvs%d" % (self.kvi % 2))
                    self.kvi += 1
                    P.op("dve", lambda e, ps=ps, st_=st_: e.tensor_copy(out=st_[:], in_=ps[:, 0:512]), reads=[pk], writes=[sk])
                    P.op("act", lambda e, st_=st_: e.copy(out=vdst, in_=st_[:, 256:512]), reads=[sk], writes=[vkey])
                    for oa in ([] if "nokvout" in DBG else out_ap):
                        o_ = P.op("sp", lambda e, oa=oa, st_=st_: e.dma_start(out=oa[0], in_=oa[1](st_)), reads=[sk], is_dma=True)
                        P.final_waits.append(o_.idx)

            def sample_outs(gg):
                W_ = (128, 512, 2048)[gg]
                pv = dr[("s_kv128", "s_kv512", "s_kv2048")[gg]]
                return [(pv[i, W_ - 8:W_, :], (lambda s, i=i: s[8 * i:8 * i + 8, :])) for i in range(16)]

            w2, wk2 = self.next_w("O2")
            kproj(w2, wk2, 0, 4)
            w4, wk4 = self.next_w("O4")
            if isP:
                for a in range(4):
                    need = (G == 3 and a == 3)
                    outs = [(dr["p_kv128"], lambda s: s[:])] if need else []
                    kv_tile(w2, wk2, 0, w4, wk4, 0, lambda k, a=a: hT[:, k, a * 128:(a + 1) * 128], need, self.v0[:, a + 1, :], ("v0", a + 1), outs)
                for r in range(4):
                    need = (G == 3)
                    pv = dr["p_kv512"]
                    oa = bass.AP(pv.tensor, pv.offset + r * 512, [[4 * 512, 128], [1, 512]])
                    outs = [(oa, lambda s: s[:])] if need else []
                    kv_tile(w2, wk2, 256, w4, wk4, 256, (lambda k, r=r: hT[:, k, r * 128:(r + 1) * 128]) if "nostride" in DBG else (lambda k, r=r: tok(hT[:, k, :], r, [[4, 128]])), need, self.v1[:, G % 2, r, :], ("v1", G % 2, r), outs)
            else:
                for gg in range(2):
                    kv_tile(w2, wk2, 256 * gg, w4, wk4, 256 * gg, lambda k: hT[:, k, 0:128], True, vnew[:, gg, :], K("vnew", gg), sample_outs(gg))
            if ODDSTOP <= 2:
                for n in ["O3", "O5", "O6", "OO0", "OO1"]:
                    self.next_w(n)
                return
            w3, wk3 = self.next_w("O3")
            kproj(w3, wk3, 4, 2)
            w5, wk5 = self.next_w("O5")
            if isP:
                for rq in range(4):
                    pv = dr["p_kv2048"]
                    oa = bass.AP(pv.tensor, pv.offset + (512 * G + rq) * 512, [[4 * 512, 128], [1, 512]])
                    kv_tile(w3, wk3, 0, w5, wk5, 0, lambda k, rq=rq: tok(hT[:, k, :], rq, [[4, 128]]), True, self.v2[:, 4 * G + rq, :], ("v2", 4 * G + rq), [(oa, lambda s: s[:])])
            else:
                kv_tile(w3, wk3, 0, w5, wk5, 0, lambda k: hT[:, k, 0:128], True, vnew[:, 2, :], K("vnew", 2), sample_outs(2))
            if ODDSTOP <= 3:
                for n in ["O6", "OO0", "OO1"]:
                    self.next_w(n)
                return
            w6, wk6 = self.next_w("O6")
            for c in range(2):
                ps, pk = self.psum()
                for k in range(8):
                    P.op("pe", lambda e, ps=ps, k=k, c=c: e.matmul(ps[:, 0:T], lhsT=w6[:, k, c * 128:(c + 1) * 128], rhs=hT[:, k, 0:T], start=(k == 0), stop=(k == 7)),
                         reads=[wk6, ("hT", k)], writes=[pk])
                P.op("act", lambda e, ps=ps, c=c: e.activation(out=uT[:, c, :], in_=ps[:, 0:T], func=AF.Gelu_apprx_tanh), reads=[pk], writes=[K("uT", c)])
            with self.phase("odG_%s%d" % (kind, g)) as pg:
                KG = pg.k
                gv = pg.buf("gv", [128, 256])
                st6 = pg.buf("st6", [128, 6])
                mv = pg.buf("mv", [128, 2])
                rs_ = pg.buf("rs_", [128, 1])
                vn = pg.buf("vn", [128, 256])
                vnb = pg.buf("vnb", [128, 256], BF16)
                tmpm = pg.buf("tmpm", [128, 256])
                wsT = self.wsT if isP else self.wsTs
                sgub = self.sgub if isP else self.sgubs
                for t in range(nt):
                    ps, pk = self.psum()
                    for k in range(8):
                        P.op("pe", lambda e, ps=ps, k=k, t=t: e.matmul(ps[:, 0:256], lhsT=hT[:, k, t * 128:(t + 1) * 128], rhs=w6[:, k, 256:512], start=(k == 0), stop=(k == 7)),
                             reads=[wk6, ("hT", k)], writes=[pk])
                    P.op("act", lambda e, ps=ps: e.activation(out=gv[:], in_=ps[:, 0:256], func=AF.Gelu_apprx_tanh), reads=[pk], writes=[KG("gv")])
                    P.op("dve", lambda e: e.bn_stats(out=st6[:], in_=gv[:]), reads=[KG("gv")], writes=[KG("st6")])
                    P.op("dve", lambda e: e.bn_aggr(out=mv[:], in_=st6[:]), reads=[KG("st6")], writes=[KG("mv")])
                    P.op("act", lambda e: e.activation(out=rs_[:], in_=mv[:, 1:2], func=AF.Sqrt, bias=self.eps_t[:], scale=1.0), reads=[KG("mv"), "eps"], writes=[KG("rs_")])
                    P.op("dve", lambda e: e.reciprocal(out=rs_[:], in_=rs_[:]), reads=[KG("rs_")], writes=[KG("rs_")])
                    P.op("dve", lambda e: e.tensor_scalar(out=vn[:], in0=gv[:], scalar1=mv[:, 0:1], scalar2=rs_[:, 0:1], op0=ALU.subtract, op1=ALU.mult),
                         reads=[KG("gv"), KG("mv"), KG("rs_")], writes=[KG("vn")])
                    P.op("dve", lambda e: e.tensor_tensor(out=vn[:], in0=vn[:], in1=self.lngb[:, 0, :], op=ALU.mult), reads=[KG("vn"), "lngb"], writes=[KG("vn")])
                    P.op("dve", lambda e: e.tensor_tensor(out=vn[:], in0=vn[:], in1=self.lngb[:, 1, :], op=ALU.add), reads=[KG("vn"), "lngb"], writes=[KG("vn")])
                    P.op("act", lambda e: e.copy(out=vnb[:], in_=vn[:]), reads=[KG("vn")], writes=[KG("vnb")])
                    if not isP:
                        o_ = P.op("sp", lambda e: e.dma_start(out=dr["s_sgu_v"], in_=vn[:]), reads=[KG("vn")], is_dma=True)
                        P.final_waits.append(o_.idx)
                    psm, pkm = self.psum()
                    for gq in range(4):
                        c, po = gq // 2, (gq % 2) * 64
                        P.op("pe", lambda e, psm=psm, gq=gq, c=c, po=po: e.matmul(psm[po:po + 64, c * 128:(c + 1) * 128], lhsT=vnb[:, gq * 64:(gq + 1) * 64], rhs=wsT[:, gq, :],
                                                                                 start=True, stop=True),
                             reads=[KG("vnb"), "wsT"], writes=[pkm])
                    P.op("dve", lambda e, psm=psm: e.tensor_tensor(out=tmpm[:], in0=psm[:, 0:256], in1=sgub[:].rearrange("p c t -> p (c t)"), op=ALU.add), reads=[pkm, "sgub"], writes=[KG("tmpm")])
                    P.op("dve", lambda e, t=t: e.tensor_tensor(out=catO[:, 2:4, t * 128:(t + 1) * 128], in0=tmpm[:].rearrange("p (c t) -> p c t", c=2), in1=uT[:, :, t * 128:(t + 1) * 128], op=ALU.mult),
                         reads=[KG("tmpm"), K("uT", 0), K("uT", 1)], writes=[K("catO", 2), K("catO", 3)])
            if ODDSTOP <= 4:
                for n in ["OO0", "OO1"]:
                    self.next_w(n)
                return
            acc = ph.buf("acc", [128, 4, T])
            P.op("pool", lambda e: e.memset(acc[:], 0.0), writes=[K("acc")])
            o_ps = d_ps = okeys = dkeys = None
            self.pti = 0

            pend_pv = [None]

            def att_pair(kT_fn, q_cg, q_pat, v_fn, vkey, kkeys, tab, tabkey, ncols=128, col0=0, out_pat=None):
                i = self.pti
                self.pti += 1
                es, ek = esb[i % 2], K("esb%d" % (i % 2))
                pt, ptk = pT[i % 2], K("pT%d" % (i % 2))
                pss = [self.psum(), self.psum()]
                for h in range(4):
                    hp, po = h // 2, (h % 2) * 64
                    ps, pk = pss[h % 2]
                    rhs = tok(qT[po:po + 64, q_cg * 2 + hp, :], q_pat[0], q_pat[1])
                    P.op("pe", lambda e, ps=ps, hp=hp, rhs=rhs, l=kT_fn(h): e.matmul(ps[:, hp * 128:hp * 128 + ncols], lhsT=l, rhs=rhs, start=True, stop=True, skip_group_check=True),
                         reads=kkeys + [K("qT", q_cg * 2 + hp)], writes=[pk])
                if pend_pv[0] is not None:
                    pend_pv[0]()
                    pend_pv[0] = None
                for par in range(2):
                    ps, pk = pss[par]
                    esv = es[:].rearrange("p (hp par q) -> p hp par q", hp=2, par=2)[:, :, par, :]
                    P.op("act", lambda e, ps=ps, esv=esv: e.activation(out=esv, in_=ps[:, 0:256].rearrange("p (hp q) -> p hp q", hp=2), func=AF.Exp), reads=[pk], writes=[ek])
                P.op("dve", lambda e, es=es, pt=pt, tab=tab: e.tensor_tensor(out=pt[:].rearrange("p h q -> p (h q)"), in0=es[:], in1=tab, op=ALU.mult), reads=[ek, tabkey], writes=[ptk])
                op_ = out_pat if out_pat is not None else q_pat
                pend_pv[0] = lambda: pv_part(pt, ptk, v_fn, vkey, ncols, op_)

            def flush_pv():
                if pend_pv[0] is not None:
                    pend_pv[0]()
                    pend_pv[0] = None

            def pv_part(pt, ptk, v_fn, vkey, ncols, op_):
                pv, pvk = self.psum()
                for h in range(4):
                    hp, po = h // 2, (h % 2) * 64
                    P.op("pe", lambda e, pv=pv, h=h, hp=hp, po=po, pt=pt, l=v_fn(h): e.matmul(pv[po:po + 64, hp * 128:hp * 128 + ncols], lhsT=l, rhs=pt[:, h, 0:ncols], start=True, stop=True, skip_group_check=True),
                         reads=[vkey, ptk], writes=[pvk])
                    P.op("pe", lambda e, pv=pv, h=h, hp=hp, po=po, pt=pt: e.matmul(pv[po:po + 64, (2 + hp) * 128:(2 + hp) * 128 + ncols], lhsT=self.ones_bf[:, 0:64], rhs=pt[:, h, 0:ncols], start=True, stop=True, skip_group_check=True),
                         reads=["ones_bf", ptk], writes=[pvk])
                av = acc[:, :, :]
                accv = bass.AP(av.tensor, av.offset + op_[0], [list(av.ap[0]), [T, 4]] + [list(d) for d in op_[1]])
                P.op("dve", lambda e, pv=pv, accv=accv: e.tensor_tensor(out=accv, in0=pv[:, 0:512].rearrange("p (j q) -> p j q", j=4), in1=accv, op=ALU.add),
                     reads=[pvk, K("acc")], writes=[K("acc")])

            if "noatt" in DBG:
                P.op("pool", lambda e: e.memset(acc[:], 1.0), writes=[K("acc")])
            elif isP:
                for a in range(4):
                    for wv, ap in ((0, a), (1, a - 1)):
                        if G == 0 and ap < 0:
                            continue
                        att_pair(lambda h, ap=ap: self.kT0[(h % 2) * 64:(h % 2) * 64 + 64, h // 2, 128 + 128 * ap:256 + 128 * ap], 0, (128 * a, [[1, 128]]),
                                 lambda h, ap=ap: self.v0[:, ap + 1, h * 64:(h + 1) * 64], ("v0", ap + 1), [("kT0", 0), ("kT0", 1)], self.tabs[:, wv, :], ("tabs", wv))
                for r in range(4):
                    for wv, Gp in ((0, G), (1, G - 1)):
                        if Gp < 0:
                            continue
                        att_pair(lambda h, r=r, Gp=Gp: tok(self.kT1[(h % 2) * 64:(h % 2) * 64 + 64, h // 2, Gp % 2, :], r, [[4, 128]]), 1, (r, [[4, 128]]),
                                 lambda h, r=r, Gp=Gp: self.v1[:, Gp % 2, r, h * 64:(h + 1) * 64], ("v1", Gp % 2, r), [("kT1", 0, Gp % 2), ("kT1", 1, Gp % 2)],
                                 self.tabs[:, 2 + wv, :], ("tabs", 2 + wv))
                for rq in range(4):
                    for Gp in range(G + 1):
                        att_pair(lambda h, rq=rq, Gp=Gp: tok(self.kT2[(h % 2) * 64:(h % 2) * 64 + 64, h // 2, :], 512 * Gp + rq, [[4, 128]]), 2,
                                 (rq, [[4, 128]]), lambda h, rq=rq, Gp=Gp: self.v2[:, 4 * Gp + rq, h * 64:(h + 1) * 64], ("v2", 4 * Gp + rq),
                                 [("kT2", 0), ("kT2", 1)], self.tabs[:, 4 + G - Gp, :], ("tabs", 4 + G - Gp))
                flush_pv()
                for hp in range(2):
                    P.op("pool", lambda e, hp=hp: e.tensor_copy(out=self.kT0[:, hp, 0:128], in_=self.kT0[:, hp, 512:640]), reads=[("kT0", hp)], writes=[("kT0", hp)])
                P.op("pool", lambda e: e.tensor_copy(out=self.v0[:, 0, :], in_=self.v0[:, 4, :]), reads=[("v0", 4)], writes=[("v0", 0)])
            else:
                for gg in range(3):
                    att_pair(lambda h, gg=gg: kTn[(h % 2) * 64:(h % 2) * 64 + 64, gg * 2 + h // 2, :], gg, (0, [[1, 128]]),
                             lambda h, gg=gg: vnew[:, gg, h * 64:(h + 1) * 64], K("vnew", gg), [K("kTn", gg * 2), K("kTn", gg * 2 + 1)], self.tabn[:, gg, :], ("tabn", gg))
                flush_pv()
                if "nocache" not in DBG:
                    self.sample_cache_attention(ph, qT, acc, esb, pT)
            with self.phase("odN_%s%d" % (kind, g)) as pn:
                rd = pn.buf("rd", [128, 2, T])
                P.op("dve", lambda e: e.reciprocal(out=rd[:], in_=acc[:, 2:4, :]), reads=[K("acc")], writes=[pn.k("rd")])
                P.op("dve", lambda e: e.tensor_tensor(out=catO[:, 0:2, :], in0=acc[:, 0:2, :], in1=rd[:], op=ALU.mult), reads=[K("acc"), pn.k("rd")], writes=[K("catO", 0), K("catO", 1)])
            for half in range(2):
                w, wk = self.next_w("OO%d" % half)
                for oc in range(4):
                    o = half * 4 + oc
                    ps, pk = self.psum()
                    for k in range(4):
                        P.op("pe", lambda e, ps=ps, k=k, oc=oc, w=w: e.matmul(ps[:, 0:T], lhsT=w[:, k, oc * 128:(oc + 1) * 128], rhs=catO[:, k, :], start=(k == 0), stop=(k == 3)),
                             reads=[wk, K("catO", k)], writes=[pk])
                    P.op("act", lambda e, ps=ps, o=o: e.copy(out=self.yT[:, o, 0:T], in_=ps[:, 0:T]), reads=[pk], writes=["yT"])
        self.postnorm_residual(3, T)

    def sample_cache_attention(self, ph, qT, acc, esb, pT):
        P, dr = self.P, self.dram
        K = ph.k
        with self.phase("odS") as pc:
            KC = pc.k
            stg = [pc.buf("cst%d" % i, [128, 13, 512]) for i in range(1)]
            kTc = [pc.buf("kTc%d" % i, [128, 26, 128], BF16) for i in range(2)]
            vc = [pc.buf("vc%d" % i, [128, 13, 256], BF16) for i in range(1)]
            psum_ = pc.buf("ptsum", [128, 32])
            for b in range(16):
                s_, sk = stg[0], KC("cst0")
                kt_, kk = kTc[b % 2], KC("kTc%d" % (b % 2))
                v_, vk = vc[0], KC("vc0")
                P.op("sp", lambda e, s_=s_, b=b: e.dma_start(out=s_[:, 0, :], in_=dr["c128"][b]), writes=[sk], is_dma=True, multi=True)
                P.op("sp", lambda e, s_=s_, b=b: e.dma_start(out=s_[:, 1:5, :], in_=dr["c512"][b].rearrange("(a p) c -> p a c", p=128)), writes=[sk], is_dma=True, multi=True)
                cv = dr["c2048"]
                src = bass.AP(cv.tensor, cv.offset + b * 2048 * 512, [[16 * 512, 128], [512, 8], [1, 512]])
                P.op("sp", lambda e, s_=s_, src=src: e.dma_start(out=s_[:, 5:13, :], in_=src), writes=[sk], is_dma=True, multi=True)
                P.op("pool", lambda e, s_=s_, v_=v_: e.tensor_copy(out=v_[:], in_=s_[:, :, 256:512]), reads=[sk], writes=[vk])
                for j0 in range(0, 26, 4):
                    n = min(4, 26 - j0)
                    ps, pk = self.psum()
                    for j in range(n):
                        tl, hp = (j0 + j) // 2, (j0 + j) % 2
                        P.op("pe", lambda e, ps=ps, j=j, tl=tl, hp=hp, s_=s_: e.transpose(out=ps[:, j * 128:(j + 1) * 128], in_=s_[:, tl, hp * 128:(hp + 1) * 128], identity=self.ident[:]),
                             reads=[sk, "ident"], writes=[pk])
                    eng = "act" if (j0 // 4) % 2 == 0 else "dve"
                    if eng == "act":
                        P.op("act", lambda e, ps=ps, j0=j0, n=n, kt_=kt_: e.copy(out=kt_[:, j0:j0 + n, :], in_=ps[:, 0:n * 128].rearrange("p (j k) -> p j k", k=128)), reads=[pk], writes=[kk])
                    else:
                        P.op("dve", lambda e, ps=ps, j0=j0, n=n, kt_=kt_: e.tensor_copy(out=kt_[:, j0:j0 + n, :], in_=ps[:, 0:n * 128].rearrange("p (j k) -> p j k", k=128)), reads=[pk], writes=[kk])
                i = self.pti
                self.pti += 1
                es, ek = esb[i % 2], K("esb%d" % (i % 2))
                pt, ptk = pT[i % 2], K("pT%d" % (i % 2))
                ptv = pt[:].rearrange("p h q -> p (h q)")[:, 0:416].rearrange("p (h q) -> p h q", h=4)
                pss = [self.psum(), self.psum()]
                for h in range(4):
                    hp, po = h // 2, (h % 2) * 64
                    ps, pk = pss[h % 2]
                    for tl in range(13):
                        gg = 0 if tl == 0 else (1 if tl < 5 else 2)
                        P.op("pe", lambda e, ps=ps, h=h, hp=hp, po=po, tl=tl, gg=gg, kt_=kt_, b=b: e.matmul(ps[:, hp * 104 + 8 * tl:hp * 104 + 8 * tl + 8], lhsT=kt_[po:po + 64, tl * 2 + hp, :],
                                                                                                         rhs=qT[po:po + 64, gg * 2 + hp, 8 * b:8 * b + 8], start=True, stop=True, skip_group_check=True),
                             reads=[kk, K("qT", gg * 2 + hp)], writes=[pk])
                for par in range(2):
                    ps, pk = pss[par]
                    esv = es[:, 0:416].rearrange("p (hp par q) -> p hp par q", hp=2, par=2)[:, :, par, :]
                    P.op("act", lambda e, ps=ps, esv=esv: e.activation(out=esv, in_=ps[:, 0:208].rearrange("p (hp q) -> p hp q", hp=2), func=AF.Exp), reads=[pk], writes=[ek])
                P.op("dve", lambda e, es=es, pt=pt: e.tensor_tensor(out=pt[:].rearrange("p h q -> p (h q)")[:, 0:416], in0=es[:, 0:416], in1=self.tabc[:, 0:416], op=ALU.mult),
                     reads=[ek, "tabc"], writes=[ptk])
                P.op("dve", lambda e, ptv=ptv: e.tensor_reduce(out=psum_[:].rearrange("p (h i) -> p h i", h=4), in_=ptv.rearrange("p h (t i) -> p h i t", i=8), axis=AX.X, op=ALU.add),
                     reads=[ptk], writes=[KC("ptsum")])
                psb, pbk = self.psum()
                for h in range(4):
                    hp, po = h // 2, (h % 2) * 64
                    for tl in range(13):
                        P.op("pe", lambda e, psb=psb, h=h, hp=hp, po=po, tl=tl, v_=v_, ptv=ptv: e.matmul(psb[po:po + 64, 8 * hp:8 * hp + 8], lhsT=v_[:, tl, h * 64:(h + 1) * 64],
                                                                                                       rhs=ptv[:, h, 8 * tl:8 * tl + 8], start=(tl == 0), stop=(tl == 12), skip_group_check=True),
                             reads=[vk, ptk], writes=[pbk])
                    P.op("pe", lambda e, psb=psb, h=h, hp=hp, po=po: e.matmul(psb[po:po + 64, 16 + 8 * hp:24 + 8 * hp], lhsT=self.ones_f[:, 0:64], rhs=psum_[:, 8 * h:8 * h + 8],
                                                                              start=True, stop=True, skip_group_check=True),
                         reads=["ones_f", KC("ptsum")], writes=[pbk])
                P.op("dve", lambda e, psb=psb, b=b: e.tensor_tensor(out=acc[:, :, 8 * b:8 * b + 8], in0=psb[:, 0:32].rearrange("p (j i) -> p j i", j=4), in1=acc[:, :, 8 * b:8 * b + 8], op=ALU.add),
                     reads=[pbk, K("acc")], writes=[K("acc")])

    def load_odd_params(self):
        P, dr = self.P, self.dram
        P.op("sp", lambda e: e.dma_start(out=self.Jrev[:], in_=dr["c_jrev"]), writes=["Jrev"], is_dma=True)
        for r, nm in enumerate(["sgu_ln_g", "sgu_ln_b"]):
            v = dr[nm]
            src = bass.AP(v.tensor, v.offset, [[0, 128], [1, 256]])
            P.op("sp", lambda e, src=src, r=r: e.dma_start(out=self.lngb[:, r, :], in_=src), writes=["lngb"], is_dma=True, multi=True)
        v = dr["sgu_b"]
        for gq in range(4):
            c, po = gq // 2, (gq % 2) * 64
            src = bass.AP(v.tensor, v.offset + gq * 128, [[0, 64], [1, 128]])
            P.op("sp", lambda e, src=src, c=c, po=po: e.dma_start(out=self.sgub[po:po + 64, c, :], in_=src), writes=["sgub"], is_dma=True, multi=True)
            src = bass.AP(v.tensor, v.offset + gq * 128, [[0, 64], [0, 16], [1, 8]])
            P.op("sp", lambda e, src=src, c=c, po=po: e.dma_start(out=self.sgubs[po:po + 64, c, :].rearrange("p (b t) -> p b t", t=8), in_=src), writes=["sgub"], is_dma=True, multi=True)
        with self.phase("sguw") as ph:
            K = ph.k
            wst = ph.buf("wst", [128, 4, 128])
            wss = ph.buf("wss", [128, 4, 128])
            tf = ph.buf("tf", [128, 4, 128])
            P.op("sp", lambda e: e.dma_start(out=wst[:], in_=dr["sgu_w"][0].rearrange("g t s -> t g s")), writes=[K("wst")], is_dma=True)
            ps, pk = self.psum()
            for gq in range(4):
                P.op("pe", lambda e, ps=ps, gq=gq: e.transpose(out=ps[:, gq * 128:(gq + 1) * 128], in_=wst[:, gq, :], identity=self.ident[:]), reads=[K("wst"), "ident"], writes=[pk])
            cm = self.causal_st[:]
            cmb = bass.AP(cm.tensor, cm.offset, [[cm.ap[0][0], 128], [0, 4], [1, 128]])
            P.op("dve", lambda e, ps=ps, cmb=cmb: e.tensor_tensor(out=self.wsT[:], in0=ps[:, 0:512].rearrange("p (g t) -> p g t", g=4), in1=cmb, op=ALU.mult), reads=[pk, "masks"], writes=["wsT"])
            P.op("pool", lambda e: e.memset(wss[:], 0.0), writes=[K("wss")])
            v = dr["sgu_w"]
            for b in range(16):
                for gq in range(4):
                    src = bass.AP(v.tensor, v.offset + gq * 128 * 128, [[1, 8], [128, 8]])
                    P.op("sp", lambda e, src=src, b=b, gq=gq: e.dma_start(out=wss[8 * b:8 * b + 8, gq, 8 * b:8 * b + 8], in_=src, allow_slow_non_contiguous=True), writes=[K("wss")], is_dma=True, multi=True)
            cm = self.causal_s8[:]
            cmb2 = bass.AP(cm.tensor, cm.offset, [[cm.ap[0][0], 128], [0, 4], [1, 128]])
            P.op("dve", lambda e, cmb2=cmb2: e.tensor_tensor(out=self.wsTs[:], in0=wss[:], in1=cmb2, op=ALU.mult), reads=[K("wss"), "masks"], writes=["wsT"])

    def ffn(self, kind, g, T, l):
        P, dr = self.P, self.dram
        self.prenorm(4 + l, T)
        with self.phase("ffn%d_%s%d" % (l, kind, g)) as ph:
            act = ph.buf("act", [128, 22, T], BF16)
            if kind == "P":
                W = 2 + T
                up = [ph.buf("up%d" % i, [128, 2, W]) for i in range(2)]
            else:
                up = [ph.buf("up%d" % i, [128, 2, 16, 10]) for i in range(2)]
                hs = ph.buf("hs", [128, 44, 32])
                ho = ph.buf("ho", [128, 44, 16, 2])
                stg = ph.buf("stg", [32, 5632])
                P.op("sp", lambda e: e.dma_start(out=stg[:], in_=dr["sffn"][l].rearrange("i r c -> (i r) c")), writes=[ph.k("stg")], is_dma=True)
                for c0 in range(0, 44, 16):
                    n = min(16, 44 - c0)
                    ps, pk = self.psum()
                    for i in range(n):
                        c = c0 + i
                        P.op("pe", lambda e, ps=ps, i=i, c=c: e.transpose(out=ps[:, i * 32:(i + 1) * 32], in_=stg[0:32, c * 128:(c + 1) * 128],
                                                                            identity=self.ident[0:32, 0:32]),
                             reads=[ph.k("stg"), "ident"], writes=[pk])
                    P.op("dve", lambda e, ps=ps, c0=c0, n=n: e.tensor_copy(out=hs[:, c0:c0 + n, :], in_=ps[:, 0:n * 32].rearrange("p (c r) -> p c r", r=32)),
                         reads=[pk], writes=[ph.k("hs")])
            c0ts = [ph.buf("c0t%d" % i, [128, 2, T]) for i in range(2)]
            gels = [ph.buf("gel%d" % i, [128, T]) for i in range(2)]
            pending_tail = [None]
            for b in range(11):
                w, wk = self.next_w("U%d_%d" % (l, b))
                for jj in range(2):
                    j = 2 * b + jj
                    u = up[j % 2]
                    uk = ph.k("up%d" % (j % 2))
                    c0t, gel = c0ts[j % 2], gels[j % 2]
                    ck, gk = "c0t%d" % (j % 2), "gel%d" % (j % 2)
                    pss = []
                    for gv in range(2):
                        ps, pk = self.psum()
                        pss.append((ps, pk))
                        for k in range(8):
                            P.op("pe", lambda e, ps=ps, k=k, gv=gv, jj=jj, w=w: e.matmul(ps[:, 0:T], lhsT=w[:, k, gv * 256 + jj * 128: gv * 256 + (jj + 1) * 128],
                                                                                       rhs=self.hT[:, k, 0:T], start=(k == 0), stop=(k == 7)),
                                 reads=[wk] + [("hT", k)], writes=[pk])
                    uks = [ph.k("up%d" % (j % 2), 0), ph.k("up%d" % (j % 2), 1)]
                    if kind == "P":
                        P.op("pool", lambda e, u=u, j=j: e.tensor_copy(out=u[:, :, 0:2], in_=self.hal[l][:, j, :, :]), reads=["hal%d" % l], writes=uks)
                    else:
                        for gv in range(2):
                            P.op("pool", lambda e, u=u, j=j, gv=gv: e.tensor_copy(out=u[:, gv, :, 0:2],
                                                                                in_=hs[:, gv * 22 + j, :].rearrange("p (i r) -> p i r", r=2)),
                                 reads=[ph.k("hs")], writes=[uks[gv]])
                    prow = 4 * l
                    views = []
                    for gv in range(2):
                        ps, pk = pss[gv]
                        if kind == "P":
                            views.append((u[:, gv, 2:2 + T], ps[:, 0:T], c0t[:, gv, :], u[:, gv, 1:1 + T], u[:, gv, 0:T]))
                        else:
                            views.append((u[:, gv, :, 2:10], ps[:, 0:T].rearrange("p (i t) -> p i t", t=8), c0t[:, gv, :].rearrange("p (i t) -> p i t", t=8),
                                          u[:, gv, :, 1:9], u[:, gv, :, 0:8]))
                    for gv in range(2):
                        raw_dst, psv, c_dst, sh1, sh2 = views[gv]
                        pk = pss[gv][1]
                        ch = gv * 22 + j
                        P.op("act", lambda e, raw_dst=raw_dst, psv=psv: e.copy(out=raw_dst, in_=psv), reads=[pk], writes=[uks[gv]])
                        P.op("act", lambda e, c_dst=c_dst, psv=psv, ch=ch: e.activation(out=c_dst, in_=psv, func=AF.Identity,
                                                                                       bias=self.ffnp[:, ch, prow + 3:prow + 4], scale=self.ffnp[:, ch, prow + 2:prow + 3]),
                             reads=[pk, "ffnp"], writes=[ph.k(ck, gv)])
                    for tap in (1, 0):
                        for gv in range(2):
                            raw_dst, psv, c_dst, sh1, sh2 = views[gv]
                            ch = gv * 22 + j
                            sh = sh1 if tap == 1 else sh2
                            P.op("dve", lambda e, c_dst=c_dst, sh=sh, ch=ch, tap=tap: e.scalar_tensor_tensor(out=c_dst, in0=sh, scalar=self.ffnp[:, ch, prow + tap:prow + tap + 1],
                                                                                                           in1=c_dst, op0=ALU.mult, op1=ALU.add),
                                 reads=[uks[gv], "ffnp", ph.k(ck, gv)], writes=[ph.k(ck, gv)])
                    if kind == "P":
                        P.op("pool", lambda e, u=u, j=j: e.tensor_copy(out=self.hal[l][:, j, :, :], in_=u[:, :, T:T + 2]), reads=uks, writes=["hal%d" % l])
                    else:
                        for gv in range(2):
                            P.op("pool", lambda e, u=u, j=j, gv=gv: e.tensor_copy(out=ho[:, gv * 22 + j, :, :], in_=u[:, gv, :, 8:10]),
                                 reads=[uks[gv]], writes=[ph.k("ho")])
                    def tail(j=j, gel=gel, c0t=c0t, ck=ck, gk=gk):
                        P.op("act", lambda e: e.activation(out=gel[:], in_=c0t[:, 0, :], func=AF.Gelu_apprx_tanh), reads=[ph.k(ck, 0)], writes=[ph.k(gk)])
                        P.op("dve", lambda e: e.tensor_tensor(out=act[:, j, :], in0=gel[:], in1=c0t[:, 1, :], op=ALU.mult),
                             reads=[ph.k(gk), ph.k(ck, 1)], writes=[ph.k("act", j)])
                    if pending_tail[0] is not None:
                        pending_tail[0]()
                    pending_tail[0] = tail
            pending_tail[0]()
            pending_tail[0] = None
            for o in range(8):
                w, wk = self.next_w("D%d_%d" % (l, o))
                ps, pk = self.psum()
                for j in range(22):
                    P.op("pe", lambda e, ps=ps, j=j, w=w: e.matmul(ps[:, 0:T], lhsT=w[:, j, :], rhs=act[:, j, :], start=(j == 0), stop=(j == 21)),
                         reads=[wk, ph.k("act", j)], writes=[pk])
                P.op("act", lambda e, ps=ps, o=o: e.copy(out=self.yT[:, o, 0:T], in_=ps[:, 0:T]), reads=[pk], writes=["yT"])
            if kind == "P" and g == 3:
                for r in range(2):
                    for gv in range(2):
                        dst = dr["p_ffn"][l, r, gv * DFF:(gv + 1) * DFF].rearrange("(j p) -> p j", p=128)
                        o_ = P.op("sp", lambda e, dst=dst, r=r, gv=gv: e.dma_start(out=dst, in_=self.hal[l][:, :, gv, r], allow_slow_non_contiguous=True),
                                  reads=["hal%d" % l], is_dma=True)
                        P.final_waits.append(o_.idx)
            if kind == "S":
                ost = ph.buf("ost", [32, 5632])
                for c0 in range(0, 44, 4):
                    ps, pk = self.psum()
                    for i in range(4):
                        c = c0 + i
                        P.op("pe", lambda e, ps=ps, i=i, c=c: e.transpose(out=ps[0:32, i * 128:(i + 1) * 128], in_=ho[:, c, :, :].rearrange("p i r -> p (i r)"),
                                                                            identity=self.ident[:]),
                             reads=[ph.k("ho"), "ident"], writes=[pk])
                    P.op("dve", lambda e, ps=ps, c0=c0: e.tensor_copy(out=ost[:, c0 * 128:(c0 + 4) * 128], in_=ps[0:32, 0:512]), reads=[pk], writes=[ph.k("ost")])
                o_ = P.op("sp", lambda e: e.dma_start(out=dr["s_ffn"][l].rearrange("i r c -> (i r) c"), in_=ost[:]), reads=[ph.k("ost")], is_dma=True)
                P.final_waits.append(o_.idx)
        self.postnorm_residual(6 + l, T)


class Phase:
    def __init__(self, kb, tag):
        self.kb, self.tag = kb, tag
        self.stack = contextlib.ExitStack()
        self.names = []
        if not hasattr(kb, "pending_alias"):
            kb.pending_alias = []
        self.inherit_obj = kb.pending_alias

    def k(self, name, *sub):
        n = self.tag + "." + name
        return (n,) + tuple(sub) if sub else n

    def buf(self, name, shape, dt=F32):
        n = self.tag + "." + name
        t = self.stack.enter_context(self.kb.nc.sbuf_tensor(n.replace(".", "_"), list(shape), dt))
        self.names.append(n)
        self.kb.P.alias[n] = list(self.kb.pending_alias)
        return t

    def close(self):
        dead = set(self.kb.P.retire(self.names))
        cur = self.kb.pending_alias
        if cur is not self.inherit_obj:
            dead |= set(cur)
        if not dead:
            dead = set(cur)
        self.kb.pending_alias = sorted(dead)
        self.stack.close()


_CACHE = {}


def _build(stages):
    if stages not in _CACHE:
        kb = KB(stages)
        nc = kb.build()
        _CACHE[stages] = (kb, nc)
    return _CACHE[stages]


def kernel(**inputs):
    stages = "all"
    kb, nc = _build(stages)
    consts = kb.consts
    f32 = lambda a: np.ascontiguousarray(np.asarray(a, dtype=np.float32))
    shared = {}
    for n in WEIGHT_SHAPES:
        shared[n] = f32(inputs[n])
    for n in SMALL_SHAPES:
        shared[n] = f32(inputs[n])
    for n, a in consts.items():
        shared["c_" + n] = a
    in_maps = []
    for c in range(NCORES):
        m = dict(shared)
        b0, b1 = 16 * c, 16 * c + 16
        m["xp"] = f32(inputs["x_prompt"][c])
        m["xs"] = f32(inputs["x_sample"][b0:b1]).reshape(128, 1024)
        m["sgla"] = f32(inputs["state_gla"][0, b0:b1])
        m["scb"] = f32(inputs["state_conv_b"][0, b0:b1])
        m["c128"] = f32(inputs["cache_c_w128"][0, b0:b1]).reshape(16, 128, 512)
        m["c512"] = f32(inputs["cache_c_w512"][0, b0:b1]).reshape(16, 512, 512)
        m["c2048"] = f32(inputs["cache_c_w2048"][0, b0:b1]).reshape(16, 2048, 512)
        m["sffn"] = f32(inputs["state_ffn_conv"][:, b0:b1])
        in_maps.append(m)
    res = run_bass_kernel_spmd(nc, in_maps, core_ids=list(range(NCORES)))
    R = res.results
    cat = lambda n, ax=0: np.concatenate([np.asarray(R[c][n]) for c in range(NCORES)], axis=ax)
    stk = lambda n: np.stack([np.asarray(R[c][n]) for c in range(NCORES)], axis=0)
    y_prompt = stk("y_p")
    y_sample = cat("y_s").reshape(128, 8, 1024)
    p_gla = stk("p_gla")[None]
    p_conv_b = stk("p_conv_b")[None]
    p_kv128 = stk("p_kv128").reshape(1, 8, 128, 2, 4, 64)
    p_kv512 = stk("p_kv512").reshape(1, 8, 512, 2, 4, 64)
    p_kv2048 = stk("p_kv2048").reshape(1, 8, 2048, 2, 4, 64)
    p_ffn = np.stack([np.asarray(R[c]["p_ffn"]) for c in range(NCORES)], axis=1)
    s_gla = cat("s_gla")[None]
    s_conv_b = cat("s_conv_b")[None]
    s_kv128 = cat("s_kv128").reshape(1, 128, 128, 2, 4, 64)
    s_kv512 = cat("s_kv512").reshape(1, 128, 512, 2, 4, 64)
    s_kv2048 = cat("s_kv2048").reshape(1, 128, 2048, 2, 4, 64)
    s_sgu_v = cat("s_sgu_v").reshape(1, 128, 8, 256)
    s_ffn = cat("s_ffn", ax=1)
    outs = (y_prompt, y_sample, p_gla, p_conv_b, p_kv128, p_kv512, p_kv2048, p_ffn,
            s_gla, s_conv_b, s_kv128, s_kv512, s_kv2048, s_sgu_v, s_ffn)
    return tuple(np.ascontiguousarray(o, dtype=np.float32) for o in outs)
```
